# Optimizing a Trainium2 kernel written in Bass

```python
import math
import jax, jax.numpy as jnp
from jax import lax
import numpy as np

D_MODEL = 1024
BATCH = 16
SEQ = 4096
DEPTH = 2

HEAD_DIM = 64
ROT_DIM = HEAD_DIM // 4
ROPE_THETA = 500000.0
N_EVEN = (DEPTH + 1) // 2
N_ODD = DEPTH // 2
BLK = 128
A_HEADS = 8
A_PATTERNS = ((128, 1), (512, 4), (2048, 16))
A_WIDTH = A_HEADS * HEAD_DIM
B_HEADS = 4
B_QK_WIDTH = B_HEADS * 2 * HEAD_DIM
B_WIDTH = B_HEADS * 2 * HEAD_DIM
QKV_COLS = 3 * A_WIDTH + 2 * B_QK_WIDTH + B_WIDTH
MIX_WIDTH = A_WIDTH + B_WIDTH
RWKV_HEADS = D_MODEL // HEAD_DIM
DECAY_LORA = 64
ICLR_LORA = 64
GATE_LORA = 160
GN_EPS = 64e-5
D_FF = 4 * D_MODEL
EPS = 1e-5

kernel_name = "hybrid_dilated_diff_rwkv7_block"


def rmsnorm(x, g):
    xf = x.astype(jnp.float32)
    y = xf * lax.rsqrt(jnp.mean(xf * xf, axis=-1, keepdims=True) + EPS)
    return (y * g.astype(jnp.float32)).astype(x.dtype)


def partial_rope(t, pos):
    half = ROT_DIM // 2
    inv_freq = ROPE_THETA ** (-jnp.arange(half, dtype=jnp.float32) / half)
    ang = pos.astype(jnp.float32)[:, None] * inv_freq[None, :]
    cos, sin = jnp.cos(ang), jnp.sin(ang)
    tf = t.astype(jnp.float32)
    t1, t2 = tf[..., :half], tf[..., half:ROT_DIM]
    out = jnp.concatenate([t1 * cos - t2 * sin, t2 * cos + t1 * sin, tf[..., ROT_DIM:]], axis=-1)
    return out.astype(t.dtype)


def dilated_window_attention(q, k, v, window, dilation):
    B, H, S, Dh = q.shape
    w_sub = window // dilation
    L = S // dilation
    nb = -(-L // BLK)
    Lp = nb * BLK

    def strided(t):
        t = t.reshape(B, H, L, dilation, Dh).transpose(0, 1, 3, 2, 4)
        return jnp.pad(t, ((0, 0), (0, 0), (0, 0), (0, Lp - L), (0, 0)))

    def banded(t):
        tb = jnp.pad(strided(t), ((0, 0), (0, 0), (0, 0), (BLK, 0), (0, 0)))
        tb = tb.reshape(B, H, dilation, nb + 1, BLK, Dh)
        return jnp.concatenate([tb[:, :, :, :-1], tb[:, :, :, 1:]], axis=4)

    qs = strided(q).reshape(B, H, dilation, nb, BLK, Dh)
    kw, vw = banded(k), banded(v)
    s = jnp.einsum('bhrnqd,bhrnkd->bhrnqk', qs, kw).astype(jnp.float32)
    qi = jnp.arange(BLK)[None, :, None]
    kc = jnp.arange(2 * BLK)[None, None, :]
    blk = jnp.arange(nb)[:, None, None]
    dist = BLK + qi - kc
    mask = (dist >= 0) & (dist <= w_sub) & ((blk - 1) * BLK + kc >= 0)
    s = jnp.where(mask, s, -jnp.inf)
    lse = jax.nn.logsumexp(s, axis=-1)
    p = jnp.exp(s - lse[..., None])
    o = jnp.einsum('bhrnqk,bhrnkd->bhrnqd', p.astype(v.dtype), vw)
    o = o.reshape(B, H, dilation, Lp, Dh)[:, :, :, :L].transpose(0, 1, 3, 2, 4).reshape(B, H, S, Dh)
    lse = lse.reshape(B, H, dilation, Lp)[..., :L].transpose(0, 1, 3, 2).reshape(B, H, S)
    return o, lse


def mixture_of_dilations(q, k, v):
    outs, lses = [], []
    for window, dilation in A_PATTERNS:
        o, l = dilated_window_attention(q, k, v, window, dilation)
        outs.append(o.astype(jnp.float32))
        lses.append(l)
    wts = jax.nn.softmax(jnp.stack(lses), axis=0)
    return jnp.sum(wts[..., None] * jnp.stack(outs), axis=0).astype(q.dtype)


def differential_attention(q, k, v, lam, gain, lam_init):
    B, H, _, S, Dh = q.shape
    nq = S // BLK
    qb = jnp.moveaxis(q.reshape(B, H, 2, nq, BLK, Dh), 3, 0)
    kpos = jnp.arange(S)

    def block(args):
        qblk, n = args
        s = jnp.einsum('bhcqd,bhckd->bhcqk', qblk, k).astype(jnp.float32)
        qpos = n * BLK + jnp.arange(BLK)
        s = jnp.where(qpos[:, None] >= kpos[None, :], s, -jnp.inf)
        p = jax.nn.softmax(s, axis=-1)
        attn = p[:, :, 0] - lam * p[:, :, 1]
        return jnp.einsum('bhqk,bhkd->bhqd', attn.astype(v.dtype), v)

    o = lax.map(block, (qb, jnp.arange(nq)))
    o = jnp.moveaxis(o, 0, 2).reshape(B, H, S, 2 * Dh)
    return rmsnorm(o, gain) * (1.0 - lam_init)


def hybrid_attention(h, w_in, w_out, lam_p, subln, lam_init):
    B, S, _ = h.shape
    pos = jnp.arange(S)
    scale = HEAD_DIM ** -0.5
    proj = h @ w_in
    cuts = [A_WIDTH, 2 * A_WIDTH, 3 * A_WIDTH, 3 * A_WIDTH + B_QK_WIDTH, 3 * A_WIDTH + 2 * B_QK_WIDTH]
    aq, ak, av, bq, bk, bv = jnp.split(proj, cuts, axis=-1)

    def heads(t, n):
        return t.reshape(B, S, n, -1).transpose(0, 2, 1, 3)

    aq = partial_rope(heads(aq, A_HEADS), pos) * scale
    ak = partial_rope(heads(ak, A_HEADS), pos)
    oa = mixture_of_dilations(aq, ak, heads(av, A_HEADS))
    oa = oa.transpose(0, 2, 1, 3).reshape(B, S, A_WIDTH)

    def two_maps(t):
        return t.reshape(B, S, B_HEADS, 2, HEAD_DIM).transpose(0, 2, 3, 1, 4)

    bq = partial_rope(two_maps(bq), pos) * scale
    bk = partial_rope(two_maps(bk), pos)
    lp = lam_p.astype(jnp.float32)
    lam = jnp.exp(jnp.sum(lp[0] * lp[1])) - jnp.exp(jnp.sum(lp[2] * lp[3])) + lam_init
    ob = differential_attention(bq, bk, heads(bv, B_HEADS), lam, subln, lam_init)
    ob = ob.transpose(0, 2, 1, 3).reshape(B, S, B_WIDTH)

    return jnp.concatenate([oa, ob], axis=-1) @ w_out


def rwkv7_time_mix(h, mu, w_r, w_k, w_v, w_o, w0, w1, w2, a0, a1, a2, g1, g2, k_k, k_a, r_k, ln_w, ln_b):
    B, S, C = h.shape
    H, N = RWKV_HEADS, HEAD_DIM
    xx = jnp.pad(h, ((0, 0), (1, 0), (0, 0)))[:, :-1] - h
    xr, xw, xk, xv, xa, xg = [h + xx * mu[i] for i in range(6)]
    r = xr @ w_r
    k = xk @ w_k
    v = xv @ w_v
    w = -jax.nn.softplus(-(w0 + jnp.tanh(xw @ w1) @ w2)) - 0.5
    decay = jnp.exp(-jnp.exp(w.astype(jnp.float32)))
    a = jax.nn.sigmoid(a0 + (xa @ a1) @ a2)
    g = jax.nn.sigmoid(xg @ g1) @ g2
    kk = (k * k_k).reshape(B, S, H, N).astype(jnp.float32)
    kk = kk / jnp.maximum(jnp.sqrt(jnp.sum(kk * kk, axis=-1, keepdims=True)), 1e-12)
    k = k * (1.0 + (a - 1.0) * k_a)

    def hd(t):
        return t.reshape(B, S, H, N).astype(jnp.float32)

    r_h, k_h, v_h, a_h = hd(r), hd(k), hd(v), hd(a)
    seqs = tuple(jnp.moveaxis(t, 1, 0) for t in (r_h, hd(decay), k_h, v_h, -kk, kk * a_h))

    def step(state, inp):
        r_t, w_t, k_t, v_t, a_t, b_t = inp
        sa = jnp.einsum('bhij,bhj->bhi', state, a_t)
        state = state * w_t[:, :, None, :] + sa[..., None] * b_t[:, :, None, :] + v_t[..., None] * k_t[:, :, None, :]
        return state, jnp.einsum('bhij,bhj->bhi', state, r_t)

    _, y = lax.scan(step, jnp.zeros((B, H, N, N), jnp.float32), seqs)
    y = jnp.moveaxis(y, 0, 1)
    mean = jnp.mean(y, axis=-1, keepdims=True)
    var = jnp.mean(jnp.square(y - mean), axis=-1, keepdims=True)
    yn = ((y - mean) * lax.rsqrt(var + GN_EPS)).reshape(B, S, C) * ln_w + ln_b
    bonus = jnp.sum(r_h * k_h * r_k.astype(jnp.float32), axis=-1, keepdims=True) * v_h
    out = (yn + bonus.reshape(B, S, C)).astype(h.dtype)
    return (out * g) @ w_o


def squared_relu_mlp(h, w1, w2):
    return jnp.square(jax.nn.relu(h @ w1)) @ w2


def setup_inputs(seed: int = 0) -> dict:
    key = jax.random.key(seed)
    ks = iter(jax.random.split(key, 40))
    C = D_MODEL

    def nrm(shape, scale):
        return jax.random.normal(next(ks), shape, jnp.float32) * scale

    def gain(shape, base=1.0):
        return base + nrm(shape, 0.02)

    return {
        "x": nrm((BATCH, SEQ, C), 1.0),
        "norm_mix": gain((DEPTH, C)),
        "norm_mlp": gain((DEPTH, C)),
        "norm_final": gain((C,)),
        "attn_w_in": nrm((N_EVEN, C, QKV_COLS), C ** -0.5),
        "attn_w_out": nrm((N_EVEN, MIX_WIDTH, C), MIX_WIDTH ** -0.5),
        "diff_lambda": nrm((N_EVEN, 4, HEAD_DIM), 0.1),
        "diff_subln": gain((N_EVEN, 2 * HEAD_DIM)),
        "rwkv_mu": jax.random.uniform(next(ks), (N_ODD, 6, C), jnp.float32),
        "rwkv_w_r": nrm((N_ODD, C, C), C ** -0.5),
        "rwkv_w_k": nrm((N_ODD, C, C), C ** -0.5),
        "rwkv_w_v": nrm((N_ODD, C, C), C ** -0.5),
        "rwkv_w_o": nrm((N_ODD, C, C), C ** -0.5),
        "rwkv_w0": jax.random.uniform(next(ks), (N_ODD, C), jnp.float32, -6.0, -1.0),
        "rwkv_w1": nrm((N_ODD, C, DECAY_LORA), C ** -0.5),
        "rwkv_w2": nrm((N_ODD, DECAY_LORA, C), 0.1 * DECAY_LORA ** -0.5),
        "rwkv_a0": nrm((N_ODD, C), 0.1),
        "rwkv_a1": nrm((N_ODD, C, ICLR_LORA), C ** -0.5),
        "rwkv_a2": nrm((N_ODD, ICLR_LORA, C), 0.1 * ICLR_LORA ** -0.5),
        "rwkv_g1": nrm((N_ODD, C, GATE_LORA), C ** -0.5),
        "rwkv_g2": nrm((N_ODD, GATE_LORA, C), GATE_LORA ** -0.5),
        "rwkv_k_k": gain((N_ODD, C), 0.85),
        "rwkv_k_a": gain((N_ODD, C)),
        "rwkv_r_k": nrm((N_ODD, RWKV_HEADS, HEAD_DIM), 0.1),
        "rwkv_ln_w": gain((N_ODD, C)),
        "rwkv_ln_b": nrm((N_ODD, C), 0.02),
        "mlp_w1": nrm((DEPTH, C, D_FF), C ** -0.5),
        "mlp_w2": nrm((DEPTH, D_FF, C), D_FF ** -0.5),
    }


def reference(x, norm_mix, norm_mlp, norm_final, attn_w_in, attn_w_out, diff_lambda, diff_subln,
              rwkv_mu, rwkv_w_r, rwkv_w_k, rwkv_w_v, rwkv_w_o, rwkv_w0, rwkv_w1, rwkv_w2,
              rwkv_a0, rwkv_a1, rwkv_a2, rwkv_g1, rwkv_g2, rwkv_k_k, rwkv_k_a, rwkv_r_k,
              rwkv_ln_w, rwkv_ln_b, mlp_w1, mlp_w2):
    h = x
    for layer in range(DEPTH):
        j = layer // 2
        hn = rmsnorm(h, norm_mix[layer])
        if layer % 2 == 0:
            lam_init = 0.8 - 0.6 * math.exp(-0.3 * layer)
            mix = hybrid_attention(hn, attn_w_in[j], attn_w_out[j], diff_lambda[j], diff_subln[j], lam_init)
        else:
            mix = rwkv7_time_mix(hn, rwkv_mu[j], rwkv_w_r[j], rwkv_w_k[j], rwkv_w_v[j], rwkv_w_o[j],
                                 rwkv_w0[j], rwkv_w1[j], rwkv_w2[j], rwkv_a0[j], rwkv_a1[j], rwkv_a2[j],
                                 rwkv_g1[j], rwkv_g2[j], rwkv_k_k[j], rwkv_k_a[j], rwkv_r_k[j],
                                 rwkv_ln_w[j], rwkv_ln_b[j])
        h = h + mix
        h = h + squared_relu_mlp(rmsnorm(h, norm_mlp[layer]), mlp_w1[layer], mlp_w2[layer])
    return rmsnorm(h, norm_final)
```

```python
import math
from contextlib import ExitStack
import numpy as np
import ml_dtypes
import concourse.bass as bass
import concourse.mybir as mybir
from concourse.bass_utils import run_bass_kernel_spmd

F32 = mybir.dt.float32
BF16 = mybir.dt.bfloat16
ALU = mybir.AluOpType
AF = mybir.ActivationFunctionType
AX = mybir.AxisListType

D = 1024
DFF = 4096
HD = 64
EPS = 1e-5
GN_EPS = 64e-5
NCORES = 8


class Buf:
    __slots__ = ("w", "rs", "ds", "ps")

    def __init__(self):
        self.w = None
        self.rs = {}
        self.ds = None
        self.ps = False


class DSem:
    __slots__ = ("h", "v", "key")

    def __init__(self, h, key):
        self.h = h
        self.v = 0
        self.key = key


class KB:
    def __init__(self, n_dsem=90):
        nc = self.nc = bass.Bass("TRN2", target_bir_lowering=False)
        self.engs = {"pe": nc.tensor, "dve": nc.vector, "act": nc.scalar, "pool": nc.gpsimd, "sp": nc.sync}
        self.sem = {}
        self.cnt = {}
        self.waited = {e: {} for e in self.engs}
        for e in self.engs:
            self.sem[e] = nc.alloc_semaphore(name="prog_" + e)
            self.cnt[e] = 0
        self.free_ds = [DSem(nc.alloc_semaphore(name="dsem%d" % i), "d%d" % i) for i in range(n_dsem)]
        self.pending = {}
        self.bar = nc.alloc_semaphore(name="barrier")
        self.barv = 0
        self.psum = []
        for i in range(8):
            t = nc.alloc_psum_tensor("psb%d" % i, [128, 512], F32)
            self.psum.append(t)
        self.pbuf = [Buf() for _ in range(8)]
        for b in self.pbuf:
            b.ps = True

    def _wait(self, e, tok):
        if tok is None:
            return
        key, h, v = tok
        if key == "pe" and e == "pe":
            return
        w = self.waited[e]
        if w.get(key, 0) >= v:
            return
        self.engs[e].wait_ge(h, v)
        w[key] = v

    def _deps(self, e, reads, writes):
        for b in reads:
            self._wait(e, b.w)
            if b.ps:
                for k2, t in b.rs.items():
                    if k2 != e:
                        self._wait(e, t)
        for b in writes:
            self._wait(e, b.w)
            for t in b.rs.values():
                self._wait(e, t)

    def op(self, e, fn, reads=(), writes=()):
        self._deps(e, reads, writes)
        ins = fn(self.engs[e])
        self.cnt[e] += 1
        ins.then_inc(self.sem[e], 1)
        tok = (e, self.sem[e], self.cnt[e])
        for b in reads:
            b.rs[e] = tok
        for b in writes:
            b.w = tok
            b.rs = {}
        return tok

    def dma(self, q, out, in_, buf, load=True, **kw):
        if buf.ds is None:
            buf.ds = self.free_ds.pop()
        ds = buf.ds
        if load:
            if buf.w is not None and buf.w[0] == ds.key:
                for t in buf.rs.values():
                    self._wait(q, t)
            else:
                self._deps(q, (), (buf,))
        else:
            self._deps(q, (buf,), ())
        ins = self.engs[q].dma_start(out=out, in_=in_, **kw)
        ds.v += 16
        ins.then_inc(ds.h, 16)
        tok = (ds.key, ds.h, ds.v)
        if load:
            buf.w = tok
            buf.rs = {}
        else:
            buf.rs[ds.key] = tok
        self.pending[ds.key] = tok
        return tok

    def alias_acquire(self, parents, kids):
        pend = {}
        for pb in parents:
            toks = list(pb.rs.values()) + ([pb.w] if pb.w is not None else [])
            for t in toks:
                if t[0] not in pend or pend[t[0]][2] < t[2]:
                    pend[t[0]] = t
        for kbuf in kids:
            kbuf.w = None
            kbuf.rs = dict(pend)

    def alias_release(self, parents, kids):
        pend = {}
        for kbuf in kids:
            toks = list(kbuf.rs.values()) + ([kbuf.w] if kbuf.w is not None else [])
            for t in toks:
                if t[0] not in pend or pend[t[0]][2] < t[2]:
                    pend[t[0]] = t
        for pb in parents:
            for k_, t in pend.items():
                if k_ not in pb.rs or pb.rs[k_][2] < t[2]:
                    pb.rs[k_] = t

    def release(self, bufs):
        for b in bufs:
            if b.ds is not None:
                self.free_ds.append(b.ds)
                b.ds = None

    def barrier(self):
        sp = "sp"
        for e in self.engs:
            if e != sp and self.cnt[e] > 0:
                self._wait(sp, (e, self.sem[e], self.cnt[e]))
        for tok in self.pending.values():
            self._wait(sp, tok)
        self.pending = {}
        self.barv += 1
        self.engs[sp].nop().then_inc(self.bar, 1)
        for e in self.engs:
            self.engs[e].wait_ge(self.bar, self.barv)
        for e in self.engs:
            for e2 in self.engs:
                self.waited[e][e2] = self.cnt[e2]

    def sbt(self, name, shape, dt):
        self.uid = getattr(self, "uid", 0) + 1
        return self.nc.sbuf_tensor("%s_u%d" % (name, self.uid), shape, dt)

    def bank(self, i, dt=F32):
        t = self.psum[i]
        ap = t[:, :] if hasattr(t, "__getitem__") else t.ap()
        if dt != F32:
            ap = ap.bitcast(dt)
        return ap


def bcast_rows(ap1d, n=128):
    return ap1d.partition_broadcast(n)


def load_weight_bf16(kb, es, w_ap, K, N, name, gcol=None, stage_cols=2048, q="sp"):
    kc = (K + 127) // 128
    wsb = es.enter_context(kb.sbt(name, [128, kc, N], BF16))
    wb = Buf()
    sc = min(stage_cols, N)
    NS = 4
    with ExitStack() as s2:
        stg = [s2.enter_context(kb.sbt(name + "_stg%d" % i, [128, sc], F32)) for i in range(NS)]
        sb = [Buf() for _ in range(NS)]
        i = 0
        for c in range(kc):
            rows = min(128, K - c * 128)
            for n0 in range(0, N, sc):
                ncol = min(sc, N - n0)
                j = i % NS
                dq = "sp" if (i % 2 == 0) else "act"
                eng = "dve" if (i % 2 == 0) else "pool"
                i += 1
                kb.dma(dq, stg[j][:rows, :ncol], w_ap[c * 128:c * 128 + rows, n0:n0 + ncol], sb[j])
                if gcol is not None:
                    kb.op(eng, lambda e, j=j, c=c, n0=n0, ncol=ncol, rows=rows: e.tensor_scalar(
                        out=wsb[:rows, c, n0:n0 + ncol], in0=stg[j][:rows, :ncol],
                        scalar1=gcol[0][:rows, c:c + 1], scalar2=None, op0=ALU.mult),
                        reads=(sb[j], gcol[1]), writes=())
                else:
                    kb.op(eng, lambda e, j=j, c=c, n0=n0, ncol=ncol, rows=rows: e.tensor_copy(
                        out=wsb[:rows, c, n0:n0 + ncol], in_=stg[j][:rows, :ncol]),
                        reads=(sb[j],), writes=())
        kb.barrier()
        kb.release(sb)
    return wsb, wb


def load_cols(kb, es, v_ap, name, q="sp"):
    nc = kb.nc
    C = v_ap.shape[0] // 128
    t = es.enter_context(kb.sbt(name, [128, C], F32))
    b = Buf()
    kb.dma(q, t[:, :], v_ap.rearrange("(c p) -> p c", p=128), b, allow_slow_non_contiguous=True)
    return t, b


def load_bcast(kb, es, v_ap, name, q="sp"):
    nc = kb.nc
    Fd = v_ap.shape[0]
    t = es.enter_context(kb.sbt(name, [128, Fd], F32))
    b = Buf()
    kb.dma(q, t[:, :], v_ap.partition_broadcast(128), b)
    return t, b


def rms_rstd(kb, x_ap, xbuf, junk, junkb, ssq, rstd, sbuf_, width, eps):
    kb.op("act", lambda e: e.activation(out=junk, in_=x_ap, func=AF.Square, accum_out=ssq),
          reads=(xbuf,), writes=(junkb, sbuf_))
    kb.op("dve", lambda e: e.tensor_scalar(out=rstd, in0=ssq, scalar1=1.0 / width, scalar2=eps,
                                           op0=ALU.mult, op1=ALU.add),
          reads=(sbuf_,), writes=(sbuf_,))
    kb.op("act", lambda e: e.activation(out=rstd, in_=rstd, func=AF.Sqrt), reads=(sbuf_,), writes=(sbuf_,))
    kb.op("dve", lambda e: e.reciprocal(out=rstd, in_=rstd), reads=(sbuf_,), writes=(sbuf_,))


def transpose_tile(kb, src_fn, srcbuf, nchunks, dst_fn, dstbuf, ident, identb, banks, state, evac_engs=("dve", "act")):
    c = 0
    while c < nchunks:
        n = min(8, nchunks - c)
        bi = banks[state["i"] % len(banks)]
        ev = evac_engs[state["i"] % len(evac_engs)]
        state["i"] += 1
        pb = kb.pbuf[bi]
        pap = kb.bank(bi, BF16)
        for k in range(n):
            kb.op("pe", lambda e, k=k, c=c: e.transpose(out=pap[:, k * 128:(k + 1) * 128], in_=src_fn(c + k), identity=ident),
                  reads=(srcbuf, identb), writes=(pb,))
        dst = dst_fn(c, n)
        src = pap[:, 0:n * 128].rearrange("p (n t) -> p n t", t=128)
        if ev == "act":
            kb.op("act", lambda e: e.copy(out=dst, in_=src), reads=(pb,), writes=(dstbuf,))
        else:
            kb.op("dve", lambda e: e.tensor_copy(out=dst, in_=src), reads=(pb,), writes=(dstbuf,))
        c += n


def phase_proj(kb, mix, w, h_in, h_out, ident_d, ntok):
    nc = kb.nc
    with ExitStack() as es:
        ident = es.enter_context(kb.sbt("pj_ident", [128, 128], BF16))
        identb = Buf()
        kb.dma("sp", ident[:, :], ident_d, identb)
        wsb, wb = load_weight_bf16(kb, es, w, D, D, "pj_w")
        NB = 2
        mx = [es.enter_context(kb.sbt("pj_mx%d" % i, [128, D], BF16)) for i in range(NB)]
        mxb = [Buf() for _ in range(NB)]
        hi = [es.enter_context(kb.sbt("pj_hi%d" % i, [128, D], F32)) for i in range(NB)]
        hib = [Buf() for _ in range(NB)]
        mT = [es.enter_context(kb.sbt("pj_mT%d" % i, [128, 8, 128], BF16)) for i in range(NB)]
        mTb = [Buf() for _ in range(NB)]
        st = {"i": 0}
        pi = 0
        for t in range(ntok // 128):
            j = t % NB
            r0 = t * 128
            if t == 0:
                kb.dma("sp", mx[j][:, :], mix[r0:r0 + 128, :], mxb[j])
                kb.dma("sp", hi[j][:, :], h_in[r0:r0 + 128, :], hib[j])
            if t + 1 < ntok // 128:
                kb.dma("sp", mx[1 - j][:, :], mix[r0 + 128:r0 + 256, :], mxb[1 - j])
                kb.dma("sp", hi[1 - j][:, :], h_in[r0 + 128:r0 + 256, :], hib[1 - j])
            transpose_tile(kb, lambda c, j=j: mx[j][:, c * 128:(c + 1) * 128], mxb[j], 8,
                           lambda c0, n, j=j: mT[j][:, c0:c0 + n, :], mTb[j], ident[:, :], identb, (0, 1), st)
            for half in range(2):
                bi = 2 + (pi % 4)
                pi += 1
                pap = kb.bank(bi)
                for c in range(8):
                    kb.op("pe", lambda e, c=c, j=j, half=half, pap=pap: e.matmul(
                        pap, lhsT=mT[j][:, c, :], rhs=wsb[:, c, half * 512:(half + 1) * 512],
                        start=(c == 0), stop=(c == 7)), reads=(mTb[j], wb), writes=(kb.pbuf[bi],))
                kb.op("dve", lambda e, j=j, half=half, pap=pap: e.tensor_tensor(
                    out=hi[j][:, half * 512:(half + 1) * 512], in0=hi[j][:, half * 512:(half + 1) * 512],
                    in1=pap, op=ALU.add), reads=(kb.pbuf[bi], hib[j]), writes=(hib[j],))
            kb.dma("sp", h_out[r0:r0 + 128, :], hi[j][:, :], hib[j], load=False)
        kb.barrier()
        kb.release(mxb + hib + [identb, wb])


def phase_mlp(kb, h_in, g, w1, w2, h_out, ident_d, ntok, gfinal=None, T=256):
    nc = kb.nc
    TT = T // 128
    with ExitStack() as es:
        ident = es.enter_context(kb.sbt("ml_ident", [128, 128], BF16))
        identb = Buf()
        kb.dma("sp", ident[:, :], ident_d, identb)
        gcol = load_cols(kb, es, g, "ml_gcol")
        w1sb, w1b = load_weight_bf16(kb, es, w1, D, DFF, "ml_w1", gcol=gcol)
        w2sb, w2b = load_weight_bf16(kb, es, w2, DFF, D, "ml_w2")
        if gfinal is not None:
            gf, gfb = load_bcast(kb, es, gfinal, "ml_gf")
        NB = 2
        hi = [es.enter_context(kb.sbt("ml_hi%d" % i, [128, TT, D], F32)) for i in range(NB)]
        hib = [Buf() for _ in range(NB)]
        hn = [es.enter_context(kb.sbt("ml_hn%d" % i, [128, TT, D], BF16)) for i in range(2)]
        hnb = [Buf() for _ in range(2)]
        hnT = [es.enter_context(kb.sbt("ml_hnT%d" % i, [128, 8, T], BF16)) for i in range(2)]
        hnTb = [Buf() for _ in range(2)]
        h1T = es.enter_context(kb.sbt("ml_h1T", [128, 32, T], BF16))
        h1Tb = [Buf() for _ in range(32)]
        junk = es.enter_context(kb.sbt("ml_junk", [128, D], F32))
        junkb = Buf()
        rl = [es.enter_context(kb.sbt("ml_rl%d" % i, [128, T], F32)) for i in range(4)]
        rlb = [Buf() for _ in range(4)]
        stt = es.enter_context(kb.sbt("ml_st", [128, 16], F32))
        sttb = [Buf() for _ in range(8)]
        st = {"i": 0}
        pi = 0
        ri = 0
        nblk = ntok // T

        def load(blk):
            r0 = blk * T
            kb.dma("sp", hi[blk % NB][:, :, :], h_in[r0:r0 + T, :].rearrange("(t p) f -> p t f", p=128), hib[blk % NB])

        def front_a(blk):
            j = blk % NB
            for tt in range(TT):
                c0 = 4 * j + 2 * tt
                rms_rstd(kb, hi[j][:, tt, :], hib[j], junk[:, :], junkb, stt[:, c0:c0 + 1], stt[:, c0 + 1:c0 + 2], sttb[2 * j + tt], D, EPS)
                kb.op("dve", lambda e, tt=tt, j=j, c0=c0: e.tensor_scalar(
                    out=hn[j][:, tt, :], in0=hi[j][:, tt, :], scalar1=stt[:, c0 + 1:c0 + 2], scalar2=None,
                    op0=ALU.mult), reads=(hib[j], sttb[2 * j + tt]), writes=(hnb[j],))

        def front_b(blk):
            j = blk % NB
            for tt in range(TT):
                transpose_tile(kb, lambda c, tt=tt: hn[j][:, tt, c * 128:(c + 1) * 128], hnb[j], 8,
                               lambda c0, n, tt=tt: hnT[j][:, c0:c0 + n, tt * 128:(tt + 1) * 128], hnTb[j],
                               ident[:, :], identb, (0, 1), st)

        load(0)
        front_a(0)
        front_b(0)
        for blk in range(nblk):
            j = blk % NB
            r0 = blk * T
            if blk + 1 < nblk:
                load(blk + 1)
            for fc in range(32):
                bi = 2 + (pi % 3)
                pi += 1
                pap = kb.bank(bi)[:, 0:T]
                for c in range(8):
                    kb.op("pe", lambda e, c=c, fc=fc, pap=pap: e.matmul(
                        pap, lhsT=w1sb[:, c, fc * 128:(fc + 1) * 128], rhs=hnT[j][:, c, :],
                        start=(c == 0), stop=(c == 7)), reads=(hnTb[j], w1b), writes=(kb.pbuf[bi],))
                k = ri % 4
                ri += 1
                kb.op("act", lambda e, k=k, pap=pap: e.activation(out=rl[k][:, :], in_=pap, func=AF.Relu),
                      reads=(kb.pbuf[bi],), writes=(rlb[k],))
                kb.op("pool", lambda e, k=k, fc=fc: e.tensor_tensor(out=h1T[:, fc, :], in0=rl[k][:, :], in1=rl[k][:, :],
                                                                   op=ALU.mult),
                      reads=(rlb[k],), writes=(h1Tb[fc],))
            if blk + 1 < nblk:
                front_a(blk + 1)
            for tt in range(TT):
                for half in range(2):
                    bi = 5 + (pi % 3)
                    pi += 1
                    pap = kb.bank(bi)
                    for fc in range(32):
                        kb.op("pe", lambda e, fc=fc, tt=tt, half=half, pap=pap: e.matmul(
                            pap, lhsT=h1T[:, fc, tt * 128:(tt + 1) * 128], rhs=w2sb[:, fc, half * 512:(half + 1) * 512],
                            start=(fc == 0), stop=(fc == 31)), reads=(h1Tb[fc], w2b), writes=(kb.pbuf[bi],))
                    kb.op("dve", lambda e, j=j, tt=tt, half=half, pap=pap: e.tensor_tensor(
                        out=hi[j][:, tt, half * 512:(half + 1) * 512], in0=hi[j][:, tt, half * 512:(half + 1) * 512],
                        in1=pap, op=ALU.add), reads=(kb.pbuf[bi], hib[j]), writes=(hib[j],))
            if gfinal is not None:
                for tt in range(TT):
                    c0 = 8 + 2 * tt
                    rms_rstd(kb, hi[j][:, tt, :], hib[j], junk[:, :], junkb, stt[:, c0:c0 + 1], stt[:, c0 + 1:c0 + 2], sttb[4 + tt], D, EPS)
                    kb.op("dve", lambda e, tt=tt, j=j, c0=c0: e.scalar_tensor_tensor(
                        out=hi[j][:, tt, :], in0=hi[j][:, tt, :], scalar=stt[:, c0 + 1:c0 + 2], in1=gf[:, :],
                        op0=ALU.mult, op1=ALU.mult), reads=(hib[j], sttb[4 + tt], gfb), writes=(hib[j],))
            kb.dma("sp", h_out[r0:r0 + T, :].rearrange("(t p) f -> p t f", p=128), hi[j][:, :, :], hib[j], load=False)
            if blk + 1 < nblk:
                front_b(blk + 1)
        kb.barrier()
        kb.release(hib + [identb, w1b, w2b, gcol[1]] + ([gfb] if gfinal is not None else []))


def phase_qkv(kb, x, g, w_in, cos_d, sin_d, ident_d, QT, KT, V, nseq, S):
    nc = kb.nc
    NT = S // 128
    with ExitStack() as es:
        ident = es.enter_context(kb.sbt("qk_ident", [128, 128], BF16))
        identb = Buf()
        kb.dma("sp", ident[:, :], ident_d, identb)
        cos = es.enter_context(kb.sbt("qk_cos", [128, NT, 64], F32))
        sin = es.enter_context(kb.sbt("qk_sin", [128, NT, 64], F32))
        csb = Buf()
        snb = Buf()
        kb.dma("sp", cos[:, :, :], cos_d.rearrange("(t p) k -> p t k", p=128), csb)
        kb.dma("sp", sin[:, :, :], sin_d.rearrange("(t p) k -> p t k", p=128), snb)
        gcol = load_cols(kb, es, g, "qk_gcol")
        wsb, wb = load_weight_bf16(kb, es, w_in, D, 3072, "qk_w", gcol=gcol)
        NB = 3
        xi = [es.enter_context(kb.sbt("qk_x%d" % i, [128, D], F32)) for i in range(NB)]
        xib = [Buf() for _ in range(NB)]
        xn_ = [es.enter_context(kb.sbt("qk_xn%d" % i, [128, D], BF16)) for i in range(2)]
        xnb_ = [Buf() for _ in range(2)]
        xT_ = [es.enter_context(kb.sbt("qk_xT%d" % i, [128, 8, 128], BF16)) for i in range(2)]
        xTb_ = [Buf() for _ in range(2)]
        junk = es.enter_context(kb.sbt("qk_junk", [128, D], F32))
        junkb = Buf()
        stt_ = [es.enter_context(kb.sbt("qk_st%d" % i, [128, 2], F32)) for i in range(2)]
        sttb_ = [Buf() for _ in range(2)]
        qf = [es.enter_context(kb.sbt("qk_qf%d" % i, [128, 8, 64], F32)) for i in range(2)]
        qfb = [Buf() for _ in range(2)]
        tmp = [es.enter_context(kb.sbt("qk_tmp%d" % i, [128, 4, 8, 8], F32)) for i in range(2)]
        tmpb = [Buf() for _ in range(2)]
        qb = [es.enter_context(kb.sbt("qk_qb%d" % i, [128, 8, 64], BF16)) for i in range(8)]
        qbb = [Buf() for _ in range(8)]
        vt = [es.enter_context(kb.sbt("qk_vt%d" % i, [128, D], BF16)) for i in range(2)]
        vtb = [Buf() for _ in range(2)]
        qst = [es.enter_context(kb.sbt("qk_qst%d" % i, [128, 8, 512], BF16)) for i in range(2)]
        qstb = [Buf() for _ in range(2)]
        kst = [es.enter_context(kb.sbt("qk_kst%d" % i, [128, 8, 512], BF16)) for i in range(2)]
        kstb = [Buf() for _ in range(2)]
        st = {"i": 0}
        pi = 0
        qi = 0
        deferred = []
        prev_jobs = []
        ntile = nseq * NT

        def load_x(g):
            kb.dma("sp", xi[g % NB][:, :], x[g * 128:(g + 1) * 128, :], xib[g % NB])

        def front_a(g):
            j3, j2 = g % NB, g % 2
            xn, xnb, stt, sttb = xn_[j2], xnb_[j2], stt_[j2], sttb_[j2]
            rms_rstd(kb, xi[j3][:, :], xib[j3], junk[:, :], junkb, stt[:, 0:1], stt[:, 1:2], sttb, D, EPS)
            kb.op("dve", lambda e: e.tensor_scalar(out=xn[:, :], in0=xi[j3][:, :], scalar1=stt[:, 1:2], scalar2=None, op0=ALU.mult),
                  reads=(xib[j3], sttb), writes=(xnb,))

        def front_b(g):
            j2 = g % 2
            xn, xnb, xT, xTb = xn_[j2], xnb_[j2], xT_[j2], xTb_[j2]
            transpose_tile(kb, lambda c: xn[:, c * 128:(c + 1) * 128], xnb, 8,
                           lambda c0, n: xT[:, c0:c0 + n, :], xTb, ident[:, :], identb, (0, 1), st)

        for s in range(nseq):
            for t in range(NT):
                gt = s * NT + t
                j = gt % NB
                r0 = gt * 128
                grp = (gt // 4) % 2
                tin = t % 4
                if gt == 0:
                    load_x(0)
                    if ntile > 1:
                        load_x(1)
                    front_a(0)
                    front_b(0)
                if gt + 2 < ntile:
                    load_x(gt + 2)
                if gt + 1 < ntile:
                    front_a(gt + 1)
                xT, xTb = xT_[gt % 2], xTb_[gt % 2]
                vj = gt % 2
                for cb in range(6):
                    bi = 2 + (pi % 6)
                    pi += 1
                    pap = kb.bank(bi)
                    for c in range(8):
                        kb.op("pe", lambda e, c=c, cb=cb, pap=pap, xT=xT: e.matmul(
                            pap, lhsT=xT[:, c, :], rhs=wsb[:, c, cb * 512:(cb + 1) * 512],
                            start=(c == 0), stop=(c == 7)), reads=(xTb, wb), writes=(kb.pbuf[bi],))
                    if cb in (2, 5):
                        c0 = 0 if cb == 2 else 512
                        kb.op("act", lambda e, pap=pap, c0=c0, vj=vj: e.copy(out=vt[vj][:, c0:c0 + 512], in_=pap),
                              reads=(kb.pbuf[bi],), writes=(vtb[vj],))
                        continue
                    k = qi % 2
                    kq = (gt % 2) * 4 + (qi % 4)
                    qi += 1
                    isq = cb in (0, 3)
                    kb.op("act", lambda e, pap=pap, k=k, isq=isq: e.mul(
                        out=qf[k][:, :, :], in_=pap.rearrange("p (h d) -> p h d", d=64), mul=(0.125 if isq else 1.0)),
                        reads=(kb.pbuf[bi],), writes=(qfb[k],))
                    q1 = qf[k][:, :, 0:8]
                    q2 = qf[k][:, :, 8:16]
                    cs = cos[:, t, :].rearrange("p (h d) -> p h d", d=8)
                    sn = sin[:, t, :].rearrange("p (h d) -> p h d", d=8)
                    tm = tmp[k]
                    kb.op("dve", lambda e, tm=tm, q1=q1, cs=cs: e.tensor_tensor(out=tm[:, 0, :, :], in0=q1, in1=cs, op=ALU.mult),
                          reads=(qfb[k], csb), writes=(tmpb[k],))
                    kb.op("dve", lambda e, tm=tm, q2=q2, sn=sn: e.tensor_tensor(out=tm[:, 1, :, :], in0=q2, in1=sn, op=ALU.mult),
                          reads=(qfb[k], snb), writes=(tmpb[k],))
                    kb.op("pool", lambda e, tm=tm, q2=q2, cs=cs: e.tensor_tensor(out=tm[:, 2, :, :], in0=q2, in1=cs, op=ALU.mult),
                          reads=(qfb[k], csb), writes=(tmpb[k],))
                    kb.op("pool", lambda e, tm=tm, q1=q1, sn=sn: e.tensor_tensor(out=tm[:, 3, :, :], in0=q1, in1=sn, op=ALU.mult),
                          reads=(qfb[k], snb), writes=(tmpb[k],))
                    kb.op("dve", lambda e, tm=tm, kq=kq: e.tensor_tensor(out=qb[kq][:, :, 0:8], in0=tm[:, 0, :, :], in1=tm[:, 1, :, :],
                                                                       op=ALU.subtract), reads=(tmpb[k],), writes=(qbb[kq],))
                    kb.op("pool", lambda e, tm=tm, kq=kq: e.tensor_tensor(out=qb[kq][:, :, 8:16], in0=tm[:, 2, :, :], in1=tm[:, 3, :, :],
                                                                        op=ALU.add), reads=(tmpb[k],), writes=(qbb[kq],))
                    kb.op("act", lambda e, k=k, kq=kq: e.copy(out=qb[kq][:, :, 16:64], in_=qf[k][:, :, 16:64]),
                          reads=(qfb[k],), writes=(qbb[kq],))
                    hp0 = 0 if cb in (0, 1) else 4
                    dst_t, dst_b = (qst[grp], qstb[grp]) if isq else (kst[grp], kstb[grp])

                    def tr_job(kq=kq, dst_t=dst_t, dst_b=dst_b, hp0=hp0, tin=tin):
                        qflat = qb[kq]
                        transpose_tile(kb, lambda c: qflat[:, 2 * c:2 * c + 2, :].rearrange("p h d -> p (h d)"), qbb[kq], 4,
                                       lambda c0, n: dst_t[:, hp0 + c0:hp0 + c0 + n, tin * 128:(tin + 1) * 128],
                                       dst_b, ident[:, :], identb, (0, 1), st)
                    deferred.append(tr_job)
                if gt + 1 < ntile:
                    front_b(gt + 1)
                kb.dma("act", V[r0:r0 + 128, :], vt[vj][:, :], vtb[vj], load=False)
                if tin == 3:
                    def st_job(s=s, t=t, grp=grp):
                        t0 = (t - 3) * 128
                        kb.dma("act", QT[s, :, :, t0:t0 + 512].rearrange("h p t -> p h t"), qst[grp][:, :, :], qstb[grp], load=False)
                        kb.dma("act", KT[s, :, :, t0:t0 + 512].rearrange("h p t -> p h t"), kst[grp][:, :, :], kstb[grp], load=False)
                    deferred.append(st_job)
                for job in prev_jobs:
                    job()
                prev_jobs = deferred
                deferred = []
        for job in prev_jobs:
            job()
        kb.barrier()
        kb.release(xib + vtb + qstb + kstb + [identb, csb, snb, wb, gcol[1]])


def phase_attn_b(kb, QT, KT, V, lam_d, subln_d, maskc_d, mix, nseq, S, lam_init):
    nc = kb.nc
    NT = S // 128
    NG = S // 512
    with ExitStack() as es:
        lp = es.enter_context(kb.sbt("ab_lp", [128, 256], F32))
        lpb = Buf()
        kb.dma("sp", lp[:, :], lam_d.rearrange("a b -> (a b)").partition_broadcast(128), lpb)
        gs = es.enter_context(kb.sbt("ab_gs", [128, 128], F32))
        gsb = Buf()
        kb.dma("sp", gs[:, :], subln_d.partition_broadcast(128), gsb)
        mk = es.enter_context(kb.sbt("ab_mk", [128, 128], BF16))
        mkb = Buf()
        kb.dma("sp", mk[:, :], maskc_d, mkb)
        sc = es.enter_context(kb.sbt("ab_sc", [128, 8], F32))
        scb = Buf()
        pr = es.enter_context(kb.sbt("ab_pr", [128, 128], F32))
        prb = Buf()
        for i in range(2):
            kb.op("dve", lambda e, i=i: e.tensor_tensor(out=pr[:, i * 64:(i + 1) * 64], in0=lp[:, i * 128:i * 128 + 64],
                                                       in1=lp[:, i * 128 + 64:i * 128 + 128], op=ALU.mult),
                  reads=(lpb,), writes=(prb,))
        kb.op("dve", lambda e: e.reduce_sum(out=sc[:, 0:2], in_=pr[:, :].rearrange("p (a b) -> p a b", a=2), axis=AX.X),
              reads=(prb,), writes=(scb,))
        kb.op("act", lambda e: e.activation(out=sc[:, 0:2], in_=sc[:, 0:2], func=AF.Exp), reads=(scb,), writes=(scb,))
        kb.op("dve", lambda e: e.tensor_tensor(out=sc[:, 2:3], in0=sc[:, 1:2], in1=sc[:, 0:1], op=ALU.subtract),
              reads=(scb,), writes=(scb,))
        kb.op("dve", lambda e: e.tensor_scalar(out=sc[:, 2:3], in0=sc[:, 2:3], scalar1=-lam_init, scalar2=None, op0=ALU.add),
              reads=(scb,), writes=(scb,))
        kb.op("dve", lambda e: e.tensor_scalar(out=gs[:, :], in0=gs[:, :], scalar1=1.0 - lam_init, scalar2=None, op0=ALU.mult),
              reads=(gsb,), writes=(gsb,))
        qt = [es.enter_context(kb.sbt("ab_qt%d" % i, [128, S], BF16)) for i in range(2)]
        qtb = [Buf() for _ in range(2)]
        kt = [es.enter_context(kb.sbt("ab_kt%d" % i, [128, S], BF16)) for i in range(2)]
        ktb = [Buf() for _ in range(2)]
        vb = es.enter_context(kb.sbt("ab_vb", [128, NT, 4, 129], BF16))
        vbb = Buf()
        kb.op("pool", lambda e: e.memset(vb[:, :, :, 128:129], 1.0), writes=(vbb,))
        NP = 4
        pt = [es.enter_context(kb.sbt("ab_pt%d" % i, [128, 512], BF16)) for i in range(NP)]
        ptb = [Buf() for _ in range(NP)]
        accb = [kb.pbuf[4 + a // 3] for a in range(8)]
        o0 = es.enter_context(kb.sbt("ab_o0", [128, 128], F32))
        o0b = Buf()
        oo = es.enter_context(kb.sbt("ab_oo", [128, 128], F32))
        oob = Buf()
        jk = es.enter_context(kb.sbt("ab_jk", [128, 128], F32))
        jkb = Buf()
        rz = es.enter_context(kb.sbt("ab_rz", [128, 4], F32))
        rzb = Buf()
        ob = [es.enter_context(kb.sbt("ab_ob%d" % i, [128, 4, 128], BF16)) for i in range(2)]
        obb = [Buf() for _ in range(2)]
        oo4 = [es.enter_context(kb.sbt("ab_oo4%d" % i, [128, 4, 128], F32)) for i in range(2)]
        oo4b = [Buf() for _ in range(2)]
        rz4 = es.enter_context(kb.sbt("ab_rz4", [128, 4, 2], F32))
        rz4b = Buf()
        ss4 = es.enter_context(kb.sbt("ab_ss4", [128, 8], F32))
        ss4b = Buf()

        def acc_ap(a):
            return kb.bank(4 + a // 3)[:, (a % 3) * 129:(a % 3) * 129 + 129]

        pi_ = [0]
        pj_ = [0]
        oi_ = [0]
        it = 0
        for s in range(nseq):
            for hh in range(4):
                kb.dma("sp", vb[:, :, hh, 0:128],
                       V[s * S:(s + 1) * S, 512 + hh * 128:512 + (hh + 1) * 128].rearrange("(t p) d -> p t d", p=128), vbb)
            for h in range(4):
                j = it % 2
                it += 1
                if s == 0 and h == 0:
                    kb.dma("sp", qt[j][:, :], QT[s, 4 + h, :, :], qtb[j])
                    kb.dma("sp", kt[j][:, :], KT[s, 4 + h, :, :], ktb[j])
                nh = s * 4 + h + 1
                if nh < nseq * 4:
                    kb.dma("sp", qt[1 - j][:, :], QT[nh // 4, 4 + nh % 4, :, :], qtb[1 - j])
                    kb.dma("sp", kt[1 - j][:, :], KT[nh // 4, 4 + nh % 4, :, :], ktb[1 - j])
                groups = [(n, mg, min(4, n + 1 - mg)) for n in range(NT) for mg in range(0, n + 1, 4)]

                def emit_qk(g, j=j):
                    n, mg, mc = g
                    bis = []
                    for c in range(2):
                        bis.append(pi_[0] % 4)
                        pi_[0] += 1
                    for i in range(mc):
                        m = mg + i
                        for c in range(2):
                            bi = bis[c]
                            kb.op("pe", lambda e, bi=bi, c=c, m=m, n=n, i=i, j=j: e.matmul(
                                kb.bank(bi)[:, i * 128:(i + 1) * 128], lhsT=kt[j][64 * c:64 * c + 64, m * 128:(m + 1) * 128],
                                rhs=qt[j][64 * c:64 * c + 64, n * 128:(n + 1) * 128], start=True, stop=True),
                                reads=(ktb[j], qtb[j]), writes=(kb.pbuf[bi],))
                    ks = []
                    for c in range(2):
                        bi = bis[c]
                        k = pj_[0] % NP
                        pj_[0] += 1
                        ks.append(k)
                        kb.op("act", lambda e, bi=bi, k=k, mc=mc: e.activation(out=pt[k][:, 0:mc * 128], in_=kb.bank(bi)[:, 0:mc * 128],
                                                                             func=AF.Exp),
                              reads=(kb.pbuf[bi],), writes=(ptb[k],))
                        if mg + mc - 1 == n:
                            i = mc - 1
                            kb.op("pool", lambda e, k=k, i=i: e.tensor_tensor(out=pt[k][:, i * 128:(i + 1) * 128],
                                                                            in0=pt[k][:, i * 128:(i + 1) * 128], in1=mk[:, :], op=ALU.mult),
                                  reads=(ptb[k], mkb), writes=(ptb[k],))
                    return ks

                def emit_pv(g, ks, h=h):
                    n, mg, mc = g
                    ab = 4 + 2 * (n % 2)
                    for c in range(2):
                        k = ks[c]
                        acc = kb.bank(ab + c)[:, 0:129]
                        for i in range(mc):
                            m = mg + i
                            kb.op("pe", lambda e, acc=acc, k=k, i=i, m=m, h=h, n=n: e.matmul(
                                acc, lhsT=pt[k][:, i * 128:(i + 1) * 128], rhs=vb[:, m, h, :],
                                start=(m == 0), stop=(m == n)), reads=(ptb[k], vbb), writes=(kb.pbuf[ab + c],))

                def emit_combine(n, s=s, h=h):
                    ab = 4 + 2 * (n % 2)
                    oj = (oi_[0] + n // 4) % 2
                    jj = n % 4
                    a0 = kb.bank(ab)[:, 0:129]
                    a1 = kb.bank(ab + 1)[:, 0:129]
                    b0 = kb.pbuf[ab]
                    b1 = kb.pbuf[ab + 1]
                    rzj = rz4[:, jj, :]
                    kb.op("dve", lambda e, a0=a0, rzj=rzj: e.reciprocal(out=rzj[:, 0:1], in_=a0[:, 128:129]), reads=(b0,), writes=(rz4b,))
                    kb.op("dve", lambda e, a1=a1, rzj=rzj: e.reciprocal(out=rzj[:, 1:2], in_=a1[:, 128:129]), reads=(b1,), writes=(rz4b,))
                    kb.op("dve", lambda e, rzj=rzj: e.tensor_tensor(out=rzj[:, 1:2], in0=rzj[:, 1:2], in1=sc[:, 2:3], op=ALU.mult),
                          reads=(rz4b, scb), writes=(rz4b,))
                    kb.op("dve", lambda e, a0=a0, rzj=rzj: e.tensor_scalar(out=o0[:, :], in0=a0[:, 0:128], scalar1=rzj[:, 0:1], scalar2=None,
                                                                          op0=ALU.mult), reads=(b0, rz4b), writes=(o0b,))
                    kb.op("dve", lambda e, a1=a1, rzj=rzj, jj=jj, oj=oj: e.scalar_tensor_tensor(
                        out=oo4[oj][:, jj, :], in0=a1[:, 0:128], scalar=rzj[:, 1:2], in1=o0[:, :], op0=ALU.mult, op1=ALU.add),
                        reads=(b1, rz4b, o0b), writes=(oo4b[oj],))
                    if jj == 3:
                        for j2 in range(4):
                            kb.op("act", lambda e, j2=j2, oj=oj: e.activation(out=jk[:, :], in_=oo4[oj][:, j2, :], func=AF.Square,
                                                                            accum_out=ss4[:, j2:j2 + 1]),
                                  reads=(oo4b[oj],), writes=(jkb, ss4b))
                        kb.op("dve", lambda e: e.tensor_scalar(out=ss4[:, 4:8], in0=ss4[:, 0:4], scalar1=1.0 / 128, scalar2=EPS,
                                                               op0=ALU.mult, op1=ALU.add), reads=(ss4b,), writes=(ss4b,))
                        kb.op("act", lambda e: e.activation(out=ss4[:, 4:8], in_=ss4[:, 4:8], func=AF.Sqrt), reads=(ss4b,), writes=(ss4b,))
                        kb.op("dve", lambda e: e.reciprocal(out=ss4[:, 4:8], in_=ss4[:, 4:8]), reads=(ss4b,), writes=(ss4b,))
                        for j2 in range(4):
                            kb.op("dve", lambda e, j2=j2, oj=oj: e.scalar_tensor_tensor(
                                out=ob[oj][:, j2, :], in0=oo4[oj][:, j2, :], scalar=ss4[:, 4 + j2:5 + j2], in1=gs[:, :],
                                op0=ALU.mult, op1=ALU.mult), reads=(oo4b[oj], ss4b, gsb), writes=(obb[oj],))
                        r0 = s * S + (n - 3) * 128
                        kb.dma("sp", mix[r0:r0 + 512, 512 + h * 128:512 + (h + 1) * 128].rearrange("(j p) d -> p j d", p=128),
                               ob[oj][:, :, :], obb[oj], load=False)

                kprev = emit_qk(groups[0])
                for gi_, g in enumerate(groups):
                    knext = emit_qk(groups[gi_ + 1]) if gi_ + 1 < len(groups) else None
                    emit_pv(g, kprev)
                    n_, mg_, mc_ = g
                    if mg_ + mc_ - 1 == n_:
                        emit_combine(n_)
                    kprev = knext
                oi_[0] += NT // 4
        kb.barrier()
        kb.release(qtb + ktb + obb + [lpb, gsb, mkb, vbb])


A_PATTERNS = ((128, 1), (512, 4), (2048, 16))


def _colsel(ap2, r, d, nb):
    if d == 1:
        return ap2[:, nb * 128:(nb + 1) * 128]
    return ap2.rearrange("p (j d) -> p j d", d=d)[:, nb * 128:(nb + 1) * 128, r]


def _rowsel(ap, r, d, nb):
    if d == 1:
        return ap[nb * 128:(nb + 1) * 128]
    if len(ap.shape) == 2:
        return ap.rearrange("(j d) c -> j d c", d=d)[nb * 128:(nb + 1) * 128, r, :]
    return ap.rearrange("(j d) h e -> j d h e", d=d)[nb * 128:(nb + 1) * 128, r, :, :]


def phase_attn_a(kb, QT, KT, V, mask4_d, NZ, nseq, S):
    nc = kb.nc
    with ExitStack() as es:
        mk = es.enter_context(kb.sbt("aa_mk", [128, 512], BF16))
        mkb = Buf()
        kb.dma("sp", mk[:, :], mask4_d, mkb)
        qt = [es.enter_context(kb.sbt("aa_qt%d" % i, [128, S], BF16)) for i in range(4)]
        qtb = [Buf() for _ in range(4)]
        kt = [es.enter_context(kb.sbt("aa_kt%d" % i, [128, S], BF16)) for i in range(4)]
        ktb = [Buf() for _ in range(4)]
        NV = 3
        vt = [es.enter_context(kb.sbt("aa_vt%d" % i, [128, 8, 72], BF16)) for i in range(NV)]
        vtb = [Buf() for _ in range(NV)]
        for i in range(NV):
            kb.op("pool", lambda e, i=i: e.memset(vt[i][:, :, 64:65], 1.0), writes=(vtb[i],))
        NP = 4
        pt = [es.enter_context(kb.sbt("aa_pt%d" % i, [128, 512], BF16)) for i in range(NP)]
        ptb = [Buf() for _ in range(NP)]
        ot = [es.enter_context(kb.sbt("aa_ot%d" % i, [128, 8, 65], F32)) for i in range(2)]
        otb = [Buf() for _ in range(2)]
        pj = [0]
        oi = 0
        vi = 0
        for s in range(nseq):
            for hp in range(4):
                kb.dma("sp", qt[hp][:, :], QT[s, hp, :, :], qtb[hp])
                kb.dma("sp", kt[hp][:, :], KT[s, hp, :, :], ktb[hp])
            units = []
            for p, (window, d) in enumerate(A_PATTERNS):
                nblk = S // d // 128
                for r in range(d):
                    prev = None
                    for nb in range(nblk):
                        cur = vi % NV
                        vi += 1
                        oj = oi % 2
                        oi += 1
                        ab = 4 + 2 * (oi % 2)
                        for hpp in range(2):
                            units.append(dict(p=p, d=d, r=r, nb=nb, cur=cur, prev=prev, oj=oj, ab=ab, hpp=hpp,
                                              first=(hpp == 0), last=(hpp == 1)))
                        prev = cur

            def emit_qk(u, s=s):
                d, r, nb, cur, hpp = u["d"], u["r"], u["nb"], u["cur"], u["hpp"]
                if u["first"]:
                    kb.dma("sp", vt[cur][:, :, 0:64],
                           _rowsel(V[s * S:(s + 1) * S, 0:512], r, d, nb).rearrange("p (h e) -> p h e", e=64), vtb[cur])
                for hl in range(2):
                    hp = 2 * hpp + hl
                    for x in range(2):
                        for hh in range(2):
                            bi = 2 * hpp + hh
                            bank = kb.bank(bi)
                            ps = slice(64 * hh, 64 * hh + 64)
                            qc_ = _colsel(qt[hp][ps, :], r, d, nb)
                            kx_ = _colsel(kt[hp][ps, :], r, d, nb if x == 0 else max(nb - 1, 0))
                            c0_ = hl * 256 + (128 if x == 0 else 0)
                            kb.op("pe", lambda e, bank=bank, c0_=c0_, kx_=kx_, qc_=qc_: e.matmul(
                                bank[:, c0_:c0_ + 128], lhsT=kx_, rhs=qc_, start=True, stop=True),
                                reads=(ktb[hp], qtb[hp]), writes=(kb.pbuf[bi],))
                ks = []
                for hh in range(2):
                    bi = 2 * hpp + hh
                    bank = kb.bank(bi)
                    k = pj[0] % NP
                    pj[0] += 1
                    ks.append(k)
                    kb.op("act", lambda e, bank=bank, k=k: e.activation(out=pt[k][:, :], in_=bank, func=AF.Exp),
                          reads=(kb.pbuf[bi],), writes=(ptb[k],))
                    kb.op("pool", lambda e, k=k: e.tensor_tensor(out=pt[k][:, :], in0=pt[k][:, :], in1=mk[:, :], op=ALU.mult),
                          reads=(ptb[k], mkb), writes=(ptb[k],))
                return ks

            def emit_pv(u, ks, s=s):
                p, d, r, nb, cur, prev, oj, ab, hpp = (u[x] for x in ("p", "d", "r", "nb", "cur", "prev", "oj", "ab", "hpp"))
                for hh in range(2):
                    k = ks[hh]
                    for hl in range(2):
                        hp = 2 * hpp + hl
                        head = 2 * hp + hh
                        acc = kb.bank(ab + head // 4)[:, (head % 4) * 128:(head % 4) * 128 + 65]
                        abuf = kb.pbuf[ab + head // 4]
                        kb.op("pe", lambda e, acc=acc, k=k, hl=hl, cur=cur, head=head, nb=nb: e.matmul(
                            acc, lhsT=pt[k][:, hl * 256 + 128:hl * 256 + 256], rhs=vt[cur][:, head, 0:65],
                            start=True, stop=(nb == 0)), reads=(ptb[k], vtb[cur]), writes=(abuf,))
                        if nb > 0:
                            kb.op("pe", lambda e, acc=acc, k=k, hl=hl, prev=prev, head=head: e.matmul(
                                acc, lhsT=pt[k][:, hl * 256:hl * 256 + 128], rhs=vt[prev][:, head, 0:65],
                                start=False, stop=True), reads=(ptb[k], vtb[prev]), writes=(abuf,))
                if u["last"]:
                    for half in range(2):
                        src = kb.bank(ab + half)[:, :].rearrange("p (h e) -> p h e", e=128)[:, :, 0:65]
                        if half == 0:
                            kb.op("act", lambda e, src=src, oj=oj: e.copy(out=ot[oj][:, 0:4, :], in_=src),
                                  reads=(kb.pbuf[ab],), writes=(otb[oj],))
                        else:
                            kb.op("dve", lambda e, src=src, oj=oj: e.tensor_copy(out=ot[oj][:, 4:8, :], in_=src),
                                  reads=(kb.pbuf[ab + 1],), writes=(otb[oj],))
                    kb.dma("sp", _rowsel(NZ[p, s * S:(s + 1) * S, :, :], r, d, nb), ot[oj][:, :, :], otb[oj], load=False)

            kprev = emit_qk(units[0])
            for ui, u in enumerate(units):
                knext = emit_qk(units[ui + 1]) if ui + 1 < len(units) else None
                emit_pv(u, kprev)
                kprev = knext
        kb.barrier()
        kb.release(vtb + otb + [mkb] + qtb + ktb)


def phase_attn_a_combine(kb, NZ, mix, ntok):
    nc = kb.nc
    with ExitStack() as es:
        nz = [[es.enter_context(kb.sbt("ac_nz%d_%d" % (i, p), [128, 8, 65], F32)) for p in range(3)] for i in range(2)]
        nzb = [[Buf() for p in range(3)] for i in range(2)]
        rz = es.enter_context(kb.sbt("ac_rz", [128, 8], F32))
        rzb = Buf()
        oa = [es.enter_context(kb.sbt("ac_oa%d" % i, [128, 8, 64], BF16)) for i in range(2)]
        oab = [Buf() for _ in range(2)]
        for t in range(ntok // 128):
            j = t % 2
            r0 = t * 128
            for p in range(3):
                kb.dma("sp", nz[j][p][:, :, :], NZ[p, r0:r0 + 128, :, :], nzb[j][p])
            kb.op("dve", lambda e, j=j: e.tensor_tensor(out=nz[j][0][:, :, :], in0=nz[j][0][:, :, :], in1=nz[j][1][:, :, :], op=ALU.add),
                  reads=(nzb[j][0], nzb[j][1]), writes=(nzb[j][0],))
            kb.op("dve", lambda e, j=j: e.tensor_tensor(out=nz[j][0][:, :, :], in0=nz[j][0][:, :, :], in1=nz[j][2][:, :, :], op=ALU.add),
                  reads=(nzb[j][0], nzb[j][2]), writes=(nzb[j][0],))
            kb.op("dve", lambda e, j=j: e.reciprocal(out=rz[:, :], in_=nz[j][0][:, :, 64]), reads=(nzb[j][0],), writes=(rzb,))
            for h in range(8):
                eng = "act" if h % 2 == 0 else "dve"
                if eng == "act":
                    kb.op("act", lambda e, j=j, h=h: e.activation(out=oa[j][:, h, :], in_=nz[j][0][:, h, 0:64], func=AF.Copy,
                                                                scale=rz[:, h:h + 1]), reads=(nzb[j][0], rzb), writes=(oab[j],))
                else:
                    kb.op("dve", lambda e, j=j, h=h: e.tensor_scalar(out=oa[j][:, h, :], in0=nz[j][0][:, h, 0:64], scalar1=rz[:, h:h + 1],
                                                                   scalar2=None, op0=ALU.mult), reads=(nzb[j][0], rzb), writes=(oab[j],))
            kb.dma("sp", mix[r0:r0 + 128, 0:512], oa[j][:, :, :].rearrange("p h e -> p (h e)"), oab[j], load=False)
        kb.barrier()
        kb.release([b for bb in nzb for b in bb] + oab)


def host_consts(S):
    bf = ml_dtypes.bfloat16
    ident = np.eye(128, dtype=np.float32).astype(bf)
    kk = np.arange(128)[:, None]
    qq = np.arange(128)[None, :]
    mcur = (kk <= qq).astype(np.float32)
    mprev = (kk >= qq).astype(np.float32)
    mask4 = np.concatenate([mprev, mcur, mprev, mcur], axis=1).astype(bf)
    half = 8
    inv_freq = (500000.0 ** (-np.arange(half, dtype=np.float32) / half)).astype(np.float32)
    ang = np.arange(S, dtype=np.float32)[:, None] * inv_freq[None, :]
    cos = np.tile(np.cos(ang).astype(np.float32), (1, 8))
    sin = np.tile(np.sin(ang).astype(np.float32), (1, 8))
    ss = np.arange(128)[:, None]
    tt = np.arange(128)[None, :]
    rep4 = lambda m: np.ascontiguousarray(np.repeat(m[:, None, :], 4, axis=1)).astype(bf)
    return dict(c_ident=ident, c_maskc=mcur.astype(bf), c_mask4=mask4, c_cos=cos, c_sin=sin,
                c_tri=(ss <= tt).astype(np.float32), c_ones32=np.ones((128, 128), np.float32),
                c_ms4=rep4((ss < tt).astype(np.float32)), c_mi4=rep4((ss <= tt).astype(np.float32)),
                c_mst4=rep4((ss > tt).astype(np.float32)), c_id4=rep4(np.eye(128, dtype=np.float32)),
                c_hm=np.stack([(np.arange(128) < 64), (np.arange(128) >= 64)], axis=1).astype(np.float32),
                c_bd=((ss // 64) == (tt // 64)).astype(np.float32),
                c_mo=np.stack([((ss // (2 * m)) == (tt // (2 * m))) & ((ss // m) % 2 == 0) & ((tt // m) % 2 == 1)
                               for m in (1, 2, 4, 8, 16, 32, 64)], axis=1).astype(np.float32).astype(bf),
                c_moT=np.stack([((ss // (2 * m)) == (tt // (2 * m))) & ((tt // m) % 2 == 0) & ((ss // m) % 2 == 1)
                                for m in (1, 2, 4, 8, 16, 32, 64)], axis=1).astype(np.float32).astype(bf))


def phase_rwkv(kb, h_in, W, ident_d, CN, lnw_d, lnb_d, mix, nseq, S):
    nc = kb.nc
    NT = S // 128
    C1 = -math.exp(-0.5)
    with ExitStack() as es:
        ident = es.enter_context(kb.sbt("rp_ident", [128, 128], BF16))
        identb = Buf()
        kb.dma("sp", ident[:, :], ident_d, identb)
        gcol = load_cols(kb, es, W["g"], "rp_gcol")
        mucol = load_cols(kb, es, W["mu"].rearrange("a b -> (a b)"), "rp_mucol")
        wr, wrb = load_weight_bf16(kb, es, W["w_r"], D, D, "rp_wr", gcol=gcol)
        wk, wkb = load_weight_bf16(kb, es, W["w_k"], D, D, "rp_wk", gcol=gcol)
        wv, wvb = load_weight_bf16(kb, es, W["w_v"], D, D, "rp_wv", gcol=gcol)
        w1, w1b = load_weight_bf16(kb, es, W["w1"], D, 64, "rp_w1", gcol=gcol)
        a1, a1b = load_weight_bf16(kb, es, W["a1"], D, 64, "rp_a1", gcol=gcol)
        g1, g1b = load_weight_bf16(kb, es, W["g1"], D, 160, "rp_g1", gcol=gcol)
        w2, w2b = load_weight_bf16(kb, es, W["w2"], 64, D, "rp_w2")
        a2, a2b = load_weight_bf16(kb, es, W["a2"], 64, D, "rp_a2")
        g2, g2b = load_weight_bf16(kb, es, W["g2"], 160, D, "rp_g2")
        w0, w0b = load_bcast(kb, es, W["w0"], "rp_w0")
        a0, a0b = load_bcast(kb, es, W["a0"], "rp_a0")
        kkw, kkwb = load_bcast(kb, es, W["k_k"], "rp_kk")
        kaw, kawb = load_bcast(kb, es, W["k_a"], "rp_ka")
        rkw, rkwb = load_bcast(kb, es, W["r_k"].rearrange("a b -> (a b)"), "rp_rk")

        def T(name, shape, dt):
            return es.enter_context(kb.sbt(name, shape, dt)), Buf()
        hi, hib = T("rp_hi", [128, D], F32)
        nb_, nbb = T("rp_nb", [128, D], BF16)
        nT, nTb = T("rp_nT", [128, 8, 128], BF16)
        nTs, nTsb = T("rp_nTs", [128, 8, 128], BF16)
        car, carb = T("rp_car", [128, 8, 1], BF16)
        xx, xxb = T("rp_xx", [128, 8, 128], BF16)
        xi = [T("rp_xi%d" % i, [128, 8, 128], BF16) for i in range(6)]
        junk, junkb = T("rp_junk", [128, D], F32)
        stt, sttb = T("rp_st", [128, 2], F32)
        tr, trb = T("rp_r", [128, D], F32)
        tk, tkb = T("rp_k", [128, D], F32)
        tv, tvb = T("rp_v", [128, D], F32)
        tw, twb = T("rp_w", [128, D], F32)
        ta, tab = T("rp_a", [128, D], F32)
        tg, tgb = T("rp_g", [128, D], F32)
        t1, t1b = T("rp_t1", [128, D], F32)
        t2, t2b = T("rp_t2", [128, D], F32)
        t3, t3b = T("rp_t3", [128, D], F32)
        sm, smb = T("rp_sm", [128, 48], F32)
        lw, lwb = T("rp_lw", [64, 128], BF16)
        la, lab = T("rp_la", [64, 128], BF16)
        lg, lgb = T("rp_lg", [128, 2, 128], BF16)
        lnw, lnwb = load_bcast(kb, es, lnw_d, "rp_lnw")
        lnb, lnbb = load_bcast(kb, es, lnb_d, "rp_lnb")
        tri, trib = T("rp_tri", [128, 128], F32)
        kb.dma("sp", tri[:, :], CN["c_tri"], trib)
        on32, on32b = T("rp_on32", [128, 128], F32)
        kb.dma("sp", on32[:, :], CN["c_ones32"], on32b)
        ms4, ms4b = T("rp_ms4", [128, 4, 128], BF16)
        kb.dma("sp", ms4[:, :, :], CN["c_ms4"], ms4b)
        mi4, mi4b = T("rp_mi4", [128, 4, 128], BF16)
        kb.dma("sp", mi4[:, :, :], CN["c_mi4"], mi4b)
        mst4, mst4b = T("rp_mst4", [128, 4, 128], BF16)
        kb.dma("sp", mst4[:, :, :], CN["c_mst4"], mst4b)
        id4, id4b = T("rp_id4", [128, 4, 128], BF16)
        kb.dma("sp", id4[:, :, :], CN["c_id4"], id4b)
        hm, hmb = T("rp_hm", [128, 2], F32)
        kb.dma("sp", hm[:, :], CN["c_hm"], hmb)
        bdm, bdmb = T("rp_bdm", [128, 128], F32)
        kb.dma("sp", bdm[:, :], CN["c_bd"], bdmb)
        stmp, stmpb = T("rp_stmp", [128, 128], F32)
        tbo, tbob = T("rp_bo", [128, D], F32)
        Atm, Atmb = T("rp_Atm", [128, D], BF16)
        Z0, Z0b = T("rp_Z0", [128, 8, 2, 128], BF16)
        Z1, Z1b = T("rp_Z1", [128, 8, 2, 128], BF16)
        RT, RTb = T("rp_RT", [128, 8, 128], BF16)
        BT, BTb = T("rp_BT", [128, 8, 128], BF16)
        KT_, KTb = T("rp_KT", [128, 8, 128], BF16)
        Xt, Xtb = T("rp_X", [128, 8, 128], BF16)
        Nq = [T("rp_N%d" % i, [128, 4, 128], BF16) for i in range(2)]
        NTq = [T("rp_NT%d" % i, [128, 4, 128], BF16) for i in range(2)]
        Tq = [T("rp_T%d" % i, [128, 4, 128], BF16) for i in range(2)]
        TTq = [T("rp_TT%d" % i, [128, 4, 128], BF16) for i in range(2)]
        mo, mob = T("rp_mo", [128, 7, 128], BF16)
        kb.dma("sp", mo[:, :, :], CN["c_mo"], mob)
        moT, moTb = T("rp_moT", [128, 7, 128], BF16)
        kb.dma("sp", moT[:, :, :], CN["c_moT"], moTb)
        Abr, Abrb = T("rp_Abr", [128, 4, 128], BF16)
        Aak, Aakb = T("rp_Aak", [128, 4, 128], BF16)
        Akr, Akrb = T("rp_Akr", [128, 4, 128], BF16)
        W2t, W2tb = T("rp_W2", [128, 4, 64], BF16)
        Ut = [T("rp_U%d" % i, [128, 128], BF16) for i in range(2)]
        Sm, Smb = T("rp_Sm", [128, 8, 128], F32)
        Sb, Sbb = T("rp_Sb", [128, 8, 128], BF16)
        PL, PLb = T("rp_PL", [128, 8], F32)
        yt, ytb = T("rp_y", [128, D], F32)
        om, omb = Atm, Atmb
        def v3t(tb):
            return (tb[0][:, :, :], tb[1])
        set0 = dict(N0=v3t(Nq[0]), N0T=v3t(NTq[0]), Aq=v3t(Nq[1]), Cq=v3t(NTq[1]), T=[v3t(Tq[0]), v3t(Tq[1])],
                    TT=[v3t(TTq[0]), v3t(TTq[1])], Abr=(Abr[:, :, :], Abrb), Aak=(Aak[:, :, :], Aakb), Akr=(Akr[:, :, :], Akrb),
                    W2=(W2t[:, :, :], W2tb), U=[(Ut[0][0][:, :], Ut[0][1]), (Ut[1][0][:, :], Ut[1][1])], stmp=(stmp[:, :], stmpb))

        def carve(tile, i):
            return tile[:, :].bitcast(BF16)[:, i * 512:(i + 1) * 512]

        pool_chunks = [carve(tt_, i) for tt_ in (t1, t3, tw, tk, ta, junk, tr, tv) for i in range(4)]

        def make_set(ch, stmp_view):
            c3 = [(x.rearrange("p (h t) -> p h t", t=128), Buf()) for x in ch[:9]]
            last = ch[9]
            w2c = (last[:, 0:256].rearrange("p (h i) -> p h i", i=64), Buf())
            u0c = (last[:, 256:384], Buf())
            u1c = (last[:, 384:512], Buf())
            sv = (stmp_view, Buf())
            d_ = dict(N0=c3[0], N0T=c3[1], Aq=c3[2], Cq=c3[2], T=[c3[3], c3[4]], TT=[c3[5], c3[5]], Abr=c3[6], Aak=c3[7], Akr=c3[8],
                      W2=w2c, U=[u0c, u1c], stmp=sv)
            return d_, c3 + [w2c, u0c, u1c, sv]

        set1, kids1 = make_set(pool_chunks[0:10], pool_chunks[30][:, 0:256].bitcast(F32))
        set2, kids2 = make_set(pool_chunks[10:20], pool_chunks[30][:, 256:512].bitcast(F32))
        set3, kids3 = make_set(pool_chunks[20:30], pool_chunks[31][:, 0:256].bitcast(F32))
        sets = [set0, set1, set2, set3]
        set1_list = kids1 + kids2 + kids3
        alias_parents = (t1b, t3b, twb, tkb, tab, junkb, trb, tvb)
        gi = [0]

        def gbank():
            bi = 2 + (gi[0] % 5)
            gi[0] += 1
            return bi
        qi_ = [0]

        def qbank():
            bi = qi_[0] % 6
            qi_[0] += 1
            return bi

        def flat(tb):
            return tb[0][:, :, :].rearrange("p c t -> p (c t)")
        st = {"i": 0}
        pi = [0]

        def mm_tok(xT, xTb, wsb, wb, evac):
            for half in range(2):
                bi = 2 + (pi[0] % 5)
                pi[0] += 1
                pap = kb.bank(bi)
                for c in range(8):
                    kb.op("pe", lambda e, c=c, half=half, pap=pap: e.matmul(
                        pap, lhsT=xT[:, c, :], rhs=wsb[:, c, half * 512:(half + 1) * 512], start=(c == 0), stop=(c == 7)),
                        reads=(xTb, wb), writes=(kb.pbuf[bi],))
                evac(pap, kb.pbuf[bi], half)

        def hv(t):
            return t[:, :].rearrange("p (h j) -> p h j", j=64)

        ntile = nseq * NT

        def front_a(g):
            r0_ = g * 128
            kb.dma("sp", hi[:, :], h_in[r0_:r0_ + 128, :], hib)
            rms_rstd(kb, hi[:, :], hib, nb_[:, :], nbb, stt[:, 0:1], stt[:, 1:2], sttb, D, EPS)
            kb.op("dve", lambda e: e.tensor_scalar(out=nb_[:, :], in0=hi[:, :], scalar1=stt[:, 1:2], scalar2=None, op0=ALU.mult),
                  reads=(hib, sttb), writes=(nbb,))

        def front_b(g):
            if g % NT == 0:
                kb.op("dve", lambda e: e.memset(car[:, :, :], 0.0), writes=(carb,))
            transpose_tile(kb, lambda c: nb_[:, c * 128:(c + 1) * 128], nbb, 8, lambda c0, n: nT[:, c0:c0 + n, :], nTb,
                           ident[:, :], identb, (0, 1), st)
            kb.op("pool", lambda e: e.tensor_copy(out=nTs[:, :, 1:128], in_=nT[:, :, 0:127]), reads=(nTb,), writes=(nTsb,))
            kb.op("pool", lambda e: e.tensor_copy(out=nTs[:, :, 0:1], in_=car[:, :, :]), reads=(carb,), writes=(nTsb,))
            kb.op("pool", lambda e: e.tensor_copy(out=car[:, :, :], in_=nT[:, :, 127:128]), reads=(nTb,), writes=(carb,))
            kb.op("dve", lambda e: e.tensor_tensor(out=xx[:, :, :], in0=nTs[:, :, :], in1=nT[:, :, :], op=ALU.subtract),
                  reads=(nTsb, nTb), writes=(xxb,))

        front_a(0)
        front_b(0)
        for s in range(nseq):
            kb.op("pool", lambda e: e.memset(Sm[:, :, :], 0.0), writes=(Smb,))
            kb.op("pool", lambda e: e.memset(Sb[:, :, :], 0.0), writes=(Sbb,))
            for t in range(NT):
                r0 = s * S + t * 128
                gt = s * NT + t
                for i in range(6):
                    eng = "pool" if i % 3 != 2 else "dve"
                    kb.op(eng, lambda e, i=i: e.tensor_tensor(out=xi[i][0][:, :, :], in0=xx[:, :, :],
                                                            in1=mucol[0][:, i * 8:(i + 1) * 8].unsqueeze(2).broadcast_to([128, 8, 128]),
                                                            op=ALU.mult), reads=(xxb, mucol[1]), writes=(xi[i][1],))
                    kb.op(eng, lambda e, i=i: e.tensor_tensor(out=xi[i][0][:, :, :], in0=xi[i][0][:, :, :], in1=nT[:, :, :], op=ALU.add),
                          reads=(xi[i][1], nTb), writes=(xi[i][1],))
                XR, XW, XK, XV, XA, XG = xi
                mm_tok(XR[0], XR[1], wr, wrb, lambda pap, pb, half: kb.op(
                    "act", lambda e: e.copy(out=tr[:, half * 512:(half + 1) * 512], in_=pap), reads=(pb,), writes=(trb,)))
                mm_tok(XK[0], XK[1], wk, wkb, lambda pap, pb, half: kb.op(
                    "act", lambda e: e.copy(out=tk[:, half * 512:(half + 1) * 512], in_=pap), reads=(pb,), writes=(tkb,)))
                mm_tok(XV[0], XV[1], wv, wvb, lambda pap, pb, half: kb.op(
                    "act", lambda e: e.copy(out=tv[:, half * 512:(half + 1) * 512], in_=pap), reads=(pb,), writes=(tvb,)))
                for (X, l1, l1b, dst, dstb, fn) in ((XW, w1, w1b, lw, lwb, AF.Tanh), (XA, a1, a1b, la, lab, AF.Copy)):
                    bi = 2 + (pi[0] % 5)
                    pi[0] += 1
                    pap = kb.bank(bi)[0:64, 0:128]
                    for c in range(8):
                        kb.op("pe", lambda e, c=c, pap=pap, X=X, l1=l1: e.matmul(pap, lhsT=l1[:, c, :], rhs=X[0][:, c, :],
                                                                              start=(c == 0), stop=(c == 7)),
                              reads=(X[1], l1b), writes=(kb.pbuf[bi],))
                    kb.op("act", lambda e, pap=pap, dst=dst, fn=fn: e.activation(out=dst[:, :], in_=pap, func=fn),
                          reads=(kb.pbuf[bi],), writes=(dstb,))
                for mc, (m0, mr) in enumerate(((0, 128), (128, 32))):
                    bi = 2 + (pi[0] % 5)
                    pi[0] += 1
                    pap = kb.bank(bi)[0:mr, 0:128]
                    for c in range(8):
                        kb.op("pe", lambda e, c=c, pap=pap, m0=m0, mr=mr: e.matmul(pap, lhsT=g1[:, c, m0:m0 + mr], rhs=XG[0][:, c, :],
                                                                               start=(c == 0), stop=(c == 7)),
                              reads=(XG[1], g1b), writes=(kb.pbuf[bi],))
                    kb.op("act", lambda e, pap=pap, mc=mc, mr=mr: e.activation(out=lg[0:mr, mc, :], in_=pap, func=AF.Sigmoid),
                          reads=(kb.pbuf[bi],), writes=(lgb,))
                for half in range(2):
                    hs = slice(half * 512, (half + 1) * 512)
                    bi = 2 + (pi[0] % 5)
                    pi[0] += 1
                    pap = kb.bank(bi)
                    kb.op("pe", lambda e, pap=pap, hs=hs: e.matmul(pap, lhsT=lw[:, :], rhs=w2[0:64, 0, hs], start=True, stop=True),
                          reads=(lwb, w2b), writes=(kb.pbuf[bi],))
                    kb.op("dve", lambda e, pap=pap, hs=hs: e.tensor_tensor(out=tw[:, hs], in0=pap, in1=w0[:, hs], op=ALU.add),
                          reads=(kb.pbuf[bi], w0b), writes=(twb,))
                    bi = 2 + (pi[0] % 5)
                    pi[0] += 1
                    pap = kb.bank(bi)
                    kb.op("pe", lambda e, pap=pap, hs=hs: e.matmul(pap, lhsT=la[:, :], rhs=a2[0:64, 0, hs], start=True, stop=True),
                          reads=(lab, a2b), writes=(kb.pbuf[bi],))
                    kb.op("dve", lambda e, pap=pap, hs=hs: e.tensor_tensor(out=ta[:, hs], in0=pap, in1=a0[:, hs], op=ALU.add),
                          reads=(kb.pbuf[bi], a0b), writes=(tab,))
                    bi = 2 + (pi[0] % 5)
                    pi[0] += 1
                    pap = kb.bank(bi)
                    kb.op("pe", lambda e, pap=pap, hs=hs: e.matmul(pap, lhsT=lg[:, 0, :], rhs=g2[:, 0, hs], start=True, stop=False),
                          reads=(lgb, g2b), writes=(kb.pbuf[bi],))
                    kb.op("pe", lambda e, pap=pap, hs=hs: e.matmul(pap, lhsT=lg[0:32, 1, :], rhs=g2[0:32, 1, hs], start=False, stop=True),
                          reads=(lgb, g2b), writes=(kb.pbuf[bi],))
                    kb.op("act", lambda e, pap=pap, hs=hs: e.copy(out=tg[:, hs], in_=pap), reads=(kb.pbuf[bi],), writes=(tgb,))
                XRb, XWb, XKb, XVb, XAb, XGb = xi
                Rtm, Bh, Kh, Bc, Kc, Vb = [(flat(x), x[1]) for x in xi]
                kb.op("act", lambda e: e.activation(out=tw[:, :], in_=tw[:, :], func=AF.Sigmoid), reads=(twb,), writes=(twb,))
                kb.op("act", lambda e: e.mul(out=tw[:, :], in_=tw[:, :], mul=C1), reads=(twb,), writes=(twb,))
                kb.op("act", lambda e: e.activation(out=ta[:, :], in_=ta[:, :], func=AF.Sigmoid), reads=(tab,), writes=(tab,))
                kb.op("dve", lambda e: e.tensor_tensor(out=t1[:, :], in0=tk[:, :], in1=kkw[:, :], op=ALU.mult), reads=(tkb, kkwb), writes=(t1b,))
                kb.op("pool", lambda e: e.tensor_tensor(out=t2[:, :], in0=t1[:, :], in1=t1[:, :], op=ALU.mult), reads=(t1b,), writes=(t2b,))
                kb.op("dve", lambda e: e.reduce_sum(out=sm[:, 0:16], in_=hv(t2), axis=AX.X), reads=(t2b,), writes=(smb,))
                kb.op("dve", lambda e: e.tensor_scalar(out=sm[:, 0:16], in0=sm[:, 0:16], scalar1=1e-24, scalar2=None, op0=ALU.max),
                      reads=(smb,), writes=(smb,))
                kb.op("act", lambda e: e.activation(out=sm[:, 0:16], in_=sm[:, 0:16], func=AF.Sqrt), reads=(smb,), writes=(smb,))
                kb.op("dve", lambda e: e.reciprocal(out=sm[:, 0:16], in_=sm[:, 0:16]), reads=(smb,), writes=(smb,))
                kb.op("dve", lambda e: e.tensor_scalar(out=sm[:, 0:16], in0=sm[:, 0:16], scalar1=-1.0, scalar2=None, op0=ALU.mult),
                      reads=(smb,), writes=(smb,))
                kb.op("dve", lambda e: e.tensor_tensor(out=hv(t2), in0=hv(t1), in1=sm[:, 0:16].unsqueeze(2).broadcast_to([128, 16, 64]),
                                                       op=ALU.mult), reads=(t1b, smb), writes=(t2b,))
                kb.op("dve", lambda e: e.scalar_tensor_tensor(out=t3[:, :], in0=t2[:, :], scalar=-1.0, in1=ta[:, :], op0=ALU.mult, op1=ALU.mult),
                      reads=(t2b, tab), writes=(t3b,))
                kb.op("dve", lambda e: e.scalar_tensor_tensor(out=t1[:, :], in0=ta[:, :], scalar=-1.0, in1=kaw[:, :], op0=ALU.add, op1=ALU.mult),
                      reads=(tab, kawb), writes=(t1b,))
                kb.op("dve", lambda e: e.scalar_tensor_tensor(out=t1[:, :], in0=t1[:, :], scalar=1.0, in1=tk[:, :], op0=ALU.add, op1=ALU.mult),
                      reads=(t1b, tkb), writes=(t1b,))
                kb.op("pool", lambda e: e.tensor_tensor(out=junk[:, :], in0=tr[:, :], in1=t1[:, :], op=ALU.mult), reads=(trb, t1b), writes=(junkb,))
                kb.op("pool", lambda e: e.tensor_tensor(out=junk[:, :], in0=junk[:, :], in1=rkw[:, :], op=ALU.mult), reads=(junkb, rkwb), writes=(junkb,))
                kb.op("dve", lambda e: e.reduce_sum(out=sm[:, 16:32], in_=hv(junk), axis=AX.X), reads=(junkb,), writes=(smb,))
                kb.op("pool", lambda e: e.tensor_tensor(out=hv(tbo), in0=hv(tv), in1=sm[:, 16:32].unsqueeze(2).broadcast_to([128, 16, 64]),
                                                        op=ALU.mult), reads=(tvb, smb), writes=(tbob,))
                epos, eposb = tk, tkb
                eneg, enegb = ta, tab
                etot, etotb = hi, hib
                enld, enldb = junk, junkb
                for half in range(2):
                    hs = slice(half * 512, (half + 1) * 512)
                    bi = gbank()
                    kb.op("pe", lambda e, bi=bi, hs=hs: e.matmul(kb.bank(bi), lhsT=tri[:, :], rhs=tw[:, hs], start=True, stop=True),
                          reads=(trib, twb), writes=(kb.pbuf[bi],))
                    kb.op("act", lambda e, bi=bi, hs=hs: e.activation(out=epos[:, hs], in_=kb.bank(bi), func=AF.Exp),
                          reads=(kb.pbuf[bi],), writes=(eposb,))
                    kb.op("act", lambda e, bi=bi, hs=hs: e.activation(out=eneg[:, hs], in_=kb.bank(bi), func=AF.Exp, scale=-1.0),
                          reads=(kb.pbuf[bi],), writes=(enegb,))
                    bi = gbank()
                    kb.op("pe", lambda e, bi=bi, hs=hs: e.matmul(kb.bank(bi), lhsT=on32[:, :], rhs=tw[:, hs], start=True, stop=True),
                          reads=(on32b, twb), writes=(kb.pbuf[bi],))
                    kb.op("act", lambda e, bi=bi, hs=hs: e.activation(out=etot[:, hs], in_=kb.bank(bi), func=AF.Exp),
                          reads=(kb.pbuf[bi],), writes=(etotb,))
                kb.op("act", lambda e: e.activation(out=enld[:, :], in_=tw[:, :], func=AF.Exp, scale=-1.0), reads=(twb,), writes=(enldb,))
                bi = gbank()
                for p in range(8):
                    kb.op("pe", lambda e, bi=bi, p=p: e.matmul(kb.bank(bi)[:, p:p + 1], lhsT=tw[:, p * 128:(p + 1) * 128], rhs=on32[:, 0:1],
                                                            start=True, stop=True), reads=(twb, on32b), writes=(kb.pbuf[bi],))
                kb.op("act", lambda e, bi=bi: e.activation(out=PL[:, :], in_=kb.bank(bi)[:, 0:8], func=AF.Exp), reads=(kb.pbuf[bi],), writes=(PLb,))
                kb.op("dve", lambda e: e.tensor_tensor(out=enld[:, :], in0=enld[:, :], in1=epos[:, :], op=ALU.mult), reads=(enldb, eposb), writes=(enldb,))
                kb.op("pool", lambda e: e.tensor_tensor(out=etot[:, :], in0=etot[:, :], in1=eneg[:, :], op=ALU.mult), reads=(etotb, enegb), writes=(etotb,))
                kb.op("dve", lambda e: e.tensor_tensor(out=Atm[:, :], in0=t2[:, :], in1=enld[:, :], op=ALU.mult), reads=(t2b, enldb), writes=(Atmb,))
                kb.op("pool", lambda e: e.tensor_tensor(out=Rtm[0], in0=tr[:, :], in1=epos[:, :], op=ALU.mult), reads=(trb, eposb), writes=(Rtm[1],))
                kb.op("dve", lambda e: e.tensor_tensor(out=Bh[0], in0=t3[:, :], in1=eneg[:, :], op=ALU.mult), reads=(t3b, enegb), writes=(Bh[1],))
                kb.op("pool", lambda e: e.tensor_tensor(out=Kh[0], in0=t1[:, :], in1=eneg[:, :], op=ALU.mult), reads=(t1b, enegb), writes=(Kh[1],))
                kb.op("dve", lambda e: e.tensor_tensor(out=Bc[0], in0=t3[:, :], in1=etot[:, :], op=ALU.mult), reads=(t3b, etotb), writes=(Bc[1],))
                kb.op("pool", lambda e: e.tensor_tensor(out=Kc[0], in0=t1[:, :], in1=etot[:, :], op=ALU.mult), reads=(t1b, etotb), writes=(Kc[1],))
                kb.op("act", lambda e: e.copy(out=Vb[0], in_=tv[:, :]), reads=(tvb,), writes=(Vb[1],))
                for (srct, which) in ((( Atm[:, :], Atmb), "A"), (Rtm, "R"), (Bh, "B"), (Kh, "K")):
                    sap, sbuf_ = srct
                    bi = st["i"] % 2
                    st["i"] += 1
                    pap = kb.bank(bi, BF16)
                    for c in range(8):
                        kb.op("pe", lambda e, c=c, pap=pap, sap=sap: e.transpose(out=pap[:, c * 128:(c + 1) * 128], in_=sap[:, c * 128:(c + 1) * 128],
                                                                              identity=ident[:, :]), reads=(sbuf_, identb), writes=(kb.pbuf[bi],))
                    p3 = pap.rearrange("p (c t) -> p c t", t=128)
                    if which == "A":
                        kb.op("dve", lambda e, p3=p3: e.tensor_scalar(out=Z0[:, :, 0, :], in0=p3, scalar1=hm[:, 0:1], scalar2=None, op0=ALU.mult),
                              reads=(kb.pbuf[bi], hmb), writes=(Z0b,))
                        kb.op("dve", lambda e, p3=p3: e.tensor_scalar(out=Z1[:, :, 0, :], in0=p3, scalar1=hm[:, 1:2], scalar2=None, op0=ALU.mult),
                              reads=(kb.pbuf[bi], hmb), writes=(Z1b,))
                    elif which == "R":
                        kb.op("act", lambda e, p3=p3: e.activation(out=Z0[:, :, 1, :], in_=p3, func=AF.Copy, scale=hm[:, 0:1]),
                              reads=(kb.pbuf[bi], hmb), writes=(Z0b,))
                        kb.op("act", lambda e, p3=p3: e.activation(out=Z1[:, :, 1, :], in_=p3, func=AF.Copy, scale=hm[:, 1:2]),
                              reads=(kb.pbuf[bi], hmb), writes=(Z1b,))
                        kb.op("act", lambda e, p3=p3: e.copy(out=RT[:, :, :], in_=p3), reads=(kb.pbuf[bi],), writes=(RTb,))
                    elif which == "B":
                        kb.op("act", lambda e, p3=p3: e.copy(out=BT[:, :, :], in_=p3), reads=(kb.pbuf[bi],), writes=(BTb,))
                    else:
                        kb.op("dve", lambda e, p3=p3: e.tensor_copy(out=KT_[:, :, :], in_=p3), reads=(kb.pbuf[bi],), writes=(KTb,))
                if gt + 1 < ntile:
                    front_a(gt + 1)
                kids = [b_ for (_, b_) in set1_list]
                kb.alias_acquire(alias_parents, kids)

                def hd(q, hq):
                    p = 2 * q + hq // 2
                    hh = hq % 2
                    Z, Zb = (Z0, Z0b) if hh == 0 else (Z1, Z1b)
                    return p, hh, Z, Zb

                def h4(bi):
                    return kb.bank(bi).rearrange("p (h t) -> p h t", t=128)

                def st_kinds(q, TS):
                    kinds = (("ab", ms4, ms4b, TS["N0"]), ("br", mi4, mi4b, TS["Abr"]), ("ak", ms4, ms4b, TS["Aak"]),
                             ("kr", mi4, mi4b, TS["Akr"]), ("nt", mst4, mst4b, TS["N0T"]))
                    for kind, mk_, mkb_, dst in kinds:
                        bi = qbank()
                        for hq in range(4):
                            p, hh, Z, Zb = hd(q, hq)
                            if kind == "ab":
                                l, lb, r_, rb = BT[:, p, :], BTb, Z[:, p, 0, :], Zb
                            elif kind == "br":
                                l, lb, r_, rb = BT[:, p, :], BTb, Z[:, p, 1, :], Zb
                            elif kind == "ak":
                                l, lb, r_, rb = KT_[:, p, :], KTb, Z[:, p, 0, :], Zb
                            elif kind == "kr":
                                l, lb, r_, rb = KT_[:, p, :], KTb, Z[:, p, 1, :], Zb
                            else:
                                l, lb, r_, rb = Z[:, p, 0, :], Zb, BT[:, p, :], BTb
                            kb.op("pe", lambda e, bi=bi, hq=hq, l=l, r_=r_: e.matmul(kb.bank(bi)[:, hq * 128:(hq + 1) * 128], lhsT=l, rhs=r_,
                                                                                  start=True, stop=True), reads=(lb, rb), writes=(kb.pbuf[bi],))
                        kb.op("dve", lambda e, bi=bi, mk_=mk_, dst=dst: e.tensor_tensor(out=dst[0], in0=h4(bi), in1=mk_[:, :, :], op=ALU.mult),
                              reads=(kb.pbuf[bi], mkb_), writes=(dst[1],))

                def st_init(q, TS):
                    kb.op("pool", lambda e: e.tensor_tensor(out=TS["Aq"][0], in0=TS["N0"][0], in1=mo[:, 0:1, :].broadcast_to([128, 4, 128]),
                                                          op=ALU.mult), reads=(TS["N0"][1], mob), writes=(TS["Aq"][1],))
                    kb.op("pool", lambda e: e.tensor_tensor(out=TS["T"][0][0], in0=TS["Aq"][0], in1=id4[:, :, :], op=ALU.add),
                          reads=(TS["Aq"][1], id4b), writes=(TS["T"][0][1],))
                    kb.op("pool", lambda e: e.tensor_tensor(out=TS["Cq"][0], in0=TS["N0T"][0], in1=moT[:, 0:1, :].broadcast_to([128, 4, 128]),
                                                          op=ALU.mult), reads=(TS["N0T"][1], moTb), writes=(TS["Cq"][1],))
                    kb.op("pool", lambda e: e.tensor_tensor(out=TS["TT"][0][0], in0=TS["Cq"][0], in1=id4[:, :, :], op=ALU.add),
                          reads=(TS["Cq"][1], id4b), writes=(TS["TT"][0][1],))
                    TS["ct"] = 0

                def st_lvl_a(q, TS, li):
                    ct = TS["ct"]
                    T_ = TS["T"][ct]
                    bi = qbank()
                    for hq in range(4):
                        kb.op("pe", lambda e, bi=bi, hq=hq: e.matmul(kb.bank(bi)[:, hq * 128:(hq + 1) * 128],
                              lhsT=TS["N0T"][0][:, hq, :], rhs=T_[0][:, hq, :], start=True, stop=True),
                              reads=(TS["N0T"][1], T_[1]), writes=(kb.pbuf[bi],))
                    kb.op("dve", lambda e, bi=bi: e.tensor_tensor(out=TS["Aq"][0], in0=h4(bi), in1=mo[:, li:li + 1, :].broadcast_to([128, 4, 128]),
                                                               op=ALU.mult), reads=(kb.pbuf[bi], mob), writes=(TS["Aq"][1],))

                def st_lvl_b(q, TS, li):
                    ct = TS["ct"]
                    T_, T2 = TS["T"][ct], TS["T"][1 - ct]
                    TT_ = TS["TT"][ct]
                    bi = qbank()
                    for hq in range(4):
                        kb.op("pe", lambda e, bi=bi, hq=hq: e.matmul(kb.bank(bi)[:, hq * 128:(hq + 1) * 128],
                              lhsT=TT_[0][:, hq, :], rhs=TS["Aq"][0][:, hq, :], start=True, stop=False),
                              reads=(TT_[1], TS["Aq"][1]), writes=(kb.pbuf[bi],))
                        kb.op("pe", lambda e, bi=bi, hq=hq: e.matmul(kb.bank(bi)[:, hq * 128:(hq + 1) * 128],
                              lhsT=TT_[0][:, hq, :], rhs=ident[:, :], start=False, stop=True),
                              reads=(TT_[1], identb), writes=(kb.pbuf[bi],))
                    kb.op("act", lambda e, bi=bi: e.copy(out=T2[0], in_=h4(bi)), reads=(kb.pbuf[bi],), writes=(T2[1],))
                    TS["ct"] = 1 - ct

                def st_lvl_c(q, TS, li):
                    ct = TS["ct"]
                    T2, TT2 = TS["T"][ct], TS["TT"][ct]
                    bi = qbank()
                    pv = kb.bank(bi, BF16)
                    for hq in range(4):
                        kb.op("pe", lambda e, pv=pv, hq=hq: e.transpose(out=pv[:, hq * 128:(hq + 1) * 128], in_=T2[0][:, hq, :], identity=ident[:, :]),
                              reads=(T2[1], identb), writes=(kb.pbuf[bi],))
                    kb.op("dve", lambda e, pv=pv: e.tensor_copy(out=TT2[0], in_=pv[:, 0:512].rearrange("p (h t) -> p h t", t=128)),
                          reads=(kb.pbuf[bi],), writes=(TT2[1],))

                def st_xw2(q, TS):
                    Tf = TS["T"][TS["ct"]]
                    bi = qbank()
                    for hq in range(4):
                        p, hh, Z, Zb = hd(q, hq)
                        kb.op("pe", lambda e, bi=bi, hq=hq, p=p: e.matmul(kb.bank(bi)[:, hq * 128:(hq + 1) * 128],
                              lhsT=Atm[:, p * 128:(p + 1) * 128], rhs=Tf[0][:, hq, :], start=True, stop=True),
                              reads=(Atmb, Tf[1]), writes=(kb.pbuf[bi],))
                    b4 = kb.bank(bi).rearrange("p (a b t) -> p a b t", b=2, t=128)
                    kb.op("dve", lambda e, b4=b4: e.tensor_scalar(out=Xt[:, 2 * q:2 * q + 2, :], in0=b4[:, :, 0, :], scalar1=hm[:, 0:1],
                                                                scalar2=None, op0=ALU.mult), reads=(kb.pbuf[bi], hmb), writes=(Xtb,))
                    kb.op("dve", lambda e, b4=b4: e.scalar_tensor_tensor(out=Xt[:, 2 * q:2 * q + 2, :], in0=b4[:, :, 1, :], scalar=hm[:, 1:2],
                                                                       in1=Xt[:, 2 * q:2 * q + 2, :], op0=ALU.mult, op1=ALU.add),
                          reads=(kb.pbuf[bi], hmb, Xtb), writes=(Xtb,))
                    bi = qbank()
                    for hq in range(4):
                        h = 4 * q + hq
                        kb.op("pe", lambda e, bi=bi, hq=hq, h=h: e.matmul(kb.bank(bi)[:, hq * 64:(hq + 1) * 64],
                              lhsT=TS["Aak"][0][:, hq, :], rhs=Vb[0][:, h * 64:(h + 1) * 64], start=True, stop=True),
                              reads=(TS["Aak"][1], Vb[1]), writes=(kb.pbuf[bi],))
                    kb.op("act", lambda e, bi=bi: e.copy(out=TS["W2"][0], in_=kb.bank(bi)[:, 0:256].rearrange("p (h i) -> p h i", i=64)),
                          reads=(kb.pbuf[bi],), writes=(TS["W2"][1],))

                def st_u(q, TS, pl):
                    Tf = TS["T"][TS["ct"]]
                    p = 2 * q + pl
                    U_, U_b = TS["U"][pl]
                    bi = qbank()
                    bu = kb.bank(bi)
                    kb.op("pe", lambda e, bu=bu, p=p: e.matmul(bu[:, 0:128], lhsT=Xt[:, p, :], rhs=Sb[:, p, :], start=True, stop=False),
                          reads=(Xtb, Sbb), writes=(kb.pbuf[bi],))
                    for hh in range(2):
                        hq = 2 * pl + hh
                        kb.op("pe", lambda e, bu=bu, hh=hh, hq=hq: e.matmul(bu[:, 64 * hh:64 * hh + 64], lhsT=Tf[0][:, hq, :],
                                                                         rhs=TS["W2"][0][:, hq, :], start=False, stop=(hh == 1)),
                              reads=(Tf[1], TS["W2"][1]), writes=(kb.pbuf[bi],))
                    kb.op("act", lambda e, bu=bu: e.copy(out=U_, in_=bu[:, 0:128]), reads=(kb.pbuf[bi],), writes=(U_b,))

                def st_ys(q, TS, pl):
                    p = 2 * q + pl
                    pcs = slice(p * 128, (p + 1) * 128)
                    U_, U_b = TS["U"][pl]
                    ybk = 7 - q // 2
                    yb_ = kb.bank(ybk)[:, (p % 4) * 128:(p % 4 + 1) * 128]
                    kb.op("pe", lambda e, yb_=yb_, p=p: e.matmul(yb_, lhsT=RT[:, p, :], rhs=Sb[:, p, :], start=True, stop=False),
                          reads=(RTb, Sbb), writes=(kb.pbuf[ybk],))
                    for hh in range(2):
                        hq = 2 * pl + hh
                        h = 4 * q + hq
                        kb.op("pe", lambda e, yb_=yb_, hh=hh, hq=hq: e.matmul(yb_[:, 64 * hh:64 * hh + 64], lhsT=TS["Abr"][0][:, hq, :],
                                                                           rhs=U_[:, 64 * hh:64 * hh + 64], start=False, stop=False),
                              reads=(TS["Abr"][1], U_b), writes=(kb.pbuf[ybk],))
                        kb.op("pe", lambda e, yb_=yb_, hh=hh, hq=hq, h=h: e.matmul(yb_[:, 64 * hh:64 * hh + 64], lhsT=TS["Akr"][0][:, hq, :],
                                                                                rhs=Vb[0][:, h * 64:(h + 1) * 64], start=False, stop=(hh == 1)),
                              reads=(TS["Akr"][1], Vb[1]), writes=(kb.pbuf[ybk],))
                    bi = qbank()
                    bs_ = kb.bank(bi)
                    kb.op("pe", lambda e, bs_=bs_, pcs=pcs: e.matmul(bs_[:, 0:128], lhsT=Bc[0][:, pcs], rhs=U_, start=True, stop=False),
                          reads=(Bc[1], U_b), writes=(kb.pbuf[bi],))
                    kb.op("pe", lambda e, bs_=bs_, pcs=pcs: e.matmul(bs_[:, 0:128], lhsT=Kc[0][:, pcs], rhs=Vb[0][:, pcs], start=False, stop=True),
                          reads=(Kc[1], Vb[1]), writes=(kb.pbuf[bi],))
                    sx, sxb = TS["stmp"]
                    kb.op("dve", lambda e, bs_=bs_: e.tensor_tensor(out=sx, in0=bs_[:, 0:128], in1=bdm[:, :], op=ALU.mult),
                          reads=(kb.pbuf[bi], bdmb), writes=(sxb,))
                    kb.op("dve", lambda e, p=p: e.scalar_tensor_tensor(out=Sm[:, p, :], in0=Sm[:, p, :], scalar=PL[:, p:p + 1], in1=sx,
                                                                      op0=ALU.mult, op1=ALU.add), reads=(Smb, PLb, sxb), writes=(Smb,))
                    kb.op("act", lambda e, p=p: e.copy(out=Sb[:, p, :], in_=Sm[:, p, :]), reads=(Smb,), writes=(Sbb,))

                def quad_stages(q, TS):
                    L = [lambda: st_kinds(q, TS), lambda: st_init(q, TS)]
                    for li in range(1, 7):
                        L.append(lambda li=li: st_lvl_a(q, TS, li))
                        L.append(lambda li=li: st_lvl_b(q, TS, li))
                        if li < 6:
                            L.append(lambda li=li: st_lvl_c(q, TS, li))
                    L.append(lambda: st_xw2(q, TS))
                    for pl in range(2):
                        L.append(lambda pl=pl: st_u(q, TS, pl))
                        L.append(lambda pl=pl: st_ys(q, TS, pl))
                    return L

                for stage_fns in zip(*[quad_stages(q, sets[q]) for q in range(4)]):
                    for fn_ in stage_fns:
                        fn_()
                for qq in range(2):
                    kb.op("act", lambda e, qq=qq: e.copy(out=yt[:, qq * 512:(qq + 1) * 512], in_=kb.bank(7 - qq)),
                          reads=(kb.pbuf[7 - qq],), writes=(ytb,))
                kb.alias_release(alias_parents, kids)
                if gt + 1 < ntile:
                    front_b(gt + 1)
                sq, sqb = t2, t2b
                kb.op("dve", lambda e: e.reduce_sum(out=sm[:, 0:16], in_=hv(yt), axis=AX.X), reads=(ytb,), writes=(smb,))
                kb.op("pool", lambda e: e.tensor_tensor(out=sq[:, :], in0=yt[:, :], in1=yt[:, :], op=ALU.mult), reads=(ytb,), writes=(sqb,))
                kb.op("dve", lambda e: e.reduce_sum(out=sm[:, 16:32], in_=hv(sq), axis=AX.X), reads=(sqb,), writes=(smb,))
                kb.op("dve", lambda e: e.tensor_scalar(out=sm[:, 0:32], in0=sm[:, 0:32], scalar1=1.0 / 64, scalar2=None, op0=ALU.mult),
                      reads=(smb,), writes=(smb,))
                kb.op("dve", lambda e: e.tensor_tensor(out=sm[:, 32:48], in0=sm[:, 0:16], in1=sm[:, 0:16], op=ALU.mult), reads=(smb,), writes=(smb,))
                kb.op("dve", lambda e: e.tensor_tensor(out=sm[:, 16:32], in0=sm[:, 16:32], in1=sm[:, 32:48], op=ALU.subtract),
                      reads=(smb,), writes=(smb,))
                kb.op("dve", lambda e: e.tensor_scalar(out=sm[:, 16:32], in0=sm[:, 16:32], scalar1=GN_EPS, scalar2=None, op0=ALU.add),
                      reads=(smb,), writes=(smb,))
                kb.op("act", lambda e: e.activation(out=sm[:, 16:32], in_=sm[:, 16:32], func=AF.Sqrt), reads=(smb,), writes=(smb,))
                kb.op("dve", lambda e: e.reciprocal(out=sm[:, 16:32], in_=sm[:, 16:32]), reads=(smb,), writes=(smb,))
                kb.op("dve", lambda e: e.tensor_tensor(out=hv(sq), in0=hv(yt), in1=sm[:, 0:16].unsqueeze(2).broadcast_to([128, 16, 64]),
                                                       op=ALU.subtract), reads=(ytb, smb), writes=(sqb,))
                kb.op("pool", lambda e: e.tensor_tensor(out=hv(sq), in0=hv(sq), in1=sm[:, 16:32].unsqueeze(2).broadcast_to([128, 16, 64]),
                                                        op=ALU.mult), reads=(sqb, smb), writes=(sqb,))
                kb.op("dve", lambda e: e.tensor_tensor(out=sq[:, :], in0=sq[:, :], in1=lnw[:, :], op=ALU.mult), reads=(sqb, lnwb), writes=(sqb,))
                kb.op("pool", lambda e: e.tensor_tensor(out=sq[:, :], in0=sq[:, :], in1=lnb[:, :], op=ALU.add), reads=(sqb, lnbb), writes=(sqb,))
                kb.op("pool", lambda e: e.tensor_tensor(out=sq[:, :], in0=sq[:, :], in1=tbo[:, :], op=ALU.add), reads=(sqb, tbob), writes=(sqb,))
                kb.op("pool", lambda e: e.tensor_tensor(out=om[:, :], in0=sq[:, :], in1=tg[:, :], op=ALU.mult), reads=(sqb, tgb), writes=(omb,))
                kb.dma("sp", mix[r0:r0 + 128, :], om[:, :], omb, load=False)
        kb.barrier()
        kb.release([identb, gcol[1], mucol[1], wrb, wkb, wvb, w1b, a1b, g1b, w2b, a2b, g2b, w0b, a0b, kkwb, kawb, rkwb,
                    hib, omb, lnwb, lnbb, trib, on32b, ms4b, mi4b, mst4b, id4b, hmb, bdmb, mob, moTb])


NSEQ = 2
SEQ = 4096

W_NAMES = ["norm_mix", "norm_mlp", "norm_final", "attn_w_in", "attn_w_out", "diff_lambda", "diff_subln", "rwkv_mu",
           "rwkv_w_r", "rwkv_w_k", "rwkv_w_v", "rwkv_w_o", "rwkv_w0", "rwkv_w1", "rwkv_w2", "rwkv_a0", "rwkv_a1", "rwkv_a2",
           "rwkv_g1", "rwkv_g2", "rwkv_k_k", "rwkv_k_a", "rwkv_r_k", "rwkv_ln_w", "rwkv_ln_b", "mlp_w1", "mlp_w2"]


def build_program(shapes, consts, nseq=NSEQ, S=SEQ):
    kb = KB()
    nc = kb.nc
    ntok = nseq * S
    A = {}
    for name, shp in shapes.items():
        A[name] = nc.dram_tensor(name, list(shp), F32, kind="ExternalInput").ap()
    Cn = {}
    for name, arr in consts.items():
        dt = BF16 if arr.dtype == ml_dtypes.bfloat16 else F32
        Cn[name] = nc.dram_tensor(name, list(arr.shape), dt, kind="ExternalInput").ap()
    out = nc.dram_tensor("out", [ntok, D], F32, kind="ExternalOutput").ap()

    def scr(name, shape, dt):
        return nc.dram_tensor(name, shape, dt, kind="Internal").ap()
    QT = scr("s_QT", [nseq, 8, 128, S], BF16)
    KT = scr("s_KT", [nseq, 8, 128, S], BF16)
    V = scr("s_V", [ntok, 1024], BF16)
    NZ = scr("s_NZ", [3, ntok, 8, 65], F32)
    mix = scr("s_mix", [ntok, 1024], BF16)
    h1 = scr("s_h1", [ntok, D], F32)
    h2 = scr("s_h2", [ntok, D], F32)
    h3 = scr("s_h3", [ntok, D], F32)
    x = A["x"]
    lam_init = 0.8 - 0.6 * math.exp(0.0)
    ident = Cn["c_ident"]
    phase_qkv(kb, x, A["norm_mix"][0], A["attn_w_in"][0], Cn["c_cos"], Cn["c_sin"], ident, QT, KT, V, nseq, S)
    phase_attn_b(kb, QT, KT, V, A["diff_lambda"][0], A["diff_subln"][0], Cn["c_maskc"], mix, nseq, S, lam_init)
    phase_attn_a(kb, QT, KT, V, Cn["c_mask4"], NZ, nseq, S)
    phase_attn_a_combine(kb, NZ, mix, ntok)
    phase_proj(kb, mix, A["attn_w_out"][0], x, h1, ident, ntok)
    phase_mlp(kb, h1, A["norm_mlp"][0], A["mlp_w1"][0], A["mlp_w2"][0], h2, ident, ntok)
    W = dict(g=A["norm_mix"][1], mu=A["rwkv_mu"][0], w_r=A["rwkv_w_r"][0], w_k=A["rwkv_w_k"][0], w_v=A["rwkv_w_v"][0],
             w1=A["rwkv_w1"][0], a1=A["rwkv_a1"][0], g1=A["rwkv_g1"][0], w2=A["rwkv_w2"][0], a2=A["rwkv_a2"][0],
             g2=A["rwkv_g2"][0], w0=A["rwkv_w0"][0], a0=A["rwkv_a0"][0], k_k=A["rwkv_k_k"][0], k_a=A["rwkv_k_a"][0],
             r_k=A["rwkv_r_k"][0])
    phase_rwkv(kb, h2, W, ident, Cn, A["rwkv_ln_w"][0], A["rwkv_ln_b"][0], mix, nseq, S)
    phase_proj(kb, mix, A["rwkv_w_o"][0], h2, h3, ident, ntok)
    phase_mlp(kb, h3, A["norm_mlp"][1], A["mlp_w1"][1], A["mlp_w2"][1], out, ident, ntok, gfinal=A["norm_final"])
    return nc


def kernel(**inputs):
    x = np.ascontiguousarray(inputs["x"], dtype=np.float32)
    B, S, C = x.shape
    nseq = B // NCORES
    consts = host_consts(S)
    shapes = {"x": (nseq * S, C)}
    wts = {}
    for n in W_NAMES:
        wts[n] = np.ascontiguousarray(inputs[n], dtype=np.float32)
        shapes[n] = wts[n].shape
    nc = build_program(shapes, consts, nseq=nseq, S=S)
    in_maps = []
    for c in range(NCORES):
        m = {"x": x[c * nseq:(c + 1) * nseq].reshape(nseq * S, C)}
        m.update(wts)
        m.update(consts)
        in_maps.append(m)
    res = run_bass_kernel_spmd(nc, in_maps, core_ids=list(range(NCORES)))
    outs = [np.asarray(r["out"]).reshape(nseq, S, C) for r in res.results]
    return np.concatenate(outs, axis=0).astype(np.float32)
```

```python
import math
from contextlib import ExitStack
import numpy as np
import ml_dtypes
import concourse.bass as bass
import concourse.mybir as mybir
from concourse.bass_utils import run_bass_kernel_spmd

F32 = mybir.dt.float32
BF16 = mybir.dt.bfloat16
ALU = mybir.AluOpType
AF = mybir.ActivationFunctionType
AX = mybir.AxisListType

D = 1024
DFF = 4096
HD = 64
EPS = 1e-5
GN_EPS = 64e-5
NCORES = 8


class Buf:
    __slots__ = ("w", "rs", "ds", "ps")

    def __init__(self):
        self.w = None
        self.rs = {}
        self.ds = None
        self.ps = False


class DSem:
    __slots__ = ("h", "v", "key")

    def __init__(self, h, key):
        self.h = h
        self.v = 0
        self.key = key


class KB:
    def __init__(self, n_dsem=90):
        nc = self.nc = bass.Bass("TRN2", target_bir_lowering=False)
        self.engs = {"pe": nc.tensor, "dve": nc.vector, "act": nc.scalar, "pool": nc.gpsimd, "sp": nc.sync}
        self.sem = {}
        self.cnt = {}
        self.waited = {e: {} for e in self.engs}
        for e in self.engs:
            self.sem[e] = nc.alloc_semaphore(name="prog_" + e)
            self.cnt[e] = 0
        self.free_ds = [DSem(nc.alloc_semaphore(name="dsem%d" % i), "d%d" % i) for i in range(n_dsem)]
        self.pending = {}
        self.bar = nc.alloc_semaphore(name="barrier")
        self.barv = 0
        self.psum = []
        for i in range(8):
            t = nc.alloc_psum_tensor("psb%d" % i, [128, 512], F32)
            self.psum.append(t)
        self.pbuf = [Buf() for _ in range(8)]
        for b in self.pbuf:
            b.ps = True

    def _wait(self, e, tok):
        if tok is None:
            return
        key, h, v = tok
        if key == "pe" and e == "pe":
            return
        w = self.waited[e]
        if w.get(key, 0) >= v:
            return
        self.engs[e].wait_ge(h, v)
        w[key] = v

    def _deps(self, e, reads, writes):
        for b in reads:
            self._wait(e, b.w)
            if b.ps:
                for k2, t in b.rs.items():
                    if k2 != e:
                        self._wait(e, t)
        for b in writes:
            self._wait(e, b.w)
            for t in b.rs.values():
                self._wait(e, t)

    def op(self, e, fn, reads=(), writes=()):
        self._deps(e, reads, writes)
        ins = fn(self.engs[e])
        self.cnt[e] += 1
        ins.then_inc(self.sem[e], 1)
        tok = (e, self.sem[e], self.cnt[e])
        for b in reads:
            b.rs[e] = tok
        for b in writes:
            b.w = tok
            b.rs = {}
        return tok

    def dma(self, q, out, in_, buf, load=True, **kw):
        if buf.ds is None:
            buf.ds = self.free_ds.pop()
        ds = buf.ds
        if load:
            if buf.w is not None and buf.w[0] == ds.key:
                for t in buf.rs.values():
                    self._wait(q, t)
            else:
                self._deps(q, (), (buf,))
        else:
            self._deps(q, (buf,), ())
        ins = self.engs[q].dma_start(out=out, in_=in_, **kw)
        ds.v += 16
        ins.then_inc(ds.h, 16)
        tok = (ds.key, ds.h, ds.v)
        if load:
            buf.w = tok
            buf.rs = {}
        else:
            buf.rs[ds.key] = tok
        self.pending[ds.key] = tok
        return tok

    def alias_acquire(self, parents, kids):
        pend = {}
        for pb in parents:
            toks = list(pb.rs.values()) + ([pb.w] if pb.w is not None else [])
            for t in toks:
                if t[0] not in pend or pend[t[0]][2] < t[2]:
                    pend[t[0]] = t
        for kbuf in kids:
            kbuf.w = None
            kbuf.rs = dict(pend)

    def alias_release(self, parents, kids):
        pend = {}
        for kbuf in kids:
            toks = list(kbuf.rs.values()) + ([kbuf.w] if kbuf.w is not None else [])
            for t in toks:
                if t[0] not in pend or pend[t[0]][2] < t[2]:
                    pend[t[0]] = t
        for pb in parents:
            for k_, t in pend.items():
                if k_ not in pb.rs or pb.rs[k_][2] < t[2]:
                    pb.rs[k_] = t

    def release(self, bufs):
        for b in bufs:
            if b.ds is not None:
                self.free_ds.append(b.ds)
                b.ds = None

    def barrier(self):
        sp = "sp"
        for e in self.engs:
            if e != sp and self.cnt[e] > 0:
                self._wait(sp, (e, self.sem[e], self.cnt[e]))
        for tok in self.pending.values():
            self._wait(sp, tok)
        self.pending = {}
        self.barv += 1
        self.engs[sp].nop().then_inc(self.bar, 1)
        for e in self.engs:
            self.engs[e].wait_ge(self.bar, self.barv)
        for e in self.engs:
            for e2 in self.engs:
                self.waited[e][e2] = self.cnt[e2]

    def sbt(self, name, shape, dt):
        self.uid = getattr(self, "uid", 0) + 1
        return self.nc.sbuf_tensor("%s_u%d" % (name, self.uid), shape, dt)

    def bank(self, i, dt=F32):
        t = self.psum[i]
        ap = t[:, :] if hasattr(t, "__getitem__") else t.ap()
        if dt != F32:
            ap = ap.bitcast(dt)
        return ap


def bcast_rows(ap1d, n=128):
    return ap1d.partition_broadcast(n)


def load_weight_bf16(kb, es, w_ap, K, N, name, gcol=None, stage_cols=2048, q="sp"):
    kc = (K + 127) // 128
    wsb = es.enter_context(kb.sbt(name, [128, kc, N], BF16))
    wb = Buf()
    sc = min(stage_cols, N)
    NS = 4
    with ExitStack() as s2:
        stg = [s2.enter_context(kb.sbt(name + "_stg%d" % i, [128, sc], F32)) for i in range(NS)]
        sb = [Buf() for _ in range(NS)]
        i = 0
        for c in range(kc):
            rows = min(128, K - c * 128)
            for n0 in range(0, N, sc):
                ncol = min(sc, N - n0)
                j = i % NS
                dq = "sp" if (i % 2 == 0) else "act"
                eng = "dve" if (i % 2 == 0) else "pool"
                i += 1
                kb.dma(dq, stg[j][:rows, :ncol], w_ap[c * 128:c * 128 + rows, n0:n0 + ncol], sb[j])
                if gcol is not None:
                    kb.op(eng, lambda e, j=j, c=c, n0=n0, ncol=ncol, rows=rows: e.tensor_scalar(
                        out=wsb[:rows, c, n0:n0 + ncol], in0=stg[j][:rows, :ncol],
                        scalar1=gcol[0][:rows, c:c + 1], scalar2=None, op0=ALU.mult),
                        reads=(sb[j], gcol[1]), writes=())
                else:
                    kb.op(eng, lambda e, j=j, c=c, n0=n0, ncol=ncol, rows=rows: e.tensor_copy(
                        out=wsb[:rows, c, n0:n0 + ncol], in_=stg[j][:rows, :ncol]),
                        reads=(sb[j],), writes=())
        kb.barrier()
        kb.release(sb)
    return wsb, wb


def load_cols(kb, es, v_ap, name, q="sp"):
    nc = kb.nc
    C = v_ap.shape[0] // 128
    t = es.enter_context(kb.sbt(name, [128, C], F32))
    b = Buf()
    kb.dma(q, t[:, :], v_ap.rearrange("(c p) -> p c", p=128), b, allow_slow_non_contiguous=True)
    return t, b


def load_bcast(kb, es, v_ap, name, q="sp"):
    nc = kb.nc
    Fd = v_ap.shape[0]
    t = es.enter_context(kb.sbt(name, [128, Fd], F32))
    b = Buf()
    kb.dma(q, t[:, :], v_ap.partition_broadcast(128), b)
    return t, b


def rms_rstd(kb, x_ap, xbuf, junk, junkb, ssq, rstd, sbuf_, width, eps):
    kb.op("act", lambda e: e.activation(out=junk, in_=x_ap, func=AF.Square, accum_out=ssq),
          reads=(xbuf,), writes=(junkb, sbuf_))
    kb.op("dve", lambda e: e.tensor_scalar(out=rstd, in0=ssq, scalar1=1.0 / width, scalar2=eps,
                                           op0=ALU.mult, op1=ALU.add),
          reads=(sbuf_,), writes=(sbuf_,))
    kb.op("act", lambda e: e.activation(out=rstd, in_=rstd, func=AF.Sqrt), reads=(sbuf_,), writes=(sbuf_,))
    kb.op("dve", lambda e: e.reciprocal(out=rstd, in_=rstd), reads=(sbuf_,), writes=(sbuf_,))


def transpose_tile(kb, src_fn, srcbuf, nchunks, dst_fn, dstbuf, ident, identb, banks, state, evac_engs=("dve", "act")):
    c = 0
    while c < nchunks:
        n = min(8, nchunks - c)
        bi = banks[state["i"] % len(banks)]
        ev = evac_engs[state["i"] % len(evac_engs)]
        state["i"] += 1
        pb = kb.pbuf[bi]
        pap = kb.bank(bi, BF16)
        for k in range(n):
            kb.op("pe", lambda e, k=k, c=c: e.transpose(out=pap[:, k * 128:(k + 1) * 128], in_=src_fn(c + k), identity=ident),
                  reads=(srcbuf, identb), writes=(pb,))
        dst = dst_fn(c, n)
        src = pap[:, 0:n * 128].rearrange("p (n t) -> p n t", t=128)
        if ev == "act":
            kb.op("act", lambda e: e.copy(out=dst, in_=src), reads=(pb,), writes=(dstbuf,))
        else:
            kb.op("dve", lambda e: e.tensor_copy(out=dst, in_=src), reads=(pb,), writes=(dstbuf,))
        c += n


def phase_proj(kb, mix, w, h_in, h_out, ident_d, ntok):
    nc = kb.nc
    with ExitStack() as es:
        ident = es.enter_context(kb.sbt("pj_ident", [128, 128], BF16))
        identb = Buf()
        kb.dma("sp", ident[:, :], ident_d, identb)
        wsb, wb = load_weight_bf16(kb, es, w, D, D, "pj_w")
        NB = 2
        mx = [es.enter_context(kb.sbt("pj_mx%d" % i, [128, D], BF16)) for i in range(NB)]
        mxb = [Buf() for _ in range(NB)]
        hi = [es.enter_context(kb.sbt("pj_hi%d" % i, [128, D], F32)) for i in range(NB)]
        hib = [Buf() for _ in range(NB)]
        mT = [es.enter_context(kb.sbt("pj_mT%d" % i, [128, 8, 128], BF16)) for i in range(NB)]
        mTb = [Buf() for _ in range(NB)]
        st = {"i": 0}
        pi = 0
        for t in range(ntok // 128):
            j = t % NB
            r0 = t * 128
            if t == 0:
                kb.dma("sp", mx[j][:, :], mix[r0:r0 + 128, :], mxb[j])
                kb.dma("sp", hi[j][:, :], h_in[r0:r0 + 128, :], hib[j])
            if t + 1 < ntok // 128:
                kb.dma("sp", mx[1 - j][:, :], mix[r0 + 128:r0 + 256, :], mxb[1 - j])
                kb.dma("sp", hi[1 - j][:, :], h_in[r0 + 128:r0 + 256, :], hib[1 - j])
            transpose_tile(kb, lambda c, j=j: mx[j][:, c * 128:(c + 1) * 128], mxb[j], 8,
                           lambda c0, n, j=j: mT[j][:, c0:c0 + n, :], mTb[j], ident[:, :], identb, (0, 1), st)
            for half in range(2):
                bi = 2 + (pi % 4)
                pi += 1
                pap = kb.bank(bi)
                for c in range(8):
                    kb.op("pe", lambda e, c=c, j=j, half=half, pap=pap: e.matmul(
                        pap, lhsT=mT[j][:, c, :], rhs=wsb[:, c, half * 512:(half + 1) * 512],
                        start=(c == 0), stop=(c == 7)), reads=(mTb[j], wb), writes=(kb.pbuf[bi],))
                kb.op("dve", lambda e, j=j, half=half, pap=pap: e.tensor_tensor(
                    out=hi[j][:, half * 512:(half + 1) * 512], in0=hi[j][:, half * 512:(half + 1) * 512],
                    in1=pap, op=ALU.add), reads=(kb.pbuf[bi], hib[j]), writes=(hib[j],))
            kb.dma("sp", h_out[r0:r0 + 128, :], hi[j][:, :], hib[j], load=False)
        kb.barrier()
        kb.release(mxb + hib + [identb, wb])


def phase_mlp(kb, h_in, g, w1, w2, h_out, ident_d, ntok, gfinal=None, T=256):
    nc = kb.nc
    TT = T // 128
    with ExitStack() as es:
        ident = es.enter_context(kb.sbt("ml_ident", [128, 128], BF16))
        identb = Buf()
        kb.dma("sp", ident[:, :], ident_d, identb)
        gcol = load_cols(kb, es, g, "ml_gcol")
        w1sb, w1b = load_weight_bf16(kb, es, w1, D, DFF, "ml_w1", gcol=gcol)
        w2sb, w2b = load_weight_bf16(kb, es, w2, DFF, D, "ml_w2")
        if gfinal is not None:
            gf, gfb = load_bcast(kb, es, gfinal, "ml_gf")
        NB = 2
        hi = [es.enter_context(kb.sbt("ml_hi%d" % i, [128, TT, D], F32)) for i in range(NB)]
        hib = [Buf() for _ in range(NB)]
        hn = [es.enter_context(kb.sbt("ml_hn%d" % i, [128, TT, D], BF16)) for i in range(2)]
        hnb = [Buf() for _ in range(2)]
        hnT = [es.enter_context(kb.sbt("ml_hnT%d" % i, [128, 8, T], BF16)) for i in range(2)]
        hnTb = [Buf() for _ in range(2)]
        h1T = es.enter_context(kb.sbt("ml_h1T", [128, 32, T], BF16))
        h1Tb = [Buf() for _ in range(32)]
        junk = es.enter_context(kb.sbt("ml_junk", [128, D], F32))
        junkb = Buf()
        rl = [es.enter_context(kb.sbt("ml_rl%d" % i, [128, T], F32)) for i in range(4)]
        rlb = [Buf() for _ in range(4)]
        stt = es.enter_context(kb.sbt("ml_st", [128, 16], F32))
        sttb = [Buf() for _ in range(8)]
        st = {"i": 0}
        pi = 0
        ri = 0
        nblk = ntok // T

        def load(blk):
            r0 = blk * T
            kb.dma("sp", hi[blk % NB][:, :, :], h_in[r0:r0 + T, :].rearrange("(t p) f -> p t f", p=128), hib[blk % NB])

        def front_a(blk):
            j = blk % NB
            for tt in range(TT):
                c0 = 4 * j + 2 * tt
                rms_rstd(kb, hi[j][:, tt, :], hib[j], junk[:, :], junkb, stt[:, c0:c0 + 1], stt[:, c0 + 1:c0 + 2], sttb[2 * j + tt], D, EPS)
                kb.op("dve", lambda e, tt=tt, j=j, c0=c0: e.tensor_scalar(
                    out=hn[j][:, tt, :], in0=hi[j][:, tt, :], scalar1=stt[:, c0 + 1:c0 + 2], scalar2=None,
                    op0=ALU.mult), reads=(hib[j], sttb[2 * j + tt]), writes=(hnb[j],))

        def front_b(blk):
            j = blk % NB
            for tt in range(TT):
                transpose_tile(kb, lambda c, tt=tt: hn[j][:, tt, c * 128:(c + 1) * 128], hnb[j], 8,
                               lambda c0, n, tt=tt: hnT[j][:, c0:c0 + n, tt * 128:(tt + 1) * 128], hnTb[j],
                               ident[:, :], identb, (0, 1), st)

        load(0)
        front_a(0)
        front_b(0)
        for blk in range(nblk):
            j = blk % NB
            r0 = blk * T
            if blk + 1 < nblk:
                load(blk + 1)
            for fc in range(32):
                bi = 2 + (pi % 3)
                pi += 1
                pap = kb.bank(bi)[:, 0:T]
                for c in range(8):
                    kb.op("pe", lambda e, c=c, fc=fc, pap=pap: e.matmul(
                        pap, lhsT=w1sb[:, c, fc * 128:(fc + 1) * 128], rhs=hnT[j][:, c, :],
                        start=(c == 0), stop=(c == 7)), reads=(hnTb[j], w1b), writes=(kb.pbuf[bi],))
                k = ri % 4
                ri += 1
                kb.op("act", lambda e, k=k, pap=pap: e.activation(out=rl[k][:, :], in_=pap, func=AF.Relu),
                      reads=(kb.pbuf[bi],), writes=(rlb[k],))
                kb.op("pool", lambda e, k=k, fc=fc: e.tensor_tensor(out=h1T[:, fc, :], in0=rl[k][:, :], in1=rl[k][:, :],
                                                                   op=ALU.mult),
                      reads=(rlb[k],), writes=(h1Tb[fc],))
            if blk + 1 < nblk:
                front_a(blk + 1)
            for tt in range(TT):
                for half in range(2):
                    bi = 5 + (pi % 3)
                    pi += 1
                    pap = kb.bank(bi)
                    for fc in range(32):
                        kb.op("pe", lambda e, fc=fc, tt=tt, half=half, pap=pap: e.matmul(
                            pap, lhsT=h1T[:, fc, tt * 128:(tt + 1) * 128], rhs=w2sb[:, fc, half * 512:(half + 1) * 512],
                            start=(fc == 0), stop=(fc == 31)), reads=(h1Tb[fc], w2b), writes=(kb.pbuf[bi],))
                    kb.op("dve", lambda e, j=j, tt=tt, half=half, pap=pap: e.tensor_tensor(
                        out=hi[j][:, tt, half * 512:(half + 1) * 512], in0=hi[j][:, tt, half * 512:(half + 1) * 512],
                        in1=pap, op=ALU.add), reads=(kb.pbuf[bi], hib[j]), writes=(hib[j],))
            if gfinal is not None:
                for tt in range(TT):
                    c0 = 8 + 2 * tt
                    rms_rstd(kb, hi[j][:, tt, :], hib[j], junk[:, :], junkb, stt[:, c0:c0 + 1], stt[:, c0 + 1:c0 + 2], sttb[4 + tt], D, EPS)
                    kb.op("dve", lambda e, tt=tt, j=j, c0=c0: e.scalar_tensor_tensor(
                        out=hi[j][:, tt, :], in0=hi[j][:, tt, :], scalar=stt[:, c0 + 1:c0 + 2], in1=gf[:, :],
                        op0=ALU.mult, op1=ALU.mult), reads=(hib[j], sttb[4 + tt], gfb), writes=(hib[j],))
            kb.dma("sp", h_out[r0:r0 + T, :].rearrange("(t p) f -> p t f", p=128), hi[j][:, :, :], hib[j], load=False)
            if blk + 1 < nblk:
                front_b(blk + 1)
        kb.barrier()
        kb.release(hib + [identb, w1b, w2b, gcol[1]] + ([gfb] if gfinal is not None else []))


def phase_qkv(kb, x, g, w_in, cos_d, sin_d, ident_d, QT, KT, V, nseq, S):
    nc = kb.nc
    NT = S // 128
    with ExitStack() as es:
        ident = es.enter_context(kb.sbt("qk_ident", [128, 128], BF16))
        identb = Buf()
        kb.dma("sp", ident[:, :], ident_d, identb)
        cos = es.enter_context(kb.sbt("qk_cos", [128, NT, 64], F32))
        sin = es.enter_context(kb.sbt("qk_sin", [128, NT, 64], F32))
        csb = Buf()
        snb = Buf()
        kb.dma("sp", cos[:, :, :], cos_d.rearrange("(t p) k -> p t k", p=128), csb)
        kb.dma("sp", sin[:, :, :], sin_d.rearrange("(t p) k -> p t k", p=128), snb)
        gcol = load_cols(kb, es, g, "qk_gcol")
        wsb, wb = load_weight_bf16(kb, es, w_in, D, 3072, "qk_w", gcol=gcol)
        NB = 3
        xi = [es.enter_context(kb.sbt("qk_x%d" % i, [128, D], F32)) for i in range(NB)]
        xib = [Buf() for _ in range(NB)]
        xn_ = [es.enter_context(kb.sbt("qk_xn%d" % i, [128, D], BF16)) for i in range(2)]
        xnb_ = [Buf() for _ in range(2)]
        xT_ = [es.enter_context(kb.sbt("qk_xT%d" % i, [128, 8, 128], BF16)) for i in range(2)]
        xTb_ = [Buf() for _ in range(2)]
        junk = es.enter_context(kb.sbt("qk_junk", [128, D], F32))
        junkb = Buf()
        stt_ = [es.enter_context(kb.sbt("qk_st%d" % i, [128, 2], F32)) for i in range(2)]
        sttb_ = [Buf() for _ in range(2)]
        qf = [es.enter_context(kb.sbt("qk_qf%d" % i, [128, 8, 64], F32)) for i in range(2)]
        qfb = [Buf() for _ in range(2)]
        tmp = [es.enter_context(kb.sbt("qk_tmp%d" % i, [128, 4, 8, 8], F32)) for i in range(2)]
        tmpb = [Buf() for _ in range(2)]
        qb = [es.enter_context(kb.sbt("qk_qb%d" % i, [128, 8, 64], BF16)) for i in range(8)]
        qbb = [Buf() for _ in range(8)]
        vt = [es.enter_context(kb.sbt("qk_vt%d" % i, [128, D], BF16)) for i in range(2)]
        vtb = [Buf() for _ in range(2)]
        qst = [es.enter_context(kb.sbt("qk_qst%d" % i, [128, 8, 512], BF16)) for i in range(2)]
        qstb = [Buf() for _ in range(2)]
        kst = [es.enter_context(kb.sbt("qk_kst%d" % i, [128, 8, 512], BF16)) for i in range(2)]
        kstb = [Buf() for _ in range(2)]
        st = {"i": 0}
        pi = 0
        qi = 0
        deferred = []
        prev_jobs = []
        ntile = nseq * NT

        def load_x(g):
            kb.dma("sp", xi[g % NB][:, :], x[g * 128:(g + 1) * 128, :], xib[g % NB])

        def front_a(g):
            j3, j2 = g % NB, g % 2
            xn, xnb, stt, sttb = xn_[j2], xnb_[j2], stt_[j2], sttb_[j2]
            rms_rstd(kb, xi[j3][:, :], xib[j3], junk[:, :], junkb, stt[:, 0:1], stt[:, 1:2], sttb, D, EPS)
            kb.op("dve", lambda e: e.tensor_scalar(out=xn[:, :], in0=xi[j3][:, :], scalar1=stt[:, 1:2], scalar2=None, op0=ALU.mult),
                  reads=(xib[j3], sttb), writes=(xnb,))

        def front_b(g):
            j2 = g % 2
            xn, xnb, xT, xTb = xn_[j2], xnb_[j2], xT_[j2], xTb_[j2]
            transpose_tile(kb, lambda c: xn[:, c * 128:(c + 1) * 128], xnb, 8,
                           lambda c0, n: xT[:, c0:c0 + n, :], xTb, ident[:, :], identb, (0, 1), st)

        for s in range(nseq):
            for t in range(NT):
                gt = s * NT + t
                j = gt % NB
                r0 = gt * 128
                grp = (gt // 4) % 2
                tin = t % 4
                if gt == 0:
                    load_x(0)
                    if ntile > 1:
                        load_x(1)
                    front_a(0)
                    front_b(0)
                if gt + 2 < ntile:
                    load_x(gt + 2)
                if gt + 1 < ntile:
                    front_a(gt + 1)
                xT, xTb = xT_[gt % 2], xTb_[gt % 2]
                vj = gt % 2
                for cb in range(6):
                    bi = 2 + (pi % 6)
                    pi += 1
                    pap = kb.bank(bi)
                    for c in range(8):
                        kb.op("pe", lambda e, c=c, cb=cb, pap=pap, xT=xT: e.matmul(
                            pap, lhsT=xT[:, c, :], rhs=wsb[:, c, cb * 512:(cb + 1) * 512],
                            start=(c == 0), stop=(c == 7)), reads=(xTb, wb), writes=(kb.pbuf[bi],))
                    if cb in (2, 5):
                        c0 = 0 if cb == 2 else 512
                        kb.op("act", lambda e, pap=pap, c0=c0, vj=vj: e.copy(out=vt[vj][:, c0:c0 + 512], in_=pap),
                              reads=(kb.pbuf[bi],), writes=(vtb[vj],))
                        continue
                    k = qi % 2
                    kq = (gt % 2) * 4 + (qi % 4)
                    qi += 1
                    isq = cb in (0, 3)
                    kb.op("act", lambda e, pap=pap, k=k, isq=isq: e.mul(
                        out=qf[k][:, :, :], in_=pap.rearrange("p (h d) -> p h d", d=64), mul=(0.125 if isq else 1.0)),
                        reads=(kb.pbuf[bi],), writes=(qfb[k],))
                    q1 = qf[k][:, :, 0:8]
                    q2 = qf[k][:, :, 8:16]
                    cs = cos[:, t, :].rearrange("p (h d) -> p h d", d=8)
                    sn = sin[:, t, :].rearrange("p (h d) -> p h d", d=8)
                    tm = tmp[k]
                    kb.op("dve", lambda e, tm=tm, q1=q1, cs=cs: e.tensor_tensor(out=tm[:, 0, :, :], in0=q1, in1=cs, op=ALU.mult),
                          reads=(qfb[k], csb), writes=(tmpb[k],))
                    kb.op("dve", lambda e, tm=tm, q2=q2, sn=sn: e.tensor_tensor(out=tm[:, 1, :, :], in0=q2, in1=sn, op=ALU.mult),
                          reads=(qfb[k], snb), writes=(tmpb[k],))
                    kb.op("pool", lambda e, tm=tm, q2=q2, cs=cs: e.tensor_tensor(out=tm[:, 2, :, :], in0=q2, in1=cs, op=ALU.mult),
                          reads=(qfb[k], csb), writes=(tmpb[k],))
                    kb.op("pool", lambda e, tm=tm, q1=q1, sn=sn: e.tensor_tensor(out=tm[:, 3, :, :], in0=q1, in1=sn, op=ALU.mult),
                          reads=(qfb[k], snb), writes=(tmpb[k],))
                    kb.op("dve", lambda e, tm=tm, kq=kq: e.tensor_tensor(out=qb[kq][:, :, 0:8], in0=tm[:, 0, :, :], in1=tm[:, 1, :, :],
                                                                       op=ALU.subtract), reads=(tmpb[k],), writes=(qbb[kq],))
                    kb.op("pool", lambda e, tm=tm, kq=kq: e.tensor_tensor(out=qb[kq][:, :, 8:16], in0=tm[:, 2, :, :], in1=tm[:, 3, :, :],
                                                                        op=ALU.add), reads=(tmpb[k],), writes=(qbb[kq],))
                    kb.op("act", lambda e, k=k, kq=kq: e.copy(out=qb[kq][:, :, 16:64], in_=qf[k][:, :, 16:64]),
                          reads=(qfb[k],), writes=(qbb[kq],))
                    hp0 = 0 if cb in (0, 1) else 4
                    dst_t, dst_b = (qst[grp], qstb[grp]) if isq else (kst[grp], kstb[grp])

                    def tr_job(kq=kq, dst_t=dst_t, dst_b=dst_b, hp0=hp0, tin=tin):
                        qflat = qb[kq]
                        transpose_tile(kb, lambda c: qflat[:, 2 * c:2 * c + 2, :].rearrange("p h d -> p (h d)"), qbb[kq], 4,
                                       lambda c0, n: dst_t[:, hp0 + c0:hp0 + c0 + n, tin * 128:(tin + 1) * 128],
                                       dst_b, ident[:, :], identb, (0, 1), st)
                    deferred.append(tr_job)
                if gt + 1 < ntile:
                    front_b(gt + 1)
                kb.dma("act", V[r0:r0 + 128, :], vt[vj][:, :], vtb[vj], load=False)
                if tin == 3:
                    def st_job(s=s, t=t, grp=grp):
                        t0 = (t - 3) * 128
                        kb.dma("act", QT[s, :, :, t0:t0 + 512].rearrange("h p t -> p h t"), qst[grp][:, :, :], qstb[grp], load=False)
                        kb.dma("act", KT[s, :, :, t0:t0 + 512].rearrange("h p t -> p h t"), kst[grp][:, :, :], kstb[grp], load=False)
                    deferred.append(st_job)
                for job in prev_jobs:
                    job()
                prev_jobs = deferred
                deferred = []
        for job in prev_jobs:
            job()
        kb.barrier()
        kb.release(xib + vtb + qstb + kstb + [identb, csb, snb, wb, gcol[1]])


def phase_attn_b(kb, QT, KT, V, lam_d, subln_d, maskc_d, mix, nseq, S, lam_init):
    nc = kb.nc
    NT = S // 128
    NG = S // 512
    with ExitStack() as es:
        lp = es.enter_context(kb.sbt("ab_lp", [128, 256], F32))
        lpb = Buf()
        kb.dma("sp", lp[:, :], lam_d.rearrange("a b -> (a b)").partition_broadcast(128), lpb)
        gs = es.enter_context(kb.sbt("ab_gs", [128, 128], F32))
        gsb = Buf()
        kb.dma("sp", gs[:, :], subln_d.partition_broadcast(128), gsb)
        mk = es.enter_context(kb.sbt("ab_mk", [128, 128], BF16))
        mkb = Buf()
        kb.dma("sp", mk[:, :], maskc_d, mkb)
        sc = es.enter_context(kb.sbt("ab_sc", [128, 8], F32))
        scb = Buf()
        pr = es.enter_context(kb.sbt("ab_pr", [128, 128], F32))
        prb = Buf()
        for i in range(2):
            kb.op("dve", lambda e, i=i: e.tensor_tensor(out=pr[:, i * 64:(i + 1) * 64], in0=lp[:, i * 128:i * 128 + 64],
                                                       in1=lp[:, i * 128 + 64:i * 128 + 128], op=ALU.mult),
                  reads=(lpb,), writes=(prb,))
        kb.op("dve", lambda e: e.reduce_sum(out=sc[:, 0:2], in_=pr[:, :].rearrange("p (a b) -> p a b", a=2), axis=AX.X),
              reads=(prb,), writes=(scb,))
        kb.op("act", lambda e: e.activation(out=sc[:, 0:2], in_=sc[:, 0:2], func=AF.Exp), reads=(scb,), writes=(scb,))
        kb.op("dve", lambda e: e.tensor_tensor(out=sc[:, 2:3], in0=sc[:, 1:2], in1=sc[:, 0:1], op=ALU.subtract),
              reads=(scb,), writes=(scb,))
        kb.op("dve", lambda e: e.tensor_scalar(out=sc[:, 2:3], in0=sc[:, 2:3], scalar1=-lam_init, scalar2=None, op0=ALU.add),
              reads=(scb,), writes=(scb,))
        kb.op("dve", lambda e: e.tensor_scalar(out=gs[:, :], in0=gs[:, :], scalar1=1.0 - lam_init, scalar2=None, op0=ALU.mult),
              reads=(gsb,), writes=(gsb,))
        qt = [es.enter_context(kb.sbt("ab_qt%d" % i, [128, S], BF16)) for i in range(2)]
        qtb = [Buf() for _ in range(2)]
        kt = [es.enter_context(kb.sbt("ab_kt%d" % i, [128, S], BF16)) for i in range(2)]
        ktb = [Buf() for _ in range(2)]
        vb = es.enter_context(kb.sbt("ab_vb", [128, NT, 4, 129], BF16))
        vbb = Buf()
        kb.op("pool", lambda e: e.memset(vb[:, :, :, 128:129], 1.0), writes=(vbb,))
        NP = 4
        pt = [es.enter_context(kb.sbt("ab_pt%d" % i, [128, 512], BF16)) for i in range(NP)]
        ptb = [Buf() for _ in range(NP)]
        accb = [kb.pbuf[4 + a // 3] for a in range(8)]
        o0 = es.enter_context(kb.sbt("ab_o0", [128, 128], F32))
        o0b = Buf()
        oo = es.enter_context(kb.sbt("ab_oo", [128, 128], F32))
        oob = Buf()
        jk = es.enter_context(kb.sbt("ab_jk", [128, 128], F32))
        jkb = Buf()
        rz = es.enter_context(kb.sbt("ab_rz", [128, 4], F32))
        rzb = Buf()
        ob = [es.enter_context(kb.sbt("ab_ob%d" % i, [128, 4, 128], BF16)) for i in range(2)]
        obb = [Buf() for _ in range(2)]
        oo4 = [es.enter_context(kb.sbt("ab_oo4%d" % i, [128, 4, 128], F32)) for i in range(2)]
        oo4b = [Buf() for _ in range(2)]
        rz4 = es.enter_context(kb.sbt("ab_rz4", [128, 4, 2], F32))
        rz4b = Buf()
        ss4 = es.enter_context(kb.sbt("ab_ss4", [128, 8], F32))
        ss4b = Buf()

        def acc_ap(a):
            return kb.bank(4 + a // 3)[:, (a % 3) * 129:(a % 3) * 129 + 129]

        pi_ = [0]
        pj_ = [0]
        oi_ = [0]
        it = 0
        for s in range(nseq):
            for hh in range(4):
                kb.dma("sp", vb[:, :, hh, 0:128],
                       V[s * S:(s + 1) * S, 512 + hh * 128:512 + (hh + 1) * 128].rearrange("(t p) d -> p t d", p=128), vbb)
            for h in range(4):
                j = it % 2
                it += 1
                if s == 0 and h == 0:
                    kb.dma("sp", qt[j][:, :], QT[s, 4 + h, :, :], qtb[j])
                    kb.dma("sp", kt[j][:, :], KT[s, 4 + h, :, :], ktb[j])
                nh = s * 4 + h + 1
                if nh < nseq * 4:
                    kb.dma("sp", qt[1 - j][:, :], QT[nh // 4, 4 + nh % 4, :, :], qtb[1 - j])
                    kb.dma("sp", kt[1 - j][:, :], KT[nh // 4, 4 + nh % 4, :, :], ktb[1 - j])
                groups = [(n, mg, min(4, n + 1 - mg)) for n in range(NT) for mg in range(0, n + 1, 4)]

                def emit_qk(g, j=j):
                    n, mg, mc = g
                    bis = []
                    for c in range(2):
                        bis.append(pi_[0] % 4)
                        pi_[0] += 1
                    for i in range(mc):
                        m = mg + i
                        for c in range(2):
                            bi = bis[c]
                            kb.op("pe", lambda e, bi=bi, c=c, m=m, n=n, i=i, j=j: e.matmul(
                                kb.bank(bi)[:, i * 128:(i + 1) * 128], lhsT=kt[j][64 * c:64 * c + 64, m * 128:(m + 1) * 128],
                                rhs=qt[j][64 * c:64 * c + 64, n * 128:(n + 1) * 128], start=True, stop=True),
                                reads=(ktb[j], qtb[j]), writes=(kb.pbuf[bi],))
                    ks = []
                    for c in range(2):
                        bi = bis[c]
                        k = pj_[0] % NP
                        pj_[0] += 1
                        ks.append(k)
                        kb.op("act", lambda e, bi=bi, k=k, mc=mc: e.activation(out=pt[k][:, 0:mc * 128], in_=kb.bank(bi)[:, 0:mc * 128],
                                                                             func=AF.Exp),
                              reads=(kb.pbuf[bi],), writes=(ptb[k],))
                        if mg + mc - 1 == n:
                            i = mc - 1
                            kb.op("pool", lambda e, k=k, i=i: e.tensor_tensor(out=pt[k][:, i * 128:(i + 1) * 128],
                                                                            in0=pt[k][:, i * 128:(i + 1) * 128], in1=mk[:, :], op=ALU.mult),
                                  reads=(ptb[k], mkb), writes=(ptb[k],))
                    return ks

                def emit_pv(g, ks, h=h):
                    n, mg, mc = g
                    ab = 4 + 2 * (n % 2)
                    for c in range(2):
                        k = ks[c]
                        acc = kb.bank(ab + c)[:, 0:129]
                        for i in range(mc):
                            m = mg + i
                            kb.op("pe", lambda e, acc=acc, k=k, i=i, m=m, h=h, n=n: e.matmul(
                                acc, lhsT=pt[k][:, i * 128:(i + 1) * 128], rhs=vb[:, m, h, :],
                                start=(m == 0), stop=(m == n)), reads=(ptb[k], vbb), writes=(kb.pbuf[ab + c],))

                def emit_combine(n, s=s, h=h):
                    ab = 4 + 2 * (n % 2)
                    oj = (oi_[0] + n // 4) % 2
                    jj = n % 4
                    a0 = kb.bank(ab)[:, 0:129]
                    a1 = kb.bank(ab + 1)[:, 0:129]
                    b0 = kb.pbuf[ab]
                    b1 = kb.pbuf[ab + 1]
                    rzj = rz4[:, jj, :]
                    kb.op("dve", lambda e, a0=a0, rzj=rzj: e.reciprocal(out=rzj[:, 0:1], in_=a0[:, 128:129]), reads=(b0,), writes=(rz4b,))
                    kb.op("dve", lambda e, a1=a1, rzj=rzj: e.reciprocal(out=rzj[:, 1:2], in_=a1[:, 128:129]), reads=(b1,), writes=(rz4b,))
                    kb.op("dve", lambda e, rzj=rzj: e.tensor_tensor(out=rzj[:, 1:2], in0=rzj[:, 1:2], in1=sc[:, 2:3], op=ALU.mult),
                          reads=(rz4b, scb), writes=(rz4b,))
                    kb.op("dve", lambda e, a0=a0, rzj=rzj: e.tensor_scalar(out=o0[:, :], in0=a0[:, 0:128], scalar1=rzj[:, 0:1], scalar2=None,
                                                                          op0=ALU.mult), reads=(b0, rz4b), writes=(o0b,))
                    kb.op("dve", lambda e, a1=a1, rzj=rzj, jj=jj, oj=oj: e.scalar_tensor_tensor(
                        out=oo4[oj][:, jj, :], in0=a1[:, 0:128], scalar=rzj[:, 1:2], in1=o0[:, :], op0=ALU.mult, op1=ALU.add),
                        reads=(b1, rz4b, o0b), writes=(oo4b[oj],))
                    if jj == 3:
                        for j2 in range(4):
                            kb.op("act", lambda e, j2=j2, oj=oj: e.activation(out=jk[:, :], in_=oo4[oj][:, j2, :], func=AF.Square,
                                                                            accum_out=ss4[:, j2:j2 + 1]),
                                  reads=(oo4b[oj],), writes=(jkb, ss4b))
                        kb.op("dve", lambda e: e.tensor_scalar(out=ss4[:, 4:8], in0=ss4[:, 0:4], scalar1=1.0 / 128, scalar2=EPS,
                                                               op0=ALU.mult, op1=ALU.add), reads=(ss4b,), writes=(ss4b,))
                        kb.op("act", lambda e: e.activation(out=ss4[:, 4:8], in_=ss4[:, 4:8], func=AF.Sqrt), reads=(ss4b,), writes=(ss4b,))
                        kb.op("dve", lambda e: e.reciprocal(out=ss4[:, 4:8], in_=ss4[:, 4:8]), reads=(ss4b,), writes=(ss4b,))
                        for j2 in range(4):
                            kb.op("dve", lambda e, j2=j2, oj=oj: e.scalar_tensor_tensor(
                                out=ob[oj][:, j2, :], in0=oo4[oj][:, j2, :], scalar=ss4[:, 4 + j2:5 + j2], in1=gs[:, :],
                                op0=ALU.mult, op1=ALU.mult), reads=(oo4b[oj], ss4b, gsb), writes=(obb[oj],))
                        r0 = s * S + (n - 3) * 128
                        kb.dma("sp", mix[r0:r0 + 512, 512 + h * 128:512 + (h + 1) * 128].rearrange("(j p) d -> p j d", p=128),
                               ob[oj][:, :, :], obb[oj], load=False)

                kprev = emit_qk(groups[0])
                for gi_, g in enumerate(groups):
                    knext = emit_qk(groups[gi_ + 1]) if gi_ + 1 < len(groups) else None
                    emit_pv(g, kprev)
                    n_, mg_, mc_ = g
                    if mg_ + mc_ - 1 == n_:
                        emit_combine(n_)
                    kprev = knext
                oi_[0] += NT // 4
        kb.barrier()
        kb.release(qtb + ktb + obb + [lpb, gsb, mkb, vbb])


A_PATTERNS = ((128, 1), (512, 4), (2048, 16))


def _colsel(ap2, r, d, nb):
    if d == 1:
        return ap2[:, nb * 128:(nb + 1) * 128]
    return ap2.rearrange("p (j d) -> p j d", d=d)[:, nb * 128:(nb + 1) * 128, r]


def _rowsel(ap, r, d, nb):
    if d == 1:
        return ap[nb * 128:(nb + 1) * 128]
    if len(ap.shape) == 2:
        return ap.rearrange("(j d) c -> j d c", d=d)[nb * 128:(nb + 1) * 128, r, :]
    return ap.rearrange("(j d) h e -> j d h e", d=d)[nb * 128:(nb + 1) * 128, r, :, :]


def phase_attn_a(kb, QT, KT, V, mask4_d, NZ, nseq, S):
    nc = kb.nc
    with ExitStack() as es:
        mk = es.enter_context(kb.sbt("aa_mk", [128, 512], BF16))
        mkb = Buf()
        kb.dma("sp", mk[:, :], mask4_d, mkb)
        qt = [es.enter_context(kb.sbt("aa_qt%d" % i, [128, S], BF16)) for i in range(4)]
        qtb = [Buf() for _ in range(4)]
        kt = [es.enter_context(kb.sbt("aa_kt%d" % i, [128, S], BF16)) for i in range(4)]
        ktb = [Buf() for _ in range(4)]
        NV = 3
        vt = [es.enter_context(kb.sbt("aa_vt%d" % i, [128, 8, 72], BF16)) for i in range(NV)]
        vtb = [Buf() for _ in range(NV)]
        for i in range(NV):
            kb.op("pool", lambda e, i=i: e.memset(vt[i][:, :, 64:65], 1.0), writes=(vtb[i],))
        NP = 4
        pt = [es.enter_context(kb.sbt("aa_pt%d" % i, [128, 512], BF16)) for i in range(NP)]
        ptb = [Buf() for _ in range(NP)]
        ot = [es.enter_context(kb.sbt("aa_ot%d" % i, [128, 8, 65], F32)) for i in range(2)]
        otb = [Buf() for _ in range(2)]
        pj = [0]
        oi = 0
        vi = 0
        for s in range(nseq):
            for hp in range(4):
                kb.dma("sp", qt[hp][:, :], QT[s, hp, :, :], qtb[hp])
                kb.dma("sp", kt[hp][:, :], KT[s, hp, :, :], ktb[hp])
            units = []
            for p, (window, d) in enumerate(A_PATTERNS):
                nblk = S // d // 128
                for r in range(d):
                    prev = None
                    for nb in range(nblk):
                        cur = vi % NV
                        vi += 1
                        oj = oi % 2
                        oi += 1
                        ab = 4 + 2 * (oi % 2)
                        for hpp in range(2):
                            units.append(dict(p=p, d=d, r=r, nb=nb, cur=cur, prev=prev, oj=oj, ab=ab, hpp=hpp,
                                              first=(hpp == 0), last=(hpp == 1)))
                        prev = cur

            def emit_qk(u, s=s):
                d, r, nb, cur, hpp = u["d"], u["r"], u["nb"], u["cur"], u["hpp"]
                if u["first"]:
                    kb.dma("sp", vt[cur][:, :, 0:64],
                           _rowsel(V[s * S:(s + 1) * S, 0:512], r, d, nb).rearrange("p (h e) -> p h e", e=64), vtb[cur])
                for hl in range(2):
                    hp = 2 * hpp + hl
                    for x in range(2):
                        for hh in range(2):
                            bi = 2 * hpp + hh
                            bank = kb.bank(bi)
                            ps = slice(64 * hh, 64 * hh + 64)
                            qc_ = _colsel(qt[hp][ps, :], r, d, nb)
                            kx_ = _colsel(kt[hp][ps, :], r, d, nb if x == 0 else max(nb - 1, 0))
                            c0_ = hl * 256 + (128 if x == 0 else 0)
                            kb.op("pe", lambda e, bank=bank, c0_=c0_, kx_=kx_, qc_=qc_: e.matmul(
                                bank[:, c0_:c0_ + 128], lhsT=kx_, rhs=qc_, start=True, stop=True),
                                reads=(ktb[hp], qtb[hp]), writes=(kb.pbuf[bi],))
                ks = []
                for hh in range(2):
                    bi = 2 * hpp + hh
                    bank = kb.bank(bi)
                    k = pj[0] % NP
                    pj[0] += 1
                    ks.append(k)
                    kb.op("act", lambda e, bank=bank, k=k: e.activation(out=pt[k][:, :], in_=bank, func=AF.Exp),
                          reads=(kb.pbuf[bi],), writes=(ptb[k],))
                    kb.op("pool", lambda e, k=k: e.tensor_tensor(out=pt[k][:, :], in0=pt[k][:, :], in1=mk[:, :], op=ALU.mult),
                          reads=(ptb[k], mkb), writes=(ptb[k],))
                return ks

            def emit_pv(u, ks, s=s):
                p, d, r, nb, cur, prev, oj, ab, hpp = (u[x] for x in ("p", "d", "r", "nb", "cur", "prev", "oj", "ab", "hpp"))
                for hh in range(2):
                    k = ks[hh]
                    for hl in range(2):
                        hp = 2 * hpp + hl
                        head = 2 * hp + hh
                        acc = kb.bank(ab + head // 4)[:, (head % 4) * 128:(head % 4) * 128 + 65]
                        abuf = kb.pbuf[ab + head // 4]
                        kb.op("pe", lambda e, acc=acc, k=k, hl=hl, cur=cur, head=head, nb=nb: e.matmul(
                            acc, lhsT=pt[k][:, hl * 256 + 128:hl * 256 + 256], rhs=vt[cur][:, head, 0:65],
                            start=True, stop=(nb == 0)), reads=(ptb[k], vtb[cur]), writes=(abuf,))
                        if nb > 0:
                            kb.op("pe", lambda e, acc=acc, k=k, hl=hl, prev=prev, head=head: e.matmul(
                                acc, lhsT=pt[k][:, hl * 256:hl * 256 + 128], rhs=vt[prev][:, head, 0:65],
                                start=False, stop=True), reads=(ptb[k], vtb[prev]), writes=(abuf,))
                if u["last"]:
                    for half in range(2):
                        src = kb.bank(ab + half)[:, :].rearrange("p (h e) -> p h e", e=128)[:, :, 0:65]
                        if half == 0:
                            kb.op("act", lambda e, src=src, oj=oj: e.copy(out=ot[oj][:, 0:4, :], in_=src),
                                  reads=(kb.pbuf[ab],), writes=(otb[oj],))
                        else:
                            kb.op("dve", lambda e, src=src, oj=oj: e.tensor_copy(out=ot[oj][:, 4:8, :], in_=src),
                                  reads=(kb.pbuf[ab + 1],), writes=(otb[oj],))
                    kb.dma("act", _rowsel(NZ[p, s * S:(s + 1) * S, :, :], r, d, nb), ot[oj][:, :, :], otb[oj], load=False)

            kprev = emit_qk(units[0])
            for ui, u in enumerate(units):
                knext = emit_qk(units[ui + 1]) if ui + 1 < len(units) else None
                emit_pv(u, kprev)
                kprev = knext
        kb.barrier()
        kb.release(vtb + otb + [mkb] + qtb + ktb)


def phase_attn_a_combine(kb, NZ, mix, ntok):
    nc = kb.nc
    with ExitStack() as es:
        nz = [[es.enter_context(kb.sbt("ac_nz%d_%d" % (i, p), [128, 8, 65], F32)) for p in range(3)] for i in range(2)]
        nzb = [[Buf() for p in range(3)] for i in range(2)]
        rz = es.enter_context(kb.sbt("ac_rz", [128, 8], F32))
        rzb = Buf()
        oa = [es.enter_context(kb.sbt("ac_oa%d" % i, [128, 8, 64], BF16)) for i in range(2)]
        oab = [Buf() for _ in range(2)]
        for t in range(ntok // 128):
            j = t % 2
            r0 = t * 128
            if t == 0:
                for p in range(3):
                    kb.dma("sp", nz[j][p][:, :, :], NZ[p, r0:r0 + 128, :, :], nzb[j][p])
            if t + 1 < ntok // 128:
                for p in range(3):
                    kb.dma("sp", nz[1 - j][p][:, :, :], NZ[p, r0 + 128:r0 + 256, :, :], nzb[1 - j][p])
            kb.op("dve", lambda e, j=j: e.tensor_tensor(out=nz[j][0][:, :, :], in0=nz[j][0][:, :, :], in1=nz[j][1][:, :, :], op=ALU.add),
                  reads=(nzb[j][0], nzb[j][1]), writes=(nzb[j][0],))
            kb.op("dve", lambda e, j=j: e.tensor_tensor(out=nz[j][0][:, :, :], in0=nz[j][0][:, :, :], in1=nz[j][2][:, :, :], op=ALU.add),
                  reads=(nzb[j][0], nzb[j][2]), writes=(nzb[j][0],))
            kb.op("dve", lambda e, j=j: e.reciprocal(out=rz[:, :], in_=nz[j][0][:, :, 64]), reads=(nzb[j][0],), writes=(rzb,))
            for h in range(8):
                eng = "act" if h % 2 == 0 else "dve"
                if eng == "act":
                    kb.op("act", lambda e, j=j, h=h: e.activation(out=oa[j][:, h, :], in_=nz[j][0][:, h, 0:64], func=AF.Copy,
                                                                scale=rz[:, h:h + 1]), reads=(nzb[j][0], rzb), writes=(oab[j],))
                else:
                    kb.op("dve", lambda e, j=j, h=h: e.tensor_scalar(out=oa[j][:, h, :], in0=nz[j][0][:, h, 0:64], scalar1=rz[:, h:h + 1],
                                                                   scalar2=None, op0=ALU.mult), reads=(nzb[j][0], rzb), writes=(oab[j],))
            kb.dma("act", mix[r0:r0 + 128, 0:512], oa[j][:, :, :].rearrange("p h e -> p (h e)"), oab[j], load=False)
        kb.barrier()
        kb.release([b for bb in nzb for b in bb] + oab)


def host_consts(S):
    bf = ml_dtypes.bfloat16
    ident = np.eye(128, dtype=np.float32).astype(bf)
    kk = np.arange(128)[:, None]
    qq = np.arange(128)[None, :]
    mcur = (kk <= qq).astype(np.float32)
    mprev = (kk >= qq).astype(np.float32)
    mask4 = np.concatenate([mprev, mcur, mprev, mcur], axis=1).astype(bf)
    half = 8
    inv_freq = (500000.0 ** (-np.arange(half, dtype=np.float32) / half)).astype(np.float32)
    ang = np.arange(S, dtype=np.float32)[:, None] * inv_freq[None, :]
    cos = np.tile(np.cos(ang).astype(np.float32), (1, 8))
    sin = np.tile(np.sin(ang).astype(np.float32), (1, 8))
    ss = np.arange(128)[:, None]
    tt = np.arange(128)[None, :]
    rep4 = lambda m: np.ascontiguousarray(np.repeat(m[:, None, :], 4, axis=1)).astype(bf)
    return dict(c_ident=ident, c_maskc=mcur.astype(bf), c_mask4=mask4, c_cos=cos, c_sin=sin,
                c_tri=(ss <= tt).astype(np.float32), c_ones32=np.ones((128, 128), np.float32),
                c_ms4=rep4((ss < tt).astype(np.float32)), c_mi4=rep4((ss <= tt).astype(np.float32)),
                c_mst4=rep4((ss > tt).astype(np.float32)), c_id4=rep4(np.eye(128, dtype=np.float32)),
                c_hm=np.stack([(np.arange(128) < 64), (np.arange(128) >= 64)], axis=1).astype(np.float32),
                c_bd=((ss // 64) == (tt // 64)).astype(np.float32),
                c_mo=np.stack([((ss // (2 * m)) == (tt // (2 * m))) & ((ss // m) % 2 == 0) & ((tt // m) % 2 == 1)
                               for m in (1, 2, 4, 8, 16, 32, 64)], axis=1).astype(np.float32).astype(bf),
                c_moT=np.stack([((ss // (2 * m)) == (tt // (2 * m))) & ((tt // m) % 2 == 0) & ((ss // m) % 2 == 1)
                                for m in (1, 2, 4, 8, 16, 32, 64)], axis=1).astype(np.float32).astype(bf))


def phase_rwkv(kb, h_in, W, ident_d, CN, lnw_d, lnb_d, mix, nseq, S):
    nc = kb.nc
    NT = S // 128
    C1 = -math.exp(-0.5)
    with ExitStack() as es:
        ident = es.enter_context(kb.sbt("rp_ident", [128, 128], BF16))
        identb = Buf()
        kb.dma("sp", ident[:, :], ident_d, identb)
        gcol = load_cols(kb, es, W["g"], "rp_gcol")
        mucol = load_cols(kb, es, W["mu"].rearrange("a b -> (a b)"), "rp_mucol")
        wr, wrb = load_weight_bf16(kb, es, W["w_r"], D, D, "rp_wr", gcol=gcol)
        wk, wkb = load_weight_bf16(kb, es, W["w_k"], D, D, "rp_wk", gcol=gcol)
        wv, wvb = load_weight_bf16(kb, es, W["w_v"], D, D, "rp_wv", gcol=gcol)
        w1, w1b = load_weight_bf16(kb, es, W["w1"], D, 64, "rp_w1", gcol=gcol)
        a1, a1b = load_weight_bf16(kb, es, W["a1"], D, 64, "rp_a1", gcol=gcol)
        g1, g1b = load_weight_bf16(kb, es, W["g1"], D, 160, "rp_g1", gcol=gcol)
        w2, w2b = load_weight_bf16(kb, es, W["w2"], 64, D, "rp_w2")
        a2, a2b = load_weight_bf16(kb, es, W["a2"], 64, D, "rp_a2")
        g2, g2b = load_weight_bf16(kb, es, W["g2"], 160, D, "rp_g2")
        w0, w0b = load_bcast(kb, es, W["w0"], "rp_w0")
        a0, a0b = load_bcast(kb, es, W["a0"], "rp_a0")
        kkw, kkwb = load_bcast(kb, es, W["k_k"], "rp_kk")
        kaw, kawb = load_bcast(kb, es, W["k_a"], "rp_ka")
        rkw, rkwb = load_bcast(kb, es, W["r_k"].rearrange("a b -> (a b)"), "rp_rk")

        def T(name, shape, dt):
            return es.enter_context(kb.sbt(name, shape, dt)), Buf()
        hi, hib = T("rp_hi", [128, D], F32)
        nb_, nbb = T("rp_nb", [128, D], BF16)
        nT, nTb = T("rp_nT", [128, 8, 128], BF16)
        nTs, nTsb = T("rp_nTs", [128, 8, 128], BF16)
        car, carb = T("rp_car", [128, 8, 1], BF16)
        xx, xxb = T("rp_xx", [128, 8, 128], BF16)
        xi = [T("rp_xi%d" % i, [128, 8, 128], BF16) for i in range(6)]
        junk, junkb = T("rp_junk", [128, D], F32)
        stt, sttb = T("rp_st", [128, 2], F32)
        tr, trb = T("rp_r", [128, D], F32)
        tk, tkb = T("rp_k", [128, D], F32)
        tv, tvb = T("rp_v", [128, D], F32)
        tw, twb = T("rp_w", [128, D], F32)
        ta, tab = T("rp_a", [128, D], F32)
        tg, tgb = T("rp_g", [128, D], F32)
        t1, t1b = T("rp_t1", [128, D], F32)
        t2, t2b = T("rp_t2", [128, D], F32)
        t3, t3b = T("rp_t3", [128, D], F32)
        sm, smb = T("rp_sm", [128, 48], F32)
        lw, lwb = T("rp_lw", [64, 128], BF16)
        la, lab = T("rp_la", [64, 128], BF16)
        lg, lgb = T("rp_lg", [128, 2, 128], BF16)
        lnw, lnwb = load_bcast(kb, es, lnw_d, "rp_lnw")
        lnb, lnbb = load_bcast(kb, es, lnb_d, "rp_lnb")
        tri, trib = T("rp_tri", [128, 128], F32)
        kb.dma("sp", tri[:, :], CN["c_tri"], trib)
        on32, on32b = T("rp_on32", [128, 128], F32)
        kb.dma("sp", on32[:, :], CN["c_ones32"], on32b)
        ms4, ms4b = T("rp_ms4", [128, 4, 128], BF16)
        kb.dma("sp", ms4[:, :, :], CN["c_ms4"], ms4b)
        mi4, mi4b = T("rp_mi4", [128, 4, 128], BF16)
        kb.dma("sp", mi4[:, :, :], CN["c_mi4"], mi4b)
        mst4, mst4b = T("rp_mst4", [128, 4, 128], BF16)
        kb.dma("sp", mst4[:, :, :], CN["c_mst4"], mst4b)
        id4, id4b = T("rp_id4", [128, 4, 128], BF16)
        kb.dma("sp", id4[:, :, :], CN["c_id4"], id4b)
        hm, hmb = T("rp_hm", [128, 2], F32)
        kb.dma("sp", hm[:, :], CN["c_hm"], hmb)
        bdm, bdmb = T("rp_bdm", [128, 128], F32)
        kb.dma("sp", bdm[:, :], CN["c_bd"], bdmb)
        stmp, stmpb = T("rp_stmp", [128, 128], F32)
        tbo, tbob = T("rp_bo", [128, D], F32)
        Atm, Atmb = T("rp_Atm", [128, D], BF16)
        Z0, Z0b = T("rp_Z0", [128, 8, 2, 128], BF16)
        Z1, Z1b = T("rp_Z1", [128, 8, 2, 128], BF16)
        RT, RTb = T("rp_RT", [128, 8, 128], BF16)
        BT, BTb = T("rp_BT", [128, 8, 128], BF16)
        KT_, KTb = T("rp_KT", [128, 8, 128], BF16)
        Xt, Xtb = T("rp_X", [128, 8, 128], BF16)
        Nq = [T("rp_N%d" % i, [128, 4, 128], BF16) for i in range(2)]
        NTq = [T("rp_NT%d" % i, [128, 4, 128], BF16) for i in range(2)]
        Tq = [T("rp_T%d" % i, [128, 4, 128], BF16) for i in range(2)]
        TTq = [T("rp_TT%d" % i, [128, 4, 128], BF16) for i in range(2)]
        mo, mob = T("rp_mo", [128, 7, 128], BF16)
        kb.dma("sp", mo[:, :, :], CN["c_mo"], mob)
        moT, moTb = T("rp_moT", [128, 7, 128], BF16)
        kb.dma("sp", moT[:, :, :], CN["c_moT"], moTb)
        Abr, Abrb = T("rp_Abr", [128, 4, 128], BF16)
        Aak, Aakb = T("rp_Aak", [128, 4, 128], BF16)
        Akr, Akrb = T("rp_Akr", [128, 4, 128], BF16)
        W2t, W2tb = T("rp_W2", [128, 4, 64], BF16)
        Ut = [T("rp_U%d" % i, [128, 128], BF16) for i in range(2)]
        Sm, Smb = T("rp_Sm", [128, 8, 128], F32)
        Sb, Sbb = T("rp_Sb", [128, 8, 128], BF16)
        PL, PLb = T("rp_PL", [128, 8], F32)
        yt, ytb = T("rp_y", [128, D], F32)
        om, omb = Atm, Atmb
        def v3t(tb):
            return (tb[0][:, :, :], tb[1])
        set0 = dict(N0=v3t(Nq[0]), N0T=v3t(NTq[0]), Aq=v3t(Nq[1]), Cq=v3t(NTq[1]), T=[v3t(Tq[0]), v3t(Tq[1])],
                    TT=[v3t(TTq[0]), v3t(TTq[1])], Abr=(Abr[:, :, :], Abrb), Aak=(Aak[:, :, :], Aakb), Akr=(Akr[:, :, :], Akrb),
                    W2=(W2t[:, :, :], W2tb), U=[(Ut[0][0][:, :], Ut[0][1]), (Ut[1][0][:, :], Ut[1][1])], stmp=(stmp[:, :], stmpb))

        def carve(tile, i):
            return tile[:, :].bitcast(BF16)[:, i * 512:(i + 1) * 512]

        pool_chunks = [carve(tt_, i) for tt_ in (t1, t3, tw, tk, ta, junk, tr, tv) for i in range(4)]

        def make_set(ch, stmp_view):
            c3 = [(x.rearrange("p (h t) -> p h t", t=128), Buf()) for x in ch[:9]]
            last = ch[9]
            w2c = (last[:, 0:256].rearrange("p (h i) -> p h i", i=64), Buf())
            u0c = (last[:, 256:384], Buf())
            u1c = (last[:, 384:512], Buf())
            sv = (stmp_view, Buf())
            d_ = dict(N0=c3[0], N0T=c3[1], Aq=c3[2], Cq=c3[2], T=[c3[3], c3[4]], TT=[c3[5], c3[5]], Abr=c3[6], Aak=c3[7], Akr=c3[8],
                      W2=w2c, U=[u0c, u1c], stmp=sv)
            return d_, c3 + [w2c, u0c, u1c, sv]

        set1, kids1 = make_set(pool_chunks[0:10], pool_chunks[30][:, 0:256].bitcast(F32))
        set2, kids2 = make_set(pool_chunks[10:20], pool_chunks[30][:, 256:512].bitcast(F32))
        set3, kids3 = make_set(pool_chunks[20:30], pool_chunks[31][:, 0:256].bitcast(F32))
        sets = [set0, set1, set2, set3]
        set1_list = kids1 + kids2 + kids3
        alias_parents = (t1b, t3b, twb, tkb, tab, junkb, trb, tvb)
        gi = [0]

        def gbank():
            bi = 2 + (gi[0] % 5)
            gi[0] += 1
            return bi
        qi_ = [0]

        def qbank():
            bi = qi_[0] % 6
            qi_[0] += 1
            return bi

        def flat(tb):
            return tb[0][:, :, :].rearrange("p c t -> p (c t)")
        st = {"i": 0}
        pi = [0]

        def mm_tok(xT, xTb, wsb, wb, evac):
            for half in range(2):
                bi = 2 + (pi[0] % 5)
                pi[0] += 1
                pap = kb.bank(bi)
                for c in range(8):
                    kb.op("pe", lambda e, c=c, half=half, pap=pap: e.matmul(
                        pap, lhsT=xT[:, c, :], rhs=wsb[:, c, half * 512:(half + 1) * 512], start=(c == 0), stop=(c == 7)),
                        reads=(xTb, wb), writes=(kb.pbuf[bi],))
                evac(pap, kb.pbuf[bi], half)

        def hv(t):
            return t[:, :].rearrange("p (h j) -> p h j", j=64)

        ntile = nseq * NT

        def front_a(g):
            r0_ = g * 128
            kb.dma("sp", hi[:, :], h_in[r0_:r0_ + 128, :], hib)
            rms_rstd(kb, hi[:, :], hib, nb_[:, :], nbb, stt[:, 0:1], stt[:, 1:2], sttb, D, EPS)
            kb.op("dve", lambda e: e.tensor_scalar(out=nb_[:, :], in0=hi[:, :], scalar1=stt[:, 1:2], scalar2=None, op0=ALU.mult),
                  reads=(hib, sttb), writes=(nbb,))

        def front_b(g):
            if g % NT == 0:
                kb.op("dve", lambda e: e.memset(car[:, :, :], 0.0), writes=(carb,))
            transpose_tile(kb, lambda c: nb_[:, c * 128:(c + 1) * 128], nbb, 8, lambda c0, n: nT[:, c0:c0 + n, :], nTb,
                           ident[:, :], identb, (0, 1), st)
            kb.op("pool", lambda e: e.tensor_copy(out=nTs[:, :, 1:128], in_=nT[:, :, 0:127]), reads=(nTb,), writes=(nTsb,))
            kb.op("pool", lambda e: e.tensor_copy(out=nTs[:, :, 0:1], in_=car[:, :, :]), reads=(carb,), writes=(nTsb,))
            kb.op("pool", lambda e: e.tensor_copy(out=car[:, :, :], in_=nT[:, :, 127:128]), reads=(nTb,), writes=(carb,))
            kb.op("dve", lambda e: e.tensor_tensor(out=xx[:, :, :], in0=nTs[:, :, :], in1=nT[:, :, :], op=ALU.subtract),
                  reads=(nTsb, nTb), writes=(xxb,))

        front_a(0)
        front_b(0)
        for s in range(nseq):
            kb.op("pool", lambda e: e.memset(Sm[:, :, :], 0.0), writes=(Smb,))
            kb.op("pool", lambda e: e.memset(Sb[:, :, :], 0.0), writes=(Sbb,))
            for t in range(NT):
                r0 = s * S + t * 128
                gt = s * NT + t
                for i in range(6):
                    eng = "pool" if i % 3 != 2 else "dve"
                    kb.op(eng, lambda e, i=i: e.tensor_tensor(out=xi[i][0][:, :, :], in0=xx[:, :, :],
                                                            in1=mucol[0][:, i * 8:(i + 1) * 8].unsqueeze(2).broadcast_to([128, 8, 128]),
                                                            op=ALU.mult), reads=(xxb, mucol[1]), writes=(xi[i][1],))
                    kb.op(eng, lambda e, i=i: e.tensor_tensor(out=xi[i][0][:, :, :], in0=xi[i][0][:, :, :], in1=nT[:, :, :], op=ALU.add),
                          reads=(xi[i][1], nTb), writes=(xi[i][1],))
                XR, XW, XK, XV, XA, XG = xi
                mm_tok(XR[0], XR[1], wr, wrb, lambda pap, pb, half: kb.op(
                    "act", lambda e: e.copy(out=tr[:, half * 512:(half + 1) * 512], in_=pap), reads=(pb,), writes=(trb,)))
                mm_tok(XK[0], XK[1], wk, wkb, lambda pap, pb, half: kb.op(
                    "act", lambda e: e.copy(out=tk[:, half * 512:(half + 1) * 512], in_=pap), reads=(pb,), writes=(tkb,)))
                mm_tok(XV[0], XV[1], wv, wvb, lambda pap, pb, half: kb.op(
                    "act", lambda e: e.copy(out=tv[:, half * 512:(half + 1) * 512], in_=pap), reads=(pb,), writes=(tvb,)))
                for (X, l1, l1b, dst, dstb, fn) in ((XW, w1, w1b, lw, lwb, AF.Tanh), (XA, a1, a1b, la, lab, AF.Copy)):
                    bi = 2 + (pi[0] % 5)
                    pi[0] += 1
                    pap = kb.bank(bi)[0:64, 0:128]
                    for c in range(8):
                        kb.op("pe", lambda e, c=c, pap=pap, X=X, l1=l1: e.matmul(pap, lhsT=l1[:, c, :], rhs=X[0][:, c, :],
                                                                              start=(c == 0), stop=(c == 7)),
                              reads=(X[1], l1b), writes=(kb.pbuf[bi],))
                    kb.op("act", lambda e, pap=pap, dst=dst, fn=fn: e.activation(out=dst[:, :], in_=pap, func=fn),
                          reads=(kb.pbuf[bi],), writes=(dstb,))
                for mc, (m0, mr) in enumerate(((0, 128), (128, 32))):
                    bi = 2 + (pi[0] % 5)
                    pi[0] += 1
                    pap = kb.bank(bi)[0:mr, 0:128]
                    for c in range(8):
                        kb.op("pe", lambda e, c=c, pap=pap, m0=m0, mr=mr: e.matmul(pap, lhsT=g1[:, c, m0:m0 + mr], rhs=XG[0][:, c, :],
                                                                               start=(c == 0), stop=(c == 7)),
                              reads=(XG[1], g1b), writes=(kb.pbuf[bi],))
                    kb.op("act", lambda e, pap=pap, mc=mc, mr=mr: e.activation(out=lg[0:mr, mc, :], in_=pap, func=AF.Sigmoid),
                          reads=(kb.pbuf[bi],), writes=(lgb,))
                for half in range(2):
                    hs = slice(half * 512, (half + 1) * 512)
                    bi = 2 + (pi[0] % 5)
                    pi[0] += 1
                    pap = kb.bank(bi)
                    kb.op("pe", lambda e, pap=pap, hs=hs: e.matmul(pap, lhsT=lw[:, :], rhs=w2[0:64, 0, hs], start=True, stop=True),
                          reads=(lwb, w2b), writes=(kb.pbuf[bi],))
                    kb.op("dve", lambda e, pap=pap, hs=hs: e.tensor_tensor(out=tw[:, hs], in0=pap, in1=w0[:, hs], op=ALU.add),
                          reads=(kb.pbuf[bi], w0b), writes=(twb,))
                    bi = 2 + (pi[0] % 5)
                    pi[0] += 1
                    pap = kb.bank(bi)
                    kb.op("pe", lambda e, pap=pap, hs=hs: e.matmul(pap, lhsT=la[:, :], rhs=a2[0:64, 0, hs], start=True, stop=True),
                          reads=(lab, a2b), writes=(kb.pbuf[bi],))
                    kb.op("dve", lambda e, pap=pap, hs=hs: e.tensor_tensor(out=ta[:, hs], in0=pap, in1=a0[:, hs], op=ALU.add),
                          reads=(kb.pbuf[bi], a0b), writes=(tab,))
                    bi = 2 + (pi[0] % 5)
                    pi[0] += 1
                    pap = kb.bank(bi)
                    kb.op("pe", lambda e, pap=pap, hs=hs: e.matmul(pap, lhsT=lg[:, 0, :], rhs=g2[:, 0, hs], start=True, stop=False),
                          reads=(lgb, g2b), writes=(kb.pbuf[bi],))
                    kb.op("pe", lambda e, pap=pap, hs=hs: e.matmul(pap, lhsT=lg[0:32, 1, :], rhs=g2[0:32, 1, hs], start=False, stop=True),
                          reads=(lgb, g2b), writes=(kb.pbuf[bi],))
                    kb.op("act", lambda e, pap=pap, hs=hs: e.copy(out=tg[:, hs], in_=pap), reads=(kb.pbuf[bi],), writes=(tgb,))
                XRb, XWb, XKb, XVb, XAb, XGb = xi
                Rtm, Bh, Kh, Bc, Kc, Vb = [(flat(x), x[1]) for x in xi]
                kb.op("act", lambda e: e.activation(out=tw[:, :], in_=tw[:, :], func=AF.Sigmoid), reads=(twb,), writes=(twb,))
                kb.op("act", lambda e: e.mul(out=tw[:, :], in_=tw[:, :], mul=C1), reads=(twb,), writes=(twb,))
                kb.op("act", lambda e: e.activation(out=ta[:, :], in_=ta[:, :], func=AF.Sigmoid), reads=(tab,), writes=(tab,))
                kb.op("dve", lambda e: e.tensor_tensor(out=t1[:, :], in0=tk[:, :], in1=kkw[:, :], op=ALU.mult), reads=(tkb, kkwb), writes=(t1b,))
                kb.op("pool", lambda e: e.tensor_tensor(out=t2[:, :], in0=t1[:, :], in1=t1[:, :], op=ALU.mult), reads=(t1b,), writes=(t2b,))
                kb.op("dve", lambda e: e.reduce_sum(out=sm[:, 0:16], in_=hv(t2), axis=AX.X), reads=(t2b,), writes=(smb,))
                kb.op("dve", lambda e: e.tensor_scalar(out=sm[:, 0:16], in0=sm[:, 0:16], scalar1=1e-24, scalar2=None, op0=ALU.max),
                      reads=(smb,), writes=(smb,))
                kb.op("act", lambda e: e.activation(out=sm[:, 0:16], in_=sm[:, 0:16], func=AF.Sqrt), reads=(smb,), writes=(smb,))
                kb.op("dve", lambda e: e.reciprocal(out=sm[:, 0:16], in_=sm[:, 0:16]), reads=(smb,), writes=(smb,))
                kb.op("dve", lambda e: e.tensor_scalar(out=sm[:, 0:16], in0=sm[:, 0:16], scalar1=-1.0, scalar2=None, op0=ALU.mult),
                      reads=(smb,), writes=(smb,))
                kb.op("dve", lambda e: e.tensor_tensor(out=hv(t2), in0=hv(t1), in1=sm[:, 0:16].unsqueeze(2).broadcast_to([128, 16, 64]),
                                                       op=ALU.mult), reads=(t1b, smb), writes=(t2b,))
                kb.op("dve", lambda e: e.scalar_tensor_tensor(out=t3[:, :], in0=t2[:, :], scalar=-1.0, in1=ta[:, :], op0=ALU.mult, op1=ALU.mult),
                      reads=(t2b, tab), writes=(t3b,))
                kb.op("dve", lambda e: e.scalar_tensor_tensor(out=t1[:, :], in0=ta[:, :], scalar=-1.0, in1=kaw[:, :], op0=ALU.add, op1=ALU.mult),
                      reads=(tab, kawb), writes=(t1b,))
                kb.op("dve", lambda e: e.scalar_tensor_tensor(out=t1[:, :], in0=t1[:, :], scalar=1.0, in1=tk[:, :], op0=ALU.add, op1=ALU.mult),
                      reads=(t1b, tkb), writes=(t1b,))
                kb.op("pool", lambda e: e.tensor_tensor(out=junk[:, :], in0=tr[:, :], in1=t1[:, :], op=ALU.mult), reads=(trb, t1b), writes=(junkb,))
                kb.op("pool", lambda e: e.tensor_tensor(out=junk[:, :], in0=junk[:, :], in1=rkw[:, :], op=ALU.mult), reads=(junkb, rkwb), writes=(junkb,))
                kb.op("dve", lambda e: e.reduce_sum(out=sm[:, 16:32], in_=hv(junk), axis=AX.X), reads=(junkb,), writes=(smb,))
                kb.op("pool", lambda e: e.tensor_tensor(out=hv(tbo), in0=hv(tv), in1=sm[:, 16:32].unsqueeze(2).broadcast_to([128, 16, 64]),
                                                        op=ALU.mult), reads=(tvb, smb), writes=(tbob,))
                epos, eposb = tk, tkb
                eneg, enegb = ta, tab
                etot, etotb = hi, hib
                enld, enldb = junk, junkb
                for half in range(2):
                    hs = slice(half * 512, (half + 1) * 512)
                    bi = gbank()
                    kb.op("pe", lambda e, bi=bi, hs=hs: e.matmul(kb.bank(bi), lhsT=tri[:, :], rhs=tw[:, hs], start=True, stop=True),
                          reads=(trib, twb), writes=(kb.pbuf[bi],))
                    kb.op("act", lambda e, bi=bi, hs=hs: e.activation(out=epos[:, hs], in_=kb.bank(bi), func=AF.Exp),
                          reads=(kb.pbuf[bi],), writes=(eposb,))
                    kb.op("act", lambda e, bi=bi, hs=hs: e.activation(out=eneg[:, hs], in_=kb.bank(bi), func=AF.Exp, scale=-1.0),
                          reads=(kb.pbuf[bi],), writes=(enegb,))
                    bi = gbank()
                    kb.op("pe", lambda e, bi=bi, hs=hs: e.matmul(kb.bank(bi), lhsT=on32[:, :], rhs=tw[:, hs], start=True, stop=True),
                          reads=(on32b, twb), writes=(kb.pbuf[bi],))
                    kb.op("act", lambda e, bi=bi, hs=hs: e.activation(out=etot[:, hs], in_=kb.bank(bi), func=AF.Exp),
                          reads=(kb.pbuf[bi],), writes=(etotb,))
                kb.op("act", lambda e: e.activation(out=enld[:, :], in_=tw[:, :], func=AF.Exp, scale=-1.0), reads=(twb,), writes=(enldb,))
                bi = gbank()
                for p in range(8):
                    kb.op("pe", lambda e, bi=bi, p=p: e.matmul(kb.bank(bi)[:, p:p + 1], lhsT=tw[:, p * 128:(p + 1) * 128], rhs=on32[:, 0:1],
                                                            start=True, stop=True), reads=(twb, on32b), writes=(kb.pbuf[bi],))
                kb.op("act", lambda e, bi=bi: e.activation(out=PL[:, :], in_=kb.bank(bi)[:, 0:8], func=AF.Exp), reads=(kb.pbuf[bi],), writes=(PLb,))
                kb.op("dve", lambda e: e.tensor_tensor(out=enld[:, :], in0=enld[:, :], in1=epos[:, :], op=ALU.mult), reads=(enldb, eposb), writes=(enldb,))
                kb.op("pool", lambda e: e.tensor_tensor(out=etot[:, :], in0=etot[:, :], in1=eneg[:, :], op=ALU.mult), reads=(etotb, enegb), writes=(etotb,))
                kb.op("dve", lambda e: e.tensor_tensor(out=Atm[:, :], in0=t2[:, :], in1=enld[:, :], op=ALU.mult), reads=(t2b, enldb), writes=(Atmb,))
                kb.op("pool", lambda e: e.tensor_tensor(out=Rtm[0], in0=tr[:, :], in1=epos[:, :], op=ALU.mult), reads=(trb, eposb), writes=(Rtm[1],))
                kb.op("dve", lambda e: e.tensor_tensor(out=Bh[0], in0=t3[:, :], in1=eneg[:, :], op=ALU.mult), reads=(t3b, enegb), writes=(Bh[1],))
                kb.op("pool", lambda e: e.tensor_tensor(out=Kh[0], in0=t1[:, :], in1=eneg[:, :], op=ALU.mult), reads=(t1b, enegb), writes=(Kh[1],))
                kb.op("dve", lambda e: e.tensor_tensor(out=Bc[0], in0=t3[:, :], in1=etot[:, :], op=ALU.mult), reads=(t3b, etotb), writes=(Bc[1],))
                kb.op("pool", lambda e: e.tensor_tensor(out=Kc[0], in0=t1[:, :], in1=etot[:, :], op=ALU.mult), reads=(t1b, etotb), writes=(Kc[1],))
                kb.op("act", lambda e: e.copy(out=Vb[0], in_=tv[:, :]), reads=(tvb,), writes=(Vb[1],))
                for (srct, which) in ((( Atm[:, :], Atmb), "A"), (Rtm, "R"), (Bh, "B"), (Kh, "K")):
                    sap, sbuf_ = srct
                    bi = st["i"] % 2
                    st["i"] += 1
                    pap = kb.bank(bi, BF16)
                    for c in range(8):
                        kb.op("pe", lambda e, c=c, pap=pap, sap=sap: e.transpose(out=pap[:, c * 128:(c + 1) * 128], in_=sap[:, c * 128:(c + 1) * 128],
                                                                              identity=ident[:, :]), reads=(sbuf_, identb), writes=(kb.pbuf[bi],))
                    p3 = pap.rearrange("p (c t) -> p c t", t=128)
                    if which == "A":
                        kb.op("dve", lambda e, p3=p3: e.tensor_scalar(out=Z0[:, :, 0, :], in0=p3, scalar1=hm[:, 0:1], scalar2=None, op0=ALU.mult),
                              reads=(kb.pbuf[bi], hmb), writes=(Z0b,))
                        kb.op("dve", lambda e, p3=p3: e.tensor_scalar(out=Z1[:, :, 0, :], in0=p3, scalar1=hm[:, 1:2], scalar2=None, op0=ALU.mult),
                              reads=(kb.pbuf[bi], hmb), writes=(Z1b,))
                    elif which == "R":
                        kb.op("act", lambda e, p3=p3: e.activation(out=Z0[:, :, 1, :], in_=p3, func=AF.Copy, scale=hm[:, 0:1]),
                              reads=(kb.pbuf[bi], hmb), writes=(Z0b,))
                        kb.op("act", lambda e, p3=p3: e.activation(out=Z1[:, :, 1, :], in_=p3, func=AF.Copy, scale=hm[:, 1:2]),
                              reads=(kb.pbuf[bi], hmb), writes=(Z1b,))
                        kb.op("act", lambda e, p3=p3: e.copy(out=RT[:, :, :], in_=p3), reads=(kb.pbuf[bi],), writes=(RTb,))
                    elif which == "B":
                        kb.op("act", lambda e, p3=p3: e.copy(out=BT[:, :, :], in_=p3), reads=(kb.pbuf[bi],), writes=(BTb,))
                    else:
                        kb.op("dve", lambda e, p3=p3: e.tensor_copy(out=KT_[:, :, :], in_=p3), reads=(kb.pbuf[bi],), writes=(KTb,))
                if gt + 1 < ntile:
                    front_a(gt + 1)
                kids = [b_ for (_, b_) in set1_list]
                kb.alias_acquire(alias_parents, kids)

                def hd(q, hq):
                    p = 2 * q + hq // 2
                    hh = hq % 2
                    Z, Zb = (Z0, Z0b) if hh == 0 else (Z1, Z1b)
                    return p, hh, Z, Zb

                def h4(bi):
                    return kb.bank(bi).rearrange("p (h t) -> p h t", t=128)

                def st_kinds(q, TS):
                    kinds = (("ab", ms4, ms4b, TS["N0"]), ("br", mi4, mi4b, TS["Abr"]), ("ak", ms4, ms4b, TS["Aak"]),
                             ("kr", mi4, mi4b, TS["Akr"]), ("nt", mst4, mst4b, TS["N0T"]))
                    for kind, mk_, mkb_, dst in kinds:
                        bi = qbank()
                        for hq in range(4):
                            p, hh, Z, Zb = hd(q, hq)
                            if kind == "ab":
                                l, lb, r_, rb = BT[:, p, :], BTb, Z[:, p, 0, :], Zb
                            elif kind == "br":
                                l, lb, r_, rb = BT[:, p, :], BTb, Z[:, p, 1, :], Zb
                            elif kind == "ak":
                                l, lb, r_, rb = KT_[:, p, :], KTb, Z[:, p, 0, :], Zb
                            elif kind == "kr":
                                l, lb, r_, rb = KT_[:, p, :], KTb, Z[:, p, 1, :], Zb
                            else:
                                l, lb, r_, rb = Z[:, p, 0, :], Zb, BT[:, p, :], BTb
                            kb.op("pe", lambda e, bi=bi, hq=hq, l=l, r_=r_: e.matmul(kb.bank(bi)[:, hq * 128:(hq + 1) * 128], lhsT=l, rhs=r_,
                                                                                  start=True, stop=True), reads=(lb, rb), writes=(kb.pbuf[bi],))
                        kb.op("dve", lambda e, bi=bi, mk_=mk_, dst=dst: e.tensor_tensor(out=dst[0], in0=h4(bi), in1=mk_[:, :, :], op=ALU.mult),
                              reads=(kb.pbuf[bi], mkb_), writes=(dst[1],))

                def st_init(q, TS):
                    kb.op("pool", lambda e: e.tensor_tensor(out=TS["Aq"][0], in0=TS["N0"][0], in1=mo[:, 0:1, :].broadcast_to([128, 4, 128]),
                                                          op=ALU.mult), reads=(TS["N0"][1], mob), writes=(TS["Aq"][1],))
                    kb.op("pool", lambda e: e.tensor_tensor(out=TS["T"][0][0], in0=TS["Aq"][0], in1=id4[:, :, :], op=ALU.add),
                          reads=(TS["Aq"][1], id4b), writes=(TS["T"][0][1],))
                    kb.op("pool", lambda e: e.tensor_tensor(out=TS["Cq"][0], in0=TS["N0T"][0], in1=moT[:, 0:1, :].broadcast_to([128, 4, 128]),
                                                          op=ALU.mult), reads=(TS["N0T"][1], moTb), writes=(TS["Cq"][1],))
                    kb.op("pool", lambda e: e.tensor_tensor(out=TS["TT"][0][0], in0=TS["Cq"][0], in1=id4[:, :, :], op=ALU.add),
                          reads=(TS["Cq"][1], id4b), writes=(TS["TT"][0][1],))
                    TS["ct"] = 0

                def st_lvl_a(q, TS, li):
                    ct = TS["ct"]
                    T_ = TS["T"][ct]
                    bi = qbank()
                    for hq in range(4):
                        kb.op("pe", lambda e, bi=bi, hq=hq: e.matmul(kb.bank(bi)[:, hq * 128:(hq + 1) * 128],
                              lhsT=TS["N0T"][0][:, hq, :], rhs=T_[0][:, hq, :], start=True, stop=True),
                              reads=(TS["N0T"][1], T_[1]), writes=(kb.pbuf[bi],))
                    kb.op("dve", lambda e, bi=bi: e.tensor_tensor(out=TS["Aq"][0], in0=h4(bi), in1=mo[:, li:li + 1, :].broadcast_to([128, 4, 128]),
                                                               op=ALU.mult), reads=(kb.pbuf[bi], mob), writes=(TS["Aq"][1],))

                def st_lvl_b(q, TS, li):
                    ct = TS["ct"]
                    T_, T2 = TS["T"][ct], TS["T"][1 - ct]
                    TT_ = TS["TT"][ct]
                    bi = qbank()
                    for hq in range(4):
                        kb.op("pe", lambda e, bi=bi, hq=hq: e.matmul(kb.bank(bi)[:, hq * 128:(hq + 1) * 128],
                              lhsT=TT_[0][:, hq, :], rhs=TS["Aq"][0][:, hq, :], start=True, stop=False),
                              reads=(TT_[1], TS["Aq"][1]), writes=(kb.pbuf[bi],))
                        kb.op("pe", lambda e, bi=bi, hq=hq: e.matmul(kb.bank(bi)[:, hq * 128:(hq + 1) * 128],
                              lhsT=TT_[0][:, hq, :], rhs=ident[:, :], start=False, stop=True),
                              reads=(TT_[1], identb), writes=(kb.pbuf[bi],))
                    kb.op("act", lambda e, bi=bi: e.copy(out=T2[0], in_=h4(bi)), reads=(kb.pbuf[bi],), writes=(T2[1],))
                    TS["ct"] = 1 - ct

                def st_lvl_c(q, TS, li):
                    ct = TS["ct"]
                    T2, TT2 = TS["T"][ct], TS["TT"][ct]
                    bi = qbank()
                    pv = kb.bank(bi, BF16)
                    for hq in range(4):
                        kb.op("pe", lambda e, pv=pv, hq=hq: e.transpose(out=pv[:, hq * 128:(hq + 1) * 128], in_=T2[0][:, hq, :], identity=ident[:, :]),
                              reads=(T2[1], identb), writes=(kb.pbuf[bi],))
                    kb.op("dve", lambda e, pv=pv: e.tensor_copy(out=TT2[0], in_=pv[:, 0:512].rearrange("p (h t) -> p h t", t=128)),
                          reads=(kb.pbuf[bi],), writes=(TT2[1],))

                def st_xw2(q, TS):
                    Tf = TS["T"][TS["ct"]]
                    bi = qbank()
                    for hq in range(4):
                        p, hh, Z, Zb = hd(q, hq)
                        kb.op("pe", lambda e, bi=bi, hq=hq, p=p: e.matmul(kb.bank(bi)[:, hq * 128:(hq + 1) * 128],
                              lhsT=Atm[:, p * 128:(p + 1) * 128], rhs=Tf[0][:, hq, :], start=True, stop=True),
                              reads=(Atmb, Tf[1]), writes=(kb.pbuf[bi],))
                    b4 = kb.bank(bi).rearrange("p (a b t) -> p a b t", b=2, t=128)
                    kb.op("dve", lambda e, b4=b4: e.tensor_scalar(out=Xt[:, 2 * q:2 * q + 2, :], in0=b4[:, :, 0, :], scalar1=hm[:, 0:1],
                                                                scalar2=None, op0=ALU.mult), reads=(kb.pbuf[bi], hmb), writes=(Xtb,))
                    kb.op("dve", lambda e, b4=b4: e.scalar_tensor_tensor(out=Xt[:, 2 * q:2 * q + 2, :], in0=b4[:, :, 1, :], scalar=hm[:, 1:2],
                                                                       in1=Xt[:, 2 * q:2 * q + 2, :], op0=ALU.mult, op1=ALU.add),
                          reads=(kb.pbuf[bi], hmb, Xtb), writes=(Xtb,))
                    bi = qbank()
                    for hq in range(4):
                        h = 4 * q + hq
                        kb.op("pe", lambda e, bi=bi, hq=hq, h=h: e.matmul(kb.bank(bi)[:, hq * 64:(hq + 1) * 64],
                              lhsT=TS["Aak"][0][:, hq, :], rhs=Vb[0][:, h * 64:(h + 1) * 64], start=True, stop=True),
                              reads=(TS["Aak"][1], Vb[1]), writes=(kb.pbuf[bi],))
                    kb.op("act", lambda e, bi=bi: e.copy(out=TS["W2"][0], in_=kb.bank(bi)[:, 0:256].rearrange("p (h i) -> p h i", i=64)),
                          reads=(kb.pbuf[bi],), writes=(TS["W2"][1],))

                def st_u(q, TS, pl):
                    Tf = TS["T"][TS["ct"]]
                    p = 2 * q + pl
                    U_, U_b = TS["U"][pl]
                    bi = qbank()
                    bu = kb.bank(bi)
                    kb.op("pe", lambda e, bu=bu, p=p: e.matmul(bu[:, 0:128], lhsT=Xt[:, p, :], rhs=Sb[:, p, :], start=True, stop=False),
                          reads=(Xtb, Sbb), writes=(kb.pbuf[bi],))
                    for hh in range(2):
                        hq = 2 * pl + hh
                        kb.op("pe", lambda e, bu=bu, hh=hh, hq=hq: e.matmul(bu[:, 64 * hh:64 * hh + 64], lhsT=Tf[0][:, hq, :],
                                                                         rhs=TS["W2"][0][:, hq, :], start=False, stop=(hh == 1)),
                              reads=(Tf[1], TS["W2"][1]), writes=(kb.pbuf[bi],))
                    kb.op("act", lambda e, bu=bu: e.copy(out=U_, in_=bu[:, 0:128]), reads=(kb.pbuf[bi],), writes=(U_b,))

                def st_ys(q, TS, pl):
                    p = 2 * q + pl
                    pcs = slice(p * 128, (p + 1) * 128)
                    U_, U_b = TS["U"][pl]
                    ybk = 7 - q // 2
                    yb_ = kb.bank(ybk)[:, (p % 4) * 128:(p % 4 + 1) * 128]
                    kb.op("pe", lambda e, yb_=yb_, p=p: e.matmul(yb_, lhsT=RT[:, p, :], rhs=Sb[:, p, :], start=True, stop=False),
                          reads=(RTb, Sbb), writes=(kb.pbuf[ybk],))
                    for hh in range(2):
                        hq = 2 * pl + hh
                        h = 4 * q + hq
                        kb.op("pe", lambda e, yb_=yb_, hh=hh, hq=hq: e.matmul(yb_[:, 64 * hh:64 * hh + 64], lhsT=TS["Abr"][0][:, hq, :],
                                                                           rhs=U_[:, 64 * hh:64 * hh + 64], start=False, stop=False),
                              reads=(TS["Abr"][1], U_b), writes=(kb.pbuf[ybk],))
                        kb.op("pe", lambda e, yb_=yb_, hh=hh, hq=hq, h=h: e.matmul(yb_[:, 64 * hh:64 * hh + 64], lhsT=TS["Akr"][0][:, hq, :],
                                                                                rhs=Vb[0][:, h * 64:(h + 1) * 64], start=False, stop=(hh == 1)),
                              reads=(TS["Akr"][1], Vb[1]), writes=(kb.pbuf[ybk],))
                    bi = qbank()
                    bs_ = kb.bank(bi)
                    kb.op("pe", lambda e, bs_=bs_, pcs=pcs: e.matmul(bs_[:, 0:128], lhsT=Bc[0][:, pcs], rhs=U_, start=True, stop=False),
                          reads=(Bc[1], U_b), writes=(kb.pbuf[bi],))
                    kb.op("pe", lambda e, bs_=bs_, pcs=pcs: e.matmul(bs_[:, 0:128], lhsT=Kc[0][:, pcs], rhs=Vb[0][:, pcs], start=False, stop=True),
                          reads=(Kc[1], Vb[1]), writes=(kb.pbuf[bi],))
                    sx, sxb = TS["stmp"]
                    kb.op("dve", lambda e, bs_=bs_: e.tensor_tensor(out=sx, in0=bs_[:, 0:128], in1=bdm[:, :], op=ALU.mult),
                          reads=(kb.pbuf[bi], bdmb), writes=(sxb,))
                    kb.op("dve", lambda e, p=p: e.scalar_tensor_tensor(out=Sm[:, p, :], in0=Sm[:, p, :], scalar=PL[:, p:p + 1], in1=sx,
                                                                      op0=ALU.mult, op1=ALU.add), reads=(Smb, PLb, sxb), writes=(Smb,))
                    kb.op("act", lambda e, p=p: e.copy(out=Sb[:, p, :], in_=Sm[:, p, :]), reads=(Smb,), writes=(Sbb,))

                def quad_stages(q, TS):
                    L = [lambda: st_kinds(q, TS), lambda: st_init(q, TS)]
                    for li in range(1, 7):
                        L.append(lambda li=li: st_lvl_a(q, TS, li))
                        L.append(lambda li=li: st_lvl_b(q, TS, li))
                        if li < 6:
                            L.append(lambda li=li: st_lvl_c(q, TS, li))
                    L.append(lambda: st_xw2(q, TS))
                    for pl in range(2):
                        L.append(lambda pl=pl: st_u(q, TS, pl))
                        L.append(lambda pl=pl: st_ys(q, TS, pl))
                    return L

                for stage_fns in zip(*[quad_stages(q, sets[q]) for q in range(4)]):
                    for fn_ in stage_fns:
                        fn_()
                for qq in range(2):
                    kb.op("act", lambda e, qq=qq: e.copy(out=yt[:, qq * 512:(qq + 1) * 512], in_=kb.bank(7 - qq)),
                          reads=(kb.pbuf[7 - qq],), writes=(ytb,))
                kb.alias_release(alias_parents, kids)
                if gt + 1 < ntile:
                    front_b(gt + 1)
                sq, sqb = t2, t2b
                kb.op("dve", lambda e: e.reduce_sum(out=sm[:, 0:16], in_=hv(yt), axis=AX.X), reads=(ytb,), writes=(smb,))
                kb.op("pool", lambda e: e.tensor_tensor(out=sq[:, :], in0=yt[:, :], in1=yt[:, :], op=ALU.mult), reads=(ytb,), writes=(sqb,))
                kb.op("dve", lambda e: e.reduce_sum(out=sm[:, 16:32], in_=hv(sq), axis=AX.X), reads=(sqb,), writes=(smb,))
                kb.op("dve", lambda e: e.tensor_scalar(out=sm[:, 0:32], in0=sm[:, 0:32], scalar1=1.0 / 64, scalar2=None, op0=ALU.mult),
                      reads=(smb,), writes=(smb,))
                kb.op("dve", lambda e: e.tensor_tensor(out=sm[:, 32:48], in0=sm[:, 0:16], in1=sm[:, 0:16], op=ALU.mult), reads=(smb,), writes=(smb,))
                kb.op("dve", lambda e: e.tensor_tensor(out=sm[:, 16:32], in0=sm[:, 16:32], in1=sm[:, 32:48], op=ALU.subtract),
                      reads=(smb,), writes=(smb,))
                kb.op("dve", lambda e: e.tensor_scalar(out=sm[:, 16:32], in0=sm[:, 16:32], scalar1=GN_EPS, scalar2=None, op0=ALU.add),
                      reads=(smb,), writes=(smb,))
                kb.op("act", lambda e: e.activation(out=sm[:, 16:32], in_=sm[:, 16:32], func=AF.Sqrt), reads=(smb,), writes=(smb,))
                kb.op("dve", lambda e: e.reciprocal(out=sm[:, 16:32], in_=sm[:, 16:32]), reads=(smb,), writes=(smb,))
                kb.op("dve", lambda e: e.tensor_tensor(out=hv(sq), in0=hv(yt), in1=sm[:, 0:16].unsqueeze(2).broadcast_to([128, 16, 64]),
                                                       op=ALU.subtract), reads=(ytb, smb), writes=(sqb,))
                kb.op("pool", lambda e: e.tensor_tensor(out=hv(sq), in0=hv(sq), in1=sm[:, 16:32].unsqueeze(2).broadcast_to([128, 16, 64]),
                                                        op=ALU.mult), reads=(sqb, smb), writes=(sqb,))
                kb.op("dve", lambda e: e.tensor_tensor(out=sq[:, :], in0=sq[:, :], in1=lnw[:, :], op=ALU.mult), reads=(sqb, lnwb), writes=(sqb,))
                kb.op("pool", lambda e: e.tensor_tensor(out=sq[:, :], in0=sq[:, :], in1=lnb[:, :], op=ALU.add), reads=(sqb, lnbb), writes=(sqb,))
                kb.op("pool", lambda e: e.tensor_tensor(out=sq[:, :], in0=sq[:, :], in1=tbo[:, :], op=ALU.add), reads=(sqb, tbob), writes=(sqb,))
                kb.op("pool", lambda e: e.tensor_tensor(out=om[:, :], in0=sq[:, :], in1=tg[:, :], op=ALU.mult), reads=(sqb, tgb), writes=(omb,))
                kb.dma("sp", mix[r0:r0 + 128, :], om[:, :], omb, load=False)
        kb.barrier()
        kb.release([identb, gcol[1], mucol[1], wrb, wkb, wvb, w1b, a1b, g1b, w2b, a2b, g2b, w0b, a0b, kkwb, kawb, rkwb,
                    hib, omb, lnwb, lnbb, trib, on32b, ms4b, mi4b, mst4b, id4b, hmb, bdmb, mob, moTb])


NSEQ = 2
SEQ = 4096

W_NAMES = ["norm_mix", "norm_mlp", "norm_final", "attn_w_in", "attn_w_out", "diff_lambda", "diff_subln", "rwkv_mu",
           "rwkv_w_r", "rwkv_w_k", "rwkv_w_v", "rwkv_w_o", "rwkv_w0", "rwkv_w1", "rwkv_w2", "rwkv_a0", "rwkv_a1", "rwkv_a2",
           "rwkv_g1", "rwkv_g2", "rwkv_k_k", "rwkv_k_a", "rwkv_r_k", "rwkv_ln_w", "rwkv_ln_b", "mlp_w1", "mlp_w2"]


def build_program(shapes, consts, nseq=NSEQ, S=SEQ):
    kb = KB()
    nc = kb.nc
    ntok = nseq * S
    A = {}
    for name, shp in shapes.items():
        A[name] = nc.dram_tensor(name, list(shp), F32, kind="ExternalInput").ap()
    Cn = {}
    for name, arr in consts.items():
        dt = BF16 if arr.dtype == ml_dtypes.bfloat16 else F32
        Cn[name] = nc.dram_tensor(name, list(arr.shape), dt, kind="ExternalInput").ap()
    out = nc.dram_tensor("out", [ntok, D], F32, kind="ExternalOutput").ap()

    def scr(name, shape, dt):
        return nc.dram_tensor(name, shape, dt, kind="Internal").ap()
    QT = scr("s_QT", [nseq, 8, 128, S], BF16)
    KT = scr("s_KT", [nseq, 8, 128, S], BF16)
    V = scr("s_V", [ntok, 1024], BF16)
    NZ = scr("s_NZ", [3, ntok, 8, 65], F32)
    mix = scr("s_mix", [ntok, 1024], BF16)
    h1 = scr("s_h1", [ntok, D], F32)
    h2 = scr("s_h2", [ntok, D], F32)
    h3 = scr("s_h3", [ntok, D], F32)
    x = A["x"]
    lam_init = 0.8 - 0.6 * math.exp(0.0)
    ident = Cn["c_ident"]
    phase_qkv(kb, x, A["norm_mix"][0], A["attn_w_in"][0], Cn["c_cos"], Cn["c_sin"], ident, QT, KT, V, nseq, S)
    phase_attn_b(kb, QT, KT, V, A["diff_lambda"][0], A["diff_subln"][0], Cn["c_maskc"], mix, nseq, S, lam_init)
    phase_attn_a(kb, QT, KT, V, Cn["c_mask4"], NZ, nseq, S)
    phase_attn_a_combine(kb, NZ, mix, ntok)
    phase_proj(kb, mix, A["attn_w_out"][0], x, h1, ident, ntok)
    phase_mlp(kb, h1, A["norm_mlp"][0], A["mlp_w1"][0], A["mlp_w2"][0], h2, ident, ntok)
    W = dict(g=A["norm_mix"][1], mu=A["rwkv_mu"][0], w_r=A["rwkv_w_r"][0], w_k=A["rwkv_w_k"][0], w_v=A["rwkv_w_v"][0],
             w1=A["rwkv_w1"][0], a1=A["rwkv_a1"][0], g1=A["rwkv_g1"][0], w2=A["rwkv_w2"][0], a2=A["rwkv_a2"][0],
             g2=A["rwkv_g2"][0], w0=A["rwkv_w0"][0], a0=A["rwkv_a0"][0], k_k=A["rwkv_k_k"][0], k_a=A["rwkv_k_a"][0],
             r_k=A["rwkv_r_k"][0])
    phase_rwkv(kb, h2, W, ident, Cn, A["rwkv_ln_w"][0], A["rwkv_ln_b"][0], mix, nseq, S)
    phase_proj(kb, mix, A["rwkv_w_o"][0], h2, h3, ident, ntok)
    phase_mlp(kb, h3, A["norm_mlp"][1], A["mlp_w1"][1], A["mlp_w2"][1], out, ident, ntok, gfinal=A["norm_final"])
    return nc


def kernel(**inputs):
    x = np.ascontiguousarray(inputs["x"], dtype=np.float32)
    B, S, C = x.shape
    nseq = B // NCORES
    consts = host_consts(S)
    shapes = {"x": (nseq * S, C)}
    wts = {}
    for n in W_NAMES:
        wts[n] = np.ascontiguousarray(inputs[n], dtype=np.float32)
        shapes[n] = wts[n].shape
    nc = build_program(shapes, consts, nseq=nseq, S=S)
    in_maps = []
    for c in range(NCORES):
        m = {"x": x[c * nseq:(c + 1) * nseq].reshape(nseq * S, C)}
        m.update(wts)
        m.update(consts)
        in_maps.append(m)
    res = run_bass_kernel_spmd(nc, in_maps, core_ids=list(range(NCORES)))
    outs = [np.asarray(r["out"]).reshape(nseq, S, C) for r in res.results]
    return np.concatenate(outs, axis=0).astype(np.float32)
```
